# Optimizing a Trainium2 kernel written in Bass

```python
import jax, jax.numpy as jnp
from jax import lax
import numpy as np

D_MODEL = 2048
BATCH = 2
SEQ = 16384
DEPTH = 4

HEAD_DIM = 128
Q_BLOCK = 128
RMS_EPS = 1e-6
NEG_INF = -1e30

MLA_HEADS = 6
MLA_Q_LORA = 512
MLA_KV_LORA = 512
MLA_NOPE = 128
MLA_ROPE = 64
MLA_V = 128
ROPE_BASE = 10000.0

DIL_PAIRS = ((128, 1), (512, 4), (2048, 16))
DIL_HEADS_PER_GROUP = 2
DIL_HEADS = DIL_HEADS_PER_GROUP * len(DIL_PAIRS)

NSA_HEADS = 4
NSA_CMP_LEN = 32
NSA_CMP_STRIDE = 16
NSA_SEL_LEN = 64
NSA_TOPK = 16
NSA_WINDOW = 512
NSA_FORCED_SCORE = 100.0

N_ALIBI = DIL_HEADS + NSA_HEADS

D_FF = 5504

IN_SPLITS = (MLA_Q_LORA, MLA_KV_LORA, MLA_ROPE,
             DIL_HEADS * HEAD_DIM, DIL_HEADS * HEAD_DIM, DIL_HEADS * HEAD_DIM,
             NSA_HEADS * HEAD_DIM,
             HEAD_DIM, HEAD_DIM, HEAD_DIM, HEAD_DIM, HEAD_DIM, HEAD_DIM,
             NSA_HEADS * 3)
IN_COLS = sum(IN_SPLITS)
IN_SPLIT_POINTS = [int(c) for c in np.cumsum(IN_SPLITS)[:-1]]
MIX_OUT = MLA_HEADS * MLA_V + DIL_HEADS * HEAD_DIM + NSA_HEADS * HEAD_DIM

kernel_name = 'hybrid_mla_dilated_nsa_macaron'


def rms_norm(x, g):
    xf = x.astype(jnp.float32)
    y = xf * lax.rsqrt(jnp.mean(xf * xf, axis=-1, keepdims=True) + RMS_EPS)
    return (y * g.astype(jnp.float32)).astype(x.dtype)


def swiglu_ffn(x, w_in, w_out):
    gate, up = jnp.split(x @ w_in, 2, axis=-1)
    return (jax.nn.silu(gate) * up) @ w_out


def alibi_slopes():
    return 2.0 ** (-8.0 * jnp.arange(1, N_ALIBI + 1, dtype=jnp.float32) / N_ALIBI)


def apply_rope(x, pos):
    half = x.shape[-1] // 2
    inv_freq = ROPE_BASE ** (-jnp.arange(half, dtype=jnp.float32) / half)
    ang = pos.astype(jnp.float32)[:, None] * inv_freq[None, :]
    cos, sin = jnp.cos(ang)[:, None, :], jnp.sin(ang)[:, None, :]
    xf = x.astype(jnp.float32)
    x1, x2 = xf[..., :half], xf[..., half:]
    return jnp.concatenate([x1 * cos - x2 * sin, x2 * cos + x1 * sin], axis=-1).astype(x.dtype)


def masked_softmax(s, mask):
    s = jnp.where(mask, s.astype(jnp.float32), NEG_INF)
    p = jax.nn.softmax(s, axis=-1)
    return jnp.where(mask, p, 0.0)


def mla_attention(q_lat, kv_lat, k_rope, q_norm, w_uq, kv_norm, w_ukv):
    B, S, _ = q_lat.shape
    pos = jnp.arange(S)
    q = (rms_norm(q_lat, q_norm) @ w_uq).reshape(B, S, MLA_HEADS, MLA_NOPE + MLA_ROPE)
    q = jnp.concatenate([q[..., :MLA_NOPE], apply_rope(q[..., MLA_NOPE:], pos)], axis=-1)
    kv = (rms_norm(kv_lat, kv_norm) @ w_ukv).reshape(B, S, MLA_HEADS, MLA_NOPE + MLA_V)
    k_pe = jnp.broadcast_to(apply_rope(k_rope[:, :, None, :], pos), (B, S, MLA_HEADS, MLA_ROPE))
    k = jnp.concatenate([kv[..., :MLA_NOPE], k_pe], axis=-1)
    v = kv[..., MLA_NOPE:]
    scale = (MLA_NOPE + MLA_ROPE) ** -0.5
    nb = S // Q_BLOCK
    q_blocks = q.reshape(B, nb, Q_BLOCK, MLA_HEADS, -1).transpose(1, 0, 2, 3, 4)

    def block(args):
        qb, i = args
        t = i * Q_BLOCK + jnp.arange(Q_BLOCK)
        s = jnp.einsum('bqhd,bkhd->bhqk', qb, k).astype(jnp.float32) * scale
        p = masked_softmax(s, pos[None, :] <= t[:, None])
        return jnp.einsum('bhqk,bkhd->bqhd', p.astype(v.dtype), v)

    o = lax.map(block, (q_blocks, jnp.arange(nb)))
    return o.transpose(1, 0, 2, 3, 4).reshape(B, S, MLA_HEADS * MLA_V)


def dilated_group(q, k, v, window, dilation, slopes):
    B, S, Hg, Dh = q.shape
    span = window // dilation
    L = S // dilation
    nb = -(-L // Q_BLOCK)
    Lp = nb * Q_BLOCK
    Z = B * dilation

    def strided(x):
        x = x.reshape(B, L, dilation, Hg, Dh).transpose(0, 2, 1, 3, 4).reshape(Z, L, Hg, Dh)
        return jnp.pad(x, ((0, 0), (0, Lp - L), (0, 0), (0, 0)))

    def band_keys(x):
        xb = jnp.pad(x, ((0, 0), (Q_BLOCK, 0), (0, 0), (0, 0))).reshape(Z, nb + 1, Q_BLOCK, Hg, Dh)
        return jnp.concatenate([xb[:, :-1], xb[:, 1:]], axis=2)

    qb = strided(q).reshape(Z, nb, Q_BLOCK, Hg, Dh)
    kb, vb = band_keys(strided(k)), band_keys(strided(v))
    a = jnp.arange(Q_BLOCK)[:, None]
    c = jnp.arange(2 * Q_BLOCK)[None, :]
    j = Q_BLOCK + a - c
    kpos = (jnp.arange(nb)[:, None, None] - 1) * Q_BLOCK + c[None]
    mask = ((j >= 0) & (j <= span))[None] & (kpos >= 0)
    mask = mask[:, None]
    s = jnp.einsum('znqhd,znkhd->znhqk', qb, kb).astype(jnp.float32) * (Dh ** -0.5)
    s = s - slopes[:, None, None] * (j * dilation).astype(jnp.float32)
    s = jnp.where(mask, s, NEG_INF)
    m = jnp.max(s, axis=-1, keepdims=True)
    e = jnp.where(mask, jnp.exp(s - m), 0.0)
    den = jnp.sum(e, axis=-1, keepdims=True)
    o = jnp.einsum('znhqk,znkhd->znqhd', (e / den).astype(vb.dtype), vb)
    lse = (m + jnp.log(den))[..., 0]
    o = o.reshape(Z, Lp, Hg, Dh)[:, :L].reshape(B, dilation, L, Hg, Dh)
    o = o.transpose(0, 2, 1, 3, 4).reshape(B, S, Hg, Dh)
    lse = lse.transpose(0, 1, 3, 2).reshape(Z, Lp, Hg)[:, :L].reshape(B, dilation, L, Hg)
    lse = lse.transpose(0, 2, 1, 3).reshape(B, S, Hg)
    return o, lse


def dilated_mixture(q, k, v, slopes):
    B, S, _, Dh = q.shape
    outs, lses = [], []
    for g, (window, dilation) in enumerate(DIL_PAIRS):
        sl = slice(g * DIL_HEADS_PER_GROUP, (g + 1) * DIL_HEADS_PER_GROUP)
        o, lse = dilated_group(q[:, :, sl], k[:, :, sl], v[:, :, sl], window, dilation, slopes[sl])
        outs.append(o)
        lses.append(lse)
    alpha = jax.nn.softmax(jnp.stack(lses, axis=0), axis=0)
    o = jnp.stack(outs, axis=0) * alpha[..., None].astype(outs[0].dtype)
    return o.transpose(1, 2, 0, 3, 4).reshape(B, S, DIL_HEADS * Dh)


def nsa_compress(x, pos_emb, w1, w2):
    B, S, Dh = x.shape
    nc = (S - NSA_CMP_LEN) // NSA_CMP_STRIDE + 1
    idx = jnp.arange(nc)[:, None] * NSA_CMP_STRIDE + jnp.arange(NSA_CMP_LEN)[None, :]
    blocks = x[:, idx] + pos_emb
    h = jax.nn.silu(blocks.reshape(B, nc, NSA_CMP_LEN * Dh) @ w1)
    return h @ w2


def nsa_attention(q, k_cmp, v_cmp, k_slc, v_slc, k_win, v_win, gate_logits,
                  cmp_pos, phi_k1, phi_k2, phi_v1, phi_v2, slopes):
    B, S, H, Dh = q.shape
    kc = nsa_compress(k_cmp, cmp_pos, phi_k1, phi_k2)
    vc = nsa_compress(v_cmp, cmp_pos, phi_v1, phi_v2)
    nc = kc.shape[1]
    ns = S // NSA_SEL_LEN
    topk = min(NSA_TOPK, ns)
    c_start = jnp.arange(nc) * NSA_CMP_STRIDE
    c_end = c_start + NSA_CMP_LEN - 1
    c_centre = c_start.astype(jnp.float32) + 0.5 * (NSA_CMP_LEN - 1)
    j_idx = jnp.arange(ns)
    cmp_to_sel = ((c_start[:, None] < (j_idx[None, :] + 1) * NSA_SEL_LEN)
                  & (c_end[:, None] >= j_idx[None, :] * NSA_SEL_LEN)).astype(jnp.float32)
    pad = ((0, 0), (NSA_WINDOW, 0), (0, 0))
    k_win_p, v_win_p = jnp.pad(k_win, pad), jnp.pad(v_win, pad)
    gates = jax.nn.sigmoid(gate_logits.astype(jnp.float32)).reshape(B, S, H, 3)
    nb = S // Q_BLOCK
    q_blocks = q.reshape(B, nb, Q_BLOCK, H, Dh).transpose(1, 0, 2, 3, 4)
    g_blocks = gates.reshape(B, nb, Q_BLOCK, H, 3).transpose(1, 0, 2, 3, 4)
    slope = slopes[:, None, None]
    scale = Dh ** -0.5
    gather = jax.vmap(lambda seq, idx: seq[idx])

    def block(args):
        qb, gb, i = args
        t = i * Q_BLOCK + jnp.arange(Q_BLOCK)
        tf = t.astype(jnp.float32)
        s = jnp.einsum('bqhd,bcd->bhqc', qb, kc).astype(jnp.float32) * scale
        s = s - slope * (tf[:, None] - c_centre[None, :])
        p_cmp = masked_softmax(s, c_end[None, :] <= t[:, None])
        o_cmp = jnp.einsum('bhqc,bcd->bqhd', p_cmp.astype(vc.dtype), vc)
        score = jnp.einsum('bhqc,cj->bqj', p_cmp, cmp_to_sel)
        cur = (t // NSA_SEL_LEN)[:, None]
        jj = j_idx[None, :]
        forced = (jj == 0) | (jj == cur) | (jj == cur - 1)
        score = jnp.where(jj > cur, -1.0, jnp.where(forced, NSA_FORCED_SCORE, score))
        _, sel = lax.top_k(score, topk)
        kpos = (sel[..., None] * NSA_SEL_LEN + jnp.arange(NSA_SEL_LEN)).reshape(B, Q_BLOCK, topk * NSA_SEL_LEN)
        kg, vg = gather(k_slc, kpos), gather(v_slc, kpos)
        dist = t[None, :, None] - kpos
        s = jnp.einsum('bqhd,bqnd->bhqn', qb, kg).astype(jnp.float32) * scale
        s = s - slope * dist[:, None].astype(jnp.float32)
        p = masked_softmax(s, (dist >= 0)[:, None])
        o_slc = jnp.einsum('bhqn,bqnd->bqhd', p.astype(vg.dtype), vg)
        kw = lax.dynamic_slice_in_dim(k_win_p, i * Q_BLOCK, NSA_WINDOW + Q_BLOCK, axis=1)
        vw = lax.dynamic_slice_in_dim(v_win_p, i * Q_BLOCK, NSA_WINDOW + Q_BLOCK, axis=1)
        wpos = i * Q_BLOCK - NSA_WINDOW + jnp.arange(NSA_WINDOW + Q_BLOCK)
        dist = t[:, None] - wpos[None, :]
        s = jnp.einsum('bqhd,bkd->bhqk', qb, kw).astype(jnp.float32) * scale
        s = s - slope * dist.astype(jnp.float32)
        p = masked_softmax(s, (dist >= 0) & (dist < NSA_WINDOW) & (wpos[None, :] >= 0))
        o_win = jnp.einsum('bhqk,bkd->bqhd', p.astype(vw.dtype), vw)
        g = gb.astype(o_cmp.dtype)
        return g[..., 0:1] * o_cmp + g[..., 1:2] * o_slc + g[..., 2:3] * o_win

    o = lax.map(block, (q_blocks, g_blocks, jnp.arange(nb)))
    return o.transpose(1, 0, 2, 3, 4).reshape(B, S, H * Dh)


def setup_inputs(seed: int = 0) -> dict:
    key = jax.random.key(seed)
    ks = jax.random.split(key, 20)

    def w(k, shape, fan_in):
        return jax.random.normal(k, shape, jnp.float32) * (fan_in ** -0.5)

    def gain(k, n):
        return 1.0 + 0.02 * jax.random.normal(k, (DEPTH, n), jnp.float32)

    return {
        'x': jax.random.normal(ks[0], (BATCH, SEQ, D_MODEL), jnp.float32),
        'ffn1_norm': gain(ks[1], D_MODEL),
        'ffn1_w_in': w(ks[2], (DEPTH, D_MODEL, 2 * D_FF), D_MODEL),
        'ffn1_w_out': w(ks[3], (DEPTH, D_FF, D_MODEL), D_FF),
        'mix_norm': gain(ks[4], D_MODEL),
        'w_mix_in': w(ks[5], (DEPTH, D_MODEL, IN_COLS), D_MODEL),
        'mla_q_norm': gain(ks[6], MLA_Q_LORA),
        'mla_w_uq': w(ks[7], (DEPTH, MLA_Q_LORA, MLA_HEADS * (MLA_NOPE + MLA_ROPE)), MLA_Q_LORA),
        'mla_kv_norm': gain(ks[8], MLA_KV_LORA),
        'mla_w_ukv': w(ks[9], (DEPTH, MLA_KV_LORA, MLA_HEADS * (MLA_NOPE + MLA_V)), MLA_KV_LORA),
        'nsa_cmp_pos': 0.1 * jax.random.normal(ks[10], (DEPTH, NSA_CMP_LEN, HEAD_DIM), jnp.float32),
        'nsa_phi_k1': w(ks[11], (DEPTH, NSA_CMP_LEN * HEAD_DIM, HEAD_DIM), NSA_CMP_LEN * HEAD_DIM),
        'nsa_phi_k2': w(ks[12], (DEPTH, HEAD_DIM, HEAD_DIM), HEAD_DIM),
        'nsa_phi_v1': w(ks[13], (DEPTH, NSA_CMP_LEN * HEAD_DIM, HEAD_DIM), NSA_CMP_LEN * HEAD_DIM),
        'nsa_phi_v2': w(ks[14], (DEPTH, HEAD_DIM, HEAD_DIM), HEAD_DIM),
        'w_mix_out': w(ks[15], (DEPTH, MIX_OUT, D_MODEL), MIX_OUT),
        'ffn2_norm': gain(ks[16], D_MODEL),
        'ffn2_w_in': w(ks[17], (DEPTH, D_MODEL, 2 * D_FF), D_MODEL),
        'ffn2_w_out': w(ks[18], (DEPTH, D_FF, D_MODEL), D_FF),
        'final_norm': 1.0 + 0.02 * jax.random.normal(ks[19], (D_MODEL,), jnp.float32),
    }


def reference(x, ffn1_norm, ffn1_w_in, ffn1_w_out, mix_norm, w_mix_in, mla_q_norm, mla_w_uq,
              mla_kv_norm, mla_w_ukv, nsa_cmp_pos, nsa_phi_k1, nsa_phi_k2, nsa_phi_v1, nsa_phi_v2,
              w_mix_out, ffn2_norm, ffn2_w_in, ffn2_w_out, final_norm):
    B, S, _ = x.shape
    slopes = alibi_slopes()
    dil_slopes, nsa_slopes = slopes[:DIL_HEADS], slopes[DIL_HEADS:]
    for l in range(DEPTH):
        x = x + 0.5 * swiglu_ffn(rms_norm(x, ffn1_norm[l]), ffn1_w_in[l], ffn1_w_out[l])
        h = rms_norm(x, mix_norm[l]) @ w_mix_in[l]
        (q_lat, kv_lat, k_rope, dq, dk, dv, nq,
         nkc, nvc, nks, nvs, nkw, nvw, ng) = jnp.split(h, IN_SPLIT_POINTS, axis=-1)
        o_mla = mla_attention(q_lat, kv_lat, k_rope, mla_q_norm[l], mla_w_uq[l],
                              mla_kv_norm[l], mla_w_ukv[l])
        o_dil = dilated_mixture(dq.reshape(B, S, DIL_HEADS, HEAD_DIM),
                                dk.reshape(B, S, DIL_HEADS, HEAD_DIM),
                                dv.reshape(B, S, DIL_HEADS, HEAD_DIM), dil_slopes)
        o_nsa = nsa_attention(nq.reshape(B, S, NSA_HEADS, HEAD_DIM), nkc, nvc, nks, nvs, nkw, nvw, ng,
                              nsa_cmp_pos[l], nsa_phi_k1[l], nsa_phi_k2[l], nsa_phi_v1[l], nsa_phi_v2[l],
                              nsa_slopes)
        x = x + jnp.concatenate([o_mla, o_dil, o_nsa], axis=-1) @ w_mix_out[l]
        x = x + 0.5 * swiglu_ffn(rms_norm(x, ffn2_norm[l]), ffn2_w_in[l], ffn2_w_out[l])
    return rms_norm(x, final_norm)
```

```python
import contextlib
import numpy as np
import ml_dtypes
import concourse.bass as bass
import concourse.mybir as mybir
from concourse.bass_utils import run_bass_kernel_spmd

F32 = mybir.dt.float32
BF16 = mybir.dt.bfloat16
AF = mybir.ActivationFunctionType
ALU = mybir.AluOpType
NPBF = ml_dtypes.bfloat16

NCORES = 8
D = 2048
DFF = 5504
NFF = 43
DEPTH = 4
B = 2
S = 16384
EPS = 1e-6


class Sem:
    __slots__ = ("h", "cnt")

    def __init__(self, h):
        self.h = h
        self.cnt = 0


class Buf:
    __slots__ = ("w", "r", "ds")

    def __init__(self):
        self.w = None
        self.r = {}
        self.ds = None


def bufs(n):
    return [Buf() for _ in range(n)]


class Sched:
    def __init__(self, nc, es):
        self.nc = nc
        self.es = es
        self.E = {"pe": nc.tensor, "act": nc.scalar, "dve": nc.vector,
                  "pool": nc.gpsimd, "sp": nc.sync}
        self.sem = {k: Sem(es.enter_context(nc.semaphore("s_" + k)))
                    for k in ("pe", "act", "dve", "pool")}
        self.seen = {k: {} for k in self.E}
        self.nds = 0
        self.out_events = []

    def _waits(self, eng, reads, writes):
        need = {}
        for b in reads:
            if b.w is not None:
                s, v = b.w
                if need.get(s, 0) < v:
                    need[s] = v
        for b in writes:
            if b.w is not None:
                s, v = b.w
                if need.get(s, 0) < v:
                    need[s] = v
            for s, v in b.r.items():
                if need.get(s, 0) < v:
                    need[s] = v
        seen = self.seen[eng]
        own = self.sem.get(eng)
        for s, v in need.items():
            if eng == "pe" and s is own:
                continue
            if seen.get(s, 0) < v:
                self.E[eng].wait_ge(s.h, v)
                seen[s] = v

    def _commit(self, ev, reads, writes):
        s, v = ev
        for b in reads:
            if b.r.get(s, 0) < v:
                b.r[s] = v
        for b in writes:
            b.w = ev
            b.r = {}

    def op(self, eng, fn, reads=(), writes=()):
        self._waits(eng, reads, writes)
        inst = fn(self.E[eng])
        s = self.sem[eng]
        s.cnt += 1
        inst.then_inc(s.h, 1)
        self._commit((s, s.cnt), reads, writes)

    def dma(self, q, out, in_, sb, reads=(), writes=(), is_output=False, **kw):
        self._waits(q, reads, writes)
        if sb.ds is None:
            sb.ds = Sem(self.es.enter_context(self.nc.semaphore("d%d" % self.nds)))
            self.nds += 1
        inst = self.E[q].dma_start(out=out, in_=in_, **kw)
        sb.ds.cnt += 16
        inst.then_inc(sb.ds.h, 16)
        ev = (sb.ds, sb.ds.cnt)
        self._commit(ev, reads, writes)
        if is_output:
            self.out_events.append(ev)

    def finish(self):
        need = {}
        for s, v in self.out_events:
            if need.get(s, 0) < v:
                need[s] = v
        for s, v in need.items():
            self.E["sp"].wait_ge(s.h, v)


def new_nc():
    return bass.Bass("TRN2", target_bir_lowering=False)


CONV_CH = 4096


def build_conv(ncols):
    nc = new_nc()
    src = nc.dram_tensor("src", [128, ncols], F32, kind="ExternalInput").ap()
    dst = nc.dram_tensor("dst", [128, ncols], BF16, kind="ExternalOutput").ap()
    es = contextlib.ExitStack()
    sc = Sched(nc, es)
    NB = 3
    st = [es.enter_context(nc.sbuf_tensor("cs%d" % i, [128, CONV_CH], F32)) for i in range(NB)]
    ot = [es.enter_context(nc.sbuf_tensor("co%d" % i, [128, CONV_CH], BF16)) for i in range(NB)]
    sb, ob = bufs(NB), bufs(NB)
    nch = ncols // CONV_CH
    engs = ["dve", "act", "pool"]
    for i in range(nch):
        k = i % NB
        sl = slice(i * CONV_CH, (i + 1) * CONV_CH)
        sc.dma("sp", st[k][:], src[:, sl], sb[k], writes=[sb[k]])
        e = engs[i % 3]
        if e == "act":
            sc.op("act", lambda E: E.activation(out=ot[k][:], in_=st[k][:], func=AF.Copy),
                  reads=[sb[k]], writes=[ob[k]])
        else:
            sc.op(e, lambda E: E.tensor_copy(out=ot[k][:], in_=st[k][:]),
                  reads=[sb[k]], writes=[ob[k]])
        sc.dma("sp", dst[:, sl], ot[k][:], ob[k], reads=[ob[k]], is_output=True)
    sc.finish()
    es.close()
    return nc


TOK = 4096
TT = 1024
FF_HALVES = ((0, 22), (22, 21))


def rms_norm_fm(sc, nc, xs, xsb, xn, xnb, gain, gain_b, ones, ones_b, ss_ps, ss_b,
                sq, sq_b, rs, rs_b, rstd, rstd_b, nchunk, dim, tt):
    nh = tt // 512
    for c in range(nchunk):
        k = c % 2
        sc.op("act", lambda E: E.activation(out=sq[k][:, :tt], in_=xs[c], func=AF.Square),
              reads=[xsb[c]], writes=[sq_b[k]])
        for h in range(nh):
            sc.op("pe", lambda E: E.matmul(ss_ps[h][:, :], lhsT=ones[:, :], rhs=sq[k][:, h * 512:(h + 1) * 512],
                                           start=(c == 0), stop=(c == nchunk - 1)),
                  reads=[sq_b[k], ones_b], writes=[ss_b[h]])
    for h in range(nh):
        sc.op("act", lambda E: E.activation(out=rs[:, h * 512:(h + 1) * 512], in_=ss_ps[h][:, :],
                                            func=AF.Sqrt, scale=1.0 / dim, bias=EPSB[0][:, 0:1]),
              reads=[ss_b[h], EPSB[1]], writes=[rs_b])
    sc.op("dve", lambda E: E.reciprocal(out=rstd[:, :tt], in_=rs[:, :tt]), reads=[rs_b], writes=[rstd_b])
    for c in range(nchunk):
        sc.op("dve", lambda E: E.scalar_tensor_tensor(out=xn[c], in0=xs[c], scalar=gain[:, c:c + 1],
                                                      in1=rstd[:, :tt], op0=ALU.mult, op1=ALU.mult),
              reads=[xsb[c], gain_b, rstd_b], writes=[xnb[c]])


EPSB = [None, None]


def make_consts(sc, nc, es):
    ones = es.enter_context(nc.sbuf_tensor("ones_f", [128, 128], F32))
    ones_b = Buf()
    sc.op("dve", lambda E: E.memset(ones[:, :], 1.0), writes=[ones_b])
    epst = es.enter_context(nc.sbuf_tensor("eps_c", [128, 1], F32))
    eps_b = Buf()
    sc.op("dve", lambda E: E.memset(epst[:, :], EPS), writes=[eps_b])
    EPSB[0] = epst
    EPSB[1] = eps_b
    return ones, ones_b


def build_pf(final_norm=False):
    nc = new_nc()
    xT = nc.dram_tensor("xT", [16, 128, TOK], F32, kind="ExternalInput").ap()
    g = nc.dram_tensor("g", [128, 16], F32, kind="ExternalInput").ap()
    w_in = nc.dram_tensor("w_in", [NFF, 128, 16 * 256], BF16, kind="ExternalInput").ap()
    w_out = nc.dram_tensor("w_out", [16, 128, NFF * 128], BF16, kind="ExternalInput").ap()
    xo = nc.dram_tensor("xo", [16, 128, TOK], F32, kind="ExternalOutput").ap()
    if final_norm:
        gf = nc.dram_tensor("gf", [128, 16], F32, kind="ExternalInput").ap()
    es = contextlib.ExitStack()
    sc = Sched(nc, es)
    sb = lambda name, shape, dt: es.enter_context(nc.sbuf_tensor(name, shape, dt))
    ones, ones_b = make_consts(sc, nc, es)
    xs = sb("xs", [128, 16, TT], F32)
    xs_b = bufs(16)
    xn = sb("xn", [128, 16, TT], BF16)
    xn_b = bufs(16)
    act = sb("act", [128, 22, TT], BF16)
    act_b = bufs(22)
    NW = 3
    win = [sb("win%d" % i, [128, 16 * 256], BF16) for i in range(NW)]
    win_b = bufs(NW)
    NWO = 2
    wout = [sb("wout%d" % i, [128, 22 * 128], BF16) for i in range(NWO)]
    wout_b = bufs(NWO)
    sq = [sb("sq%d" % i, [128, TT], F32) for i in range(2)]
    sq_b = bufs(2)
    rs = sb("rs", [128, TT], F32)
    rs_b = Buf()
    rstd = sb("rstd", [128, TT], F32)
    rstd_b = Buf()
    sg = [sb("sg%d" % i, [128, 512], F32) for i in range(2)]
    sg_b = bufs(2)
    gt = sb("gt", [128, 16], F32)
    gt_b = Buf()
    sc.dma("sp", gt[:, :], g, gt_b, writes=[gt_b])
    if final_norm:
        gft = sb("gft", [128, 16], F32)
        gft_b = Buf()
        sc.dma("sp", gft[:, :], gf, gft_b, writes=[gft_b])
        xo_t = [sb("xot%d" % i, [128, TT], F32) for i in range(2)]
        xo_b = bufs(2)
    ps = [es.enter_context(nc.psum_tensor("ps%d" % i, [128, 512], F32)) for i in range(8)]
    ps_b = bufs(8)

    win_i = 0
    wout_i = 0
    for t in range(TOK // TT):
        tsl = slice(t * TT, (t + 1) * TT)
        for c in range(16):
            sc.dma("sp", xs[:, c, :], xT[c, :, tsl], xs_b[c], writes=[xs_b[c]])
        rms_norm_fm(sc, nc, [xs[:, c, :] for c in range(16)], xs_b, [xn[:, c, :] for c in range(16)], xn_b,
                    gt, gt_b, ones, ones_b, [ps[0], ps[1]], [ps_b[0], ps_b[1]], sq, sq_b, rs, rs_b,
                    rstd, rstd_b, 16, D, TT)
        gu = 0
        for (f0, nf) in FF_HALVES:
            pend = []

            def load_win(f):
                nonlocal win_i
                k = win_i % NW
                win_i += 1
                sc.dma("sp", win[k][:, :], w_in[f], win_b[k], writes=[win_b[k]])
                return k
            pend.append(load_win(f0))
            for fi in range(nf):
                if fi + 1 < nf:
                    pend.append(load_win(f0 + fi + 1))
                k = pend.pop(0)
                for h in range(2):
                    hs = slice(h * 512, (h + 1) * 512)
                    gb = (gu % 2) * 2
                    gu += 1
                    G, U = ps[gb], ps[gb + 1]
                    for kc in range(16):
                        sc.op("pe", lambda E: E.matmul(G[:, :], lhsT=win[k][:, kc * 256:kc * 256 + 128],
                                                       rhs=xn[:, kc, hs], start=(kc == 0), stop=(kc == 15)),
                              reads=[win_b[k], xn_b[kc]], writes=[ps_b[gb]])
                    for kc in range(16):
                        sc.op("pe", lambda E: E.matmul(U[:, :], lhsT=win[k][:, kc * 256 + 128:kc * 256 + 256],
                                                       rhs=xn[:, kc, hs], start=(kc == 0), stop=(kc == 15)),
                              reads=[win_b[k], xn_b[kc]], writes=[ps_b[gb + 1]])
                    j = gu % 2
                    sc.op("act", lambda E: E.activation(out=sg[j][:, :], in_=G[:, :], func=AF.Silu),
                          reads=[ps_b[gb]], writes=[sg_b[j]])
                    sc.op("dve", lambda E: E.tensor_tensor(out=act[:, fi, hs], in0=sg[j][:, :], in1=U[:, :],
                                                           op=ALU.mult),
                          reads=[sg_b[j], ps_b[gb + 1]], writes=[act_b[fi]])
            pend = []

            def load_wout(dmc):
                nonlocal wout_i
                k = wout_i % NWO
                wout_i += 1
                sc.dma("sp", wout[k][:, :nf * 128], w_out[dmc, :, f0 * 128:(f0 + nf) * 128], wout_b[k],
                       writes=[wout_b[k]])
                return k
            pend.append(load_wout(0))
            for dmc in range(16):
                if dmc + 1 < 16:
                    pend.append(load_wout(dmc + 1))
                k = pend.pop(0)
                for h in range(2):
                    hs = slice(h * 512, (h + 1) * 512)
                    yb = 4 + (dmc % 2) * 2 + h
                    Y = ps[yb]
                    for fi in range(nf):
                        sc.op("pe", lambda E: E.matmul(Y[:, :], lhsT=wout[k][:, fi * 128:(fi + 1) * 128],
                                                       rhs=act[:, fi, hs], start=(fi == 0), stop=(fi == nf - 1)),
                              reads=[wout_b[k], act_b[fi]], writes=[ps_b[yb]])
                    sc.op("dve", lambda E: E.scalar_tensor_tensor(out=xs[:, dmc, hs], in0=Y[:, :], scalar=0.5,
                                                                  in1=xs[:, dmc, hs], op0=ALU.mult, op1=ALU.add),
                          reads=[ps_b[yb], xs_b[dmc]], writes=[xs_b[dmc]])
        if not final_norm:
            for c in range(16):
                sc.dma("sp", xo[c, :, tsl], xs[:, c, :], xs_b[c], reads=[xs_b[c]], is_output=True)
        else:
            nh = TT // 512
            for c in range(16):
                k = c % 2
                sc.op("act", lambda E: E.activation(out=sq[k][:, :], in_=xs[:, c, :], func=AF.Square),
                      reads=[xs_b[c]], writes=[sq_b[k]])
                for h in range(nh):
                    sc.op("pe", lambda E: E.matmul(ps[h][:, :], lhsT=ones[:, :], rhs=sq[k][:, h * 512:(h + 1) * 512],
                                                   start=(c == 0), stop=(c == 15)),
                          reads=[sq_b[k], ones_b], writes=[ps_b[h]])
            for h in range(nh):
                sc.op("act", lambda E: E.activation(out=rs[:, h * 512:(h + 1) * 512], in_=ps[h][:, :],
                                                    func=AF.Sqrt, scale=1.0 / D, bias=EPSB[0][:, 0:1]),
                      reads=[ps_b[h], EPSB[1]], writes=[rs_b])
            sc.op("dve", lambda E: E.reciprocal(out=rstd[:, :], in_=rs[:, :]), reads=[rs_b], writes=[rstd_b])
            for c in range(16):
                k = c % 2
                sc.op("dve", lambda E: E.scalar_tensor_tensor(out=xo_t[k][:, :], in0=xs[:, c, :],
                                                              scalar=gft[:, c:c + 1], in1=rstd[:, :],
                                                              op0=ALU.mult, op1=ALU.mult),
                      reads=[xs_b[c], gft_b, rstd_b], writes=[xo_b[k]])
                sc.dma("sp", xo[c, :, tsl], xo_t[k][:, :], xo_b[k], reads=[xo_b[k]], is_output=True)
    sc.finish()
    es.close()
    return nc


PT = 512
N_FM = 28
N_FM_OUT = 20
TM_COLS = 1040


class PsumRing:
    def __init__(self, nc, es, n=8, base=0):
        self.t = [es.enter_context(nc.psum_tensor("pr%d" % (i + base), [128, 512], F32)) for i in range(n)]
        self.b = bufs(n)
        self.i = 0
        self.n = n

    def next(self):
        k = self.i % self.n
        self.i += 1
        return self.t[k], self.b[k]


def build_pp():
    nc = new_nc()
    dt_in = lambda name, shape, dt: nc.dram_tensor(name, shape, dt, kind="ExternalInput").ap()
    dt_out = lambda name, shape, dt: nc.dram_tensor(name, shape, dt, kind="ExternalOutput").ap()
    xT = dt_in("xT", [16, 128, TOK], F32)
    g = dt_in("g", [128, 16], F32)
    gq = dt_in("gq", [128, 4], F32)
    gkv = dt_in("gkv", [128, 4], F32)
    wfm = dt_in("wfm", [N_FM, 128, 16 * 128], BF16)
    wkr = dt_in("wkr", [128, 16 * 128], BF16)
    wtm = dt_in("wtm", [128, 16 * TM_COLS], BF16)
    wqn = dt_in("wqn", [128, 4 * 768], BF16)
    wqr = dt_in("wqr", [128, 4 * 384], BF16)
    wqs = dt_in("wqs", [128, 4 * 384], BF16)
    wkn = dt_in("wkn", [128, 4 * 768], BF16)
    wv = dt_in("wv", [128, 4 * 768], BF16)
    cos2 = dt_in("cos2", [64, TOK], F32)
    sgs = dt_in("sgs", [64, TOK], F32)
    o_qn = dt_out("o_qn", [6, 128, TOK], BF16)
    o_qr = dt_out("o_qr", [6, 64, TOK], BF16)
    o_kn = dt_out("o_kn", [6, 128, TOK], BF16)
    o_kpe = dt_out("o_kpe", [64, TOK], BF16)
    o_fm = dt_out("o_fm", [N_FM_OUT, 128, TOK], BF16)
    o_mv = dt_out("o_mv", [TOK, 6 * 129], BF16)
    o_dv = dt_out("o_dv", [TOK, 6 * 129], BF16)
    o_nvs = dt_out("o_nvs", [TOK, 129], BF16)
    o_nvw = dt_out("o_nvw", [TOK, 129], BF16)
    o_g = dt_out("o_g", [TOK, 12], F32)

    es = contextlib.ExitStack()
    sc = Sched(nc, es)
    sb = lambda name, shape, dt: es.enter_context(nc.sbuf_tensor(name, shape, dt))
    ones, ones_b = make_consts(sc, nc, es)
    pr = PsumRing(nc, es)

    def const_load(name, src, shape, dt):
        t = sb(name + "_s", shape, dt)
        b = Buf()
        sc.dma("sp", t[:, :], src, b, writes=[b])
        return t, b
    gt, gt_b = const_load("gt", g, [128, 16], F32)
    gqt, gq_b = const_load("gqt", gq, [128, 4], F32)
    gkvt, gkv_b = const_load("gkvt", gkv, [128, 4], F32)
    wtm_t, wtm_b = const_load("wtm_t", wtm, [128, 16 * TM_COLS], BF16)
    wkr_t, wkr_b = const_load("wkr_t", wkr, [128, 16 * 128], BF16)
    wqn_t, wqn_b = const_load("wqn_t", wqn, [128, 4 * 768], BF16)
    wqr_t, wqr_b = const_load("wqr_t", wqr, [128, 4 * 384], BF16)
    wqs_t, wqs_b = const_load("wqs_t", wqs, [128, 4 * 384], BF16)
    wkn_t, wkn_b = const_load("wkn_t", wkn, [128, 4 * 768], BF16)
    wv_t, wv_b = const_load("wv_t", wv, [128, 4 * 768], BF16)

    xs = sb("xs", [128, 16, PT], F32)
    xs_b = bufs(16)
    xn = sb("xn", [128, 16, PT], BF16)
    xn_b = bufs(16)
    sq = [sb("sq%d" % i, [128, PT], F32) for i in range(2)]
    sq_b = bufs(2)
    rs = sb("rs", [128, PT], F32)
    rs_b = Buf()
    rstd = sb("rstd", [128, PT], F32)
    rstd_b = Buf()
    NW = 3
    wch = [sb("wch%d" % i, [128, 16 * 128], BF16) for i in range(NW)]
    wch_b = bufs(NW)
    lat = sb("lat", [128, 8, PT], F32)
    lat_b = bufs(8)
    latn = sb("latn", [128, 8, PT], BF16)
    latn_b = bufs(8)
    NST = 4
    stg = [sb("stg%d" % i, [128, PT], BF16) for i in range(NST)]
    stg_b = bufs(NST)
    stg_i = [0]
    cs_t = sb("cs_t", [64, PT], F32)
    cs_b = Buf()
    sn_t = sb("sn_t", [64, PT], F32)
    sn_b = Buf()
    r1 = sb("r1", [64, PT], F32)
    r1_b = Buf()
    r2 = sb("r2", [64, PT], F32)
    r2_b = Buf()
    vst = [sb("vst%d" % i, [128, 6, 129], BF16) for i in range(2)]
    vst_b = bufs(2)
    nst = [sb("nst%d" % i, [128, 2, 129], BF16) for i in range(2)]
    nst_b = bufs(2)
    gst = [sb("gst%d" % i, [128, 12], F32) for i in range(2)]
    gst_b = bufs(2)
    for i in range(2):
        sc.op("dve", lambda E: E.memset(vst[i][:, :, :], 1.0), writes=[vst_b[i]])
        sc.op("dve", lambda E: E.memset(nst[i][:, :, :], 1.0), writes=[nst_b[i]])

    def stage_out(ps, ps_b, dst, npart=128, eng="act"):
        k = stg_i[0] % NST
        stg_i[0] += 1
        if eng == "act":
            sc.op("act", lambda E: E.activation(out=stg[k][:npart, :], in_=ps[:npart, :], func=AF.Copy),
                  reads=[ps_b], writes=[stg_b[k]])
        else:
            sc.op("dve", lambda E: E.tensor_copy(out=stg[k][:npart, :], in_=ps[:npart, :]),
                  reads=[ps_b], writes=[stg_b[k]])
        sc.dma("sp", dst, stg[k][:npart, :], stg_b[k], reads=[stg_b[k]], is_output=True)

    def rope_out(psA, psA_b, psB, psB_b, dst):
        sc.op("dve", lambda E: E.tensor_tensor(out=r1[:, :], in0=psA[:64, :], in1=cs_t[:, :], op=ALU.mult),
              reads=[psA_b, cs_b], writes=[r1_b])
        sc.op("dve", lambda E: E.tensor_tensor(out=r2[:, :], in0=psB[:64, :], in1=sn_t[:, :], op=ALU.mult),
              reads=[psB_b, sn_b], writes=[r2_b])
        k = stg_i[0] % NST
        stg_i[0] += 1
        sc.op("dve", lambda E: E.tensor_tensor(out=stg[k][:64, :], in0=r1[:, :], in1=r2[:, :], op=ALU.add),
              reads=[r1_b, r2_b], writes=[stg_b[k]])
        sc.dma("sp", dst, stg[k][:64, :], stg_b[k], reads=[stg_b[k]], is_output=True)

    wi = 0
    vi = 0
    for t in range(TOK // PT):
        tsl = slice(t * PT, (t + 1) * PT)
        for c in range(16):
            sc.dma("sp", xs[:, c, :], xT[c, :, tsl], xs_b[c], writes=[xs_b[c]])
        sc.dma("sp", cs_t[:, :], cos2[:, tsl], cs_b, writes=[cs_b])
        sc.dma("sp", sn_t[:, :], sgs[:, tsl], sn_b, writes=[sn_b])
        ps0, ps0_b = pr.next()
        rms_norm_fm(sc, nc, [xs[:, c, :] for c in range(16)], xs_b, [xn[:, c, :] for c in range(16)], xn_b,
                    gt, gt_b, ones, ones_b, [ps0], [ps0_b], sq, sq_b, rs, rs_b, rstd, rstd_b, 16, D, PT)
        pend = []

        def load_w(j):
            nonlocal wi
            k = wi % NW
            wi += 1
            sc.dma("sp", wch[k][:, :], wfm[j], wch_b[k], writes=[wch_b[k]])
            return k
        pend.append(load_w(0))
        for j in range(N_FM):
            if j + 1 < N_FM:
                pend.append(load_w(j + 1))
            k = pend.pop(0)
            ps, ps_b = pr.next()
            for kc in range(16):
                sc.op("pe", lambda E: E.matmul(ps[:, :], lhsT=wch[k][:, kc * 128:(kc + 1) * 128], rhs=xn[:, kc, :],
                                               start=(kc == 0), stop=(kc == 15)),
                      reads=[wch_b[k], xn_b[kc]], writes=[ps_b])
            if j < 8:
                sc.op("act", lambda E: E.activation(out=lat[:, j, :], in_=ps[:, :], func=AF.Copy),
                      reads=[ps_b], writes=[lat_b[j]])
            else:
                stage_out(ps, ps_b, o_fm[j - 8, :, tsl], eng=("act" if j % 2 else "dve"))
        psA, psA_b = pr.next()
        psB, psB_b = pr.next()
        for kc in range(16):
            sc.op("pe", lambda E: E.matmul(psA[:64, :], lhsT=wkr_t[:, kc * 128:kc * 128 + 64], rhs=xn[:, kc, :],
                                           start=(kc == 0), stop=(kc == 15)),
                  reads=[wkr_b, xn_b[kc]], writes=[psA_b])
        for kc in range(16):
            sc.op("pe", lambda E: E.matmul(psB[:64, :], lhsT=wkr_t[:, kc * 128 + 64:kc * 128 + 128], rhs=xn[:, kc, :],
                                           start=(kc == 0), stop=(kc == 15)),
                  reads=[wkr_b, xn_b[kc]], writes=[psB_b])
        rope_out(psA, psA_b, psB, psB_b, o_kpe[:, tsl])
        for sub in range(PT // 128):
            ssl = slice(sub * 128, (sub + 1) * 128)
            rows = slice(t * PT + sub * 128, t * PT + (sub + 1) * 128)
            k2 = vi % 2
            vi += 1
            pa, pa_b = pr.next()
            pb, pb_b = pr.next()
            pc, pc_b = pr.next()
            for (pp_, pp_b, c0, ncol) in ((pa, pa_b, 0, 512), (pb, pb_b, 512, 512), (pc, pc_b, 1024, 12)):
                for kc in range(16):
                    sc.op("pe", lambda E: E.matmul(pp_[:, :ncol], lhsT=xn[:, kc, ssl],
                                                   rhs=wtm_t[:, kc * TM_COLS + c0:kc * TM_COLS + c0 + ncol],
                                                   start=(kc == 0), stop=(kc == 15)),
                          reads=[wtm_b, xn_b[kc]], writes=[pp_b])
            sc.op("act", lambda E: E.activation(out=vst[k2][:, 0:4, 0:128],
                                                in_=pa[:, :].rearrange("p (h d) -> p h d", h=4), func=AF.Copy),
                  reads=[pa_b], writes=[vst_b[k2]])
            sc.op("dve", lambda E: E.tensor_copy(out=vst[k2][:, 4:6, 0:128],
                                                 in_=pb[:, 0:256].rearrange("p (h d) -> p h d", h=2)),
                  reads=[pb_b], writes=[vst_b[k2]])
            sc.op("dve", lambda E: E.tensor_copy(out=nst[k2][:, :, 0:128],
                                                 in_=pb[:, 256:512].rearrange("p (h d) -> p h d", h=2)),
                  reads=[pb_b], writes=[nst_b[k2]])
            sc.op("act", lambda E: E.activation(out=gst[k2][:, :], in_=pc[:, 0:12], func=AF.Sigmoid),
                  reads=[pc_b], writes=[gst_b[k2]])
            sc.dma("sp", o_dv[rows, :], vst[k2][:, :, :].rearrange("p h d -> p (h d)"), vst_b[k2],
                   reads=[vst_b[k2]], is_output=True)
            sc.dma("sp", o_nvs[rows, :], nst[k2][:, 0, :], nst_b[k2], reads=[nst_b[k2]], is_output=True)
            sc.dma("sp", o_nvw[rows, :], nst[k2][:, 1, :], nst_b[k2], reads=[nst_b[k2]], is_output=True)
            sc.dma("sp", o_g[rows, :], gst[k2][:, :], gst_b[k2], reads=[gst_b[k2]], is_output=True)
        ps1, ps1_b = pr.next()
        rms_norm_fm(sc, nc, [lat[:, c, :] for c in range(4)], lat_b[0:4], [latn[:, c, :] for c in range(4)],
                    latn_b[0:4], gqt, gq_b, ones, ones_b, [ps1], [ps1_b], sq, sq_b, rs, rs_b, rstd, rstd_b,
                    4, 512, PT)
        ps2, ps2_b = pr.next()
        rms_norm_fm(sc, nc, [lat[:, 4 + c, :] for c in range(4)], lat_b[4:8],
                    [latn[:, 4 + c, :] for c in range(4)], latn_b[4:8], gkvt, gkv_b, ones, ones_b, [ps2], [ps2_b],
                    sq, sq_b, rs, rs_b, rstd, rstd_b, 4, 512, PT)
        for h in range(6):
            ps, ps_b = pr.next()
            for kc in range(4):
                sc.op("pe", lambda E: E.matmul(ps[:, :], lhsT=wqn_t[:, kc * 768 + h * 128:kc * 768 + (h + 1) * 128],
                                               rhs=latn[:, kc, :], start=(kc == 0), stop=(kc == 3)),
                      reads=[wqn_b, latn_b[kc]], writes=[ps_b])
            stage_out(ps, ps_b, o_qn[h, :, tsl], eng="act")
            psA, psA_b = pr.next()
            psB, psB_b = pr.next()
            for kc in range(4):
                sc.op("pe", lambda E: E.matmul(psA[:64, :], lhsT=wqr_t[:, kc * 384 + h * 64:kc * 384 + (h + 1) * 64],
                                               rhs=latn[:, kc, :], start=(kc == 0), stop=(kc == 3)),
                      reads=[wqr_b, latn_b[kc]], writes=[psA_b])
            for kc in range(4):
                sc.op("pe", lambda E: E.matmul(psB[:64, :], lhsT=wqs_t[:, kc * 384 + h * 64:kc * 384 + (h + 1) * 64],
                                               rhs=latn[:, kc, :], start=(kc == 0), stop=(kc == 3)),
                      reads=[wqs_b, latn_b[kc]], writes=[psB_b])
            rope_out(psA, psA_b, psB, psB_b, o_qr[h, :, tsl])
            ps, ps_b = pr.next()
            for kc in range(4):
                sc.op("pe", lambda E: E.matmul(ps[:, :], lhsT=wkn_t[:, kc * 768 + h * 128:kc * 768 + (h + 1) * 128],
                                               rhs=latn[:, 4 + kc, :], start=(kc == 0), stop=(kc == 3)),
                      reads=[wkn_b, latn_b[4 + kc]], writes=[ps_b])
            stage_out(ps, ps_b, o_kn[h, :, tsl], eng="dve")
        for sub in range(PT // 128):
            ssl = slice(sub * 128, (sub + 1) * 128)
            rows = slice(t * PT + sub * 128, t * PT + (sub + 1) * 128)
            k2 = vi % 2
            vi += 1
            pa, pa_b = pr.next()
            pb, pb_b = pr.next()
            for (pp_, pp_b, c0, ncol) in ((pa, pa_b, 0, 512), (pb, pb_b, 512, 256)):
                for kc in range(4):
                    sc.op("pe", lambda E: E.matmul(pp_[:, :ncol], lhsT=latn[:, 4 + kc, ssl],
                                                   rhs=wv_t[:, kc * 768 + c0:kc * 768 + c0 + ncol],
                                                   start=(kc == 0), stop=(kc == 3)),
                          reads=[wv_b, latn_b[4 + kc]], writes=[pp_b])
            sc.op("act", lambda E: E.activation(out=vst[k2][:, 0:4, 0:128],
                                                in_=pa[:, :].rearrange("p (h d) -> p h d", h=4), func=AF.Copy),
                  reads=[pa_b], writes=[vst_b[k2]])
            sc.op("dve", lambda E: E.tensor_copy(out=vst[k2][:, 4:6, 0:128],
                                                 in_=pb[:, 0:256].rearrange("p (h d) -> p h d", h=2)),
                  reads=[pb_b], writes=[vst_b[k2]])
            sc.dma("sp", o_mv[rows, :], vst[k2][:, :, :].rearrange("p h d -> p (h d)"), vst_b[k2],
                   reads=[vst_b[k2]], is_output=True)
    sc.finish()
    es.close()
    return nc


def lay_gain(g):
    n = g.shape[0] // 128
    return np.ascontiguousarray(g.reshape(n, 128).T)


def lay_k(w):
    kc = w.shape[0] // 128
    return np.ascontiguousarray(w.reshape(kc, 128, -1).transpose(1, 0, 2)).reshape(128, -1)


def lay_w_in(w):
    wg = w[:, :DFF].reshape(16, 128, NFF, 128)
    wu = w[:, DFF:].reshape(16, 128, NFF, 128)
    a = np.stack([wg, wu], axis=3)
    return np.ascontiguousarray(a.transpose(2, 1, 0, 3, 4)).reshape(NFF, 128, 16 * 256)


def lay_w_out(w):
    a = w.reshape(NFF, 128, 16, 128)
    return np.ascontiguousarray(a.transpose(2, 1, 0, 3)).reshape(16, 128, NFF * 128)


FM_STARTS = ([0, 128, 256, 384] + [512 + 128 * i for i in range(4)] + [1088 + 128 * i for i in range(6)]
             + [1856 + 128 * i for i in range(6)] + [3392 + 128 * i for i in range(4)] + [3904, 4032, 4160, 4416])
SWAP64 = np.array([(i + 32) % 64 for i in range(64)])


def lay_pp_weights(w_mix_in, w_uq, w_ukv):
    o = {}
    o["wfm"] = np.stack([lay_k(w_mix_in[:, c:c + 128]) for c in FM_STARTS])
    kr = w_mix_in[:, 1024:1088]
    o["wkr"] = lay_k(np.concatenate([kr, kr[:, SWAP64]], axis=1))
    pad = np.zeros((D, 4), dtype=w_mix_in.dtype)
    o["wtm"] = lay_k(np.concatenate([w_mix_in[:, 2624:3392], w_mix_in[:, 4288:4416], w_mix_in[:, 4544:4672],
                                     w_mix_in[:, 4672:4684], pad], axis=1))
    uq = w_uq.reshape(512, 6, 192)
    o["wqn"] = lay_k(uq[:, :, :128].reshape(512, 768))
    o["wqr"] = lay_k(uq[:, :, 128:].reshape(512, 384))
    o["wqs"] = lay_k(uq[:, :, 128:][:, :, SWAP64].reshape(512, 384))
    ukv = w_ukv.reshape(512, 6, 256)
    o["wkn"] = lay_k(ukv[:, :, :128].reshape(512, 768))
    o["wv"] = lay_k(ukv[:, :, 128:].reshape(512, 768))
    return o


def core_positions(r):
    i = np.arange(8)[:, None]
    u = np.arange(512)[None, :]
    return (512 * (4 * i + r) + u).reshape(-1)


def rope_tables(pos):
    inv = (np.float32(10000.0) ** (-np.arange(32, dtype=np.float32) / np.float32(32))).astype(np.float32)
    ang = pos.astype(np.float32)[None, :] * inv[:, None]
    c, s = np.cos(ang).astype(np.float32), np.sin(ang).astype(np.float32)
    return np.concatenate([c, c], 0), np.concatenate([-s, s], 0)


NEG = -30000.0
SLAB = 128 * 129
TABW = 2944
MLA_SCALE = 192.0 ** -0.5
HD_SCALE = 128.0 ** -0.5


def att_block(sc, steps, sbanks, pTs, accs, nv):
    n = len(steps)
    ns, npt = len(sbanks), len(pTs)

    def emit_S(k):
        st = steps[k]
        bank, bb = sbanks[k % ns]
        m = len(st["mm"])
        for idx, (l, r, rb) in enumerate(st["mm"]):
            sc.op("pe", lambda E: E.matmul(bank[:, :], lhsT=l, rhs=r, start=(idx == 0), stop=(idx == m - 1)),
                  reads=rb, writes=[bb])

    emit_S(0)
    for k in range(n):
        if k + 1 < n:
            emit_S(k + 1)
        st = steps[k]
        bank, bb = sbanks[k % ns]
        pt, pb = pTs[k % npt]
        if st["bias"] is None:
            sc.op("act", lambda E: E.activation(out=pt[:, :], in_=bank[:, :], func=AF.Exp, scale=st["scale"]),
                  reads=[bb], writes=[pb])
        else:
            bap, bbuf = st["bias"]
            sc.op("act", lambda E: E.activation(out=pt[:, :], in_=bank[:, :], func=AF.Exp, scale=st["scale"],
                                                bias=bap),
                  reads=[bb, bbuf], writes=[pb])
        vap, vb = st["v"]
        for qs in range(4):
            acc, ab = accs[qs]
            sc.op("pe", lambda E: E.matmul(acc[:, :nv], lhsT=pt[:, qs * 128:(qs + 1) * 128], rhs=vap,
                                           start=(k == 0), stop=(k == n - 1)),
                  reads=[pb] + vb, writes=[ab])


def build_p2(do_mla=True, do_dil=True, do_nsa=True, nblk=8):
    nc = new_nc()
    dt_in = lambda name, shape, dt: nc.dram_tensor(name, shape, dt, kind="ExternalInput").ap()
    dt_out = lambda name, shape, dt: nc.dram_tensor(name, shape, dt, kind="ExternalOutput").ap()
    m_qn = dt_in("m_qn", [6, 128, TOK], BF16)
    m_qr = dt_in("m_qr", [6, 64, TOK], BF16)
    m_kn = dt_in("m_kn", [6, 128, S], BF16)
    m_kpe = dt_in("m_kpe", [64, S], BF16)
    m_v = dt_in("m_v", [6, 128, SLAB], BF16)
    tc_d = dt_in("tc", [128, TABW], BF16)
    tw_d = dt_in("tw", [128, TABW], BF16)
    ident_d = dt_in("ident", [128, 128], BF16)
    o_mla = dt_out("o_mla", [TOK, 768], BF16)
    d_q = dt_in("d_q", [6, 128, 32 * 128], BF16)
    d_k = dt_in("d_k", [6, 128, 33 * 128], BF16)
    d_v = dt_in("d_v", [6, 128, 33 * 129], BF16)
    d_bias = dt_in("d_bias", [128, 8 * 256], F32)
    o_dil = dt_out("o_dil", [6, 32 * 128, 129], F32)
    n_q = dt_in("n_q", [4, 128, TOK], BF16)
    n_g = dt_in("n_g", [128, 32 * 12], F32)
    n_kc = dt_in("n_kc", [128, S], BF16)
    n_vc = dt_in("n_vc", [128, S], BF16)
    n_ks = dt_in("n_ks", [128, S], BF16)
    n_vs = dt_in("n_vs", [128, SLAB], BF16)
    n_kw = dt_in("n_kw", [128, S], BF16)
    n_vw = dt_in("n_vw", [128, SLAB], BF16)
    w1k_d = dt_in("w1k", [128, 32 * 128], BF16)
    w1v_d = dt_in("w1v", [128, 32 * 128], BF16)
    w2k_d = dt_in("w2k", [128, 128], BF16)
    w2v_d = dt_in("w2v", [128, 128], BF16)
    posT_d = dt_in("posT", [128, 32], BF16)
    cmpsel_d = dt_in("cmpsel", [128, 8, 257], BF16)
    ewide_d = dt_in("ewide", [128, 8192], BF16)
    cmask_d = dt_in("cmask", [128, 512], BF16)
    cmaskp_d = dt_in("cmaskp", [128, 512], BF16)
    cap_d = dt_in("cap", [128, 512], F32)
    floor_d = dt_in("floor", [128, 512], F32)
    bcmp_d = dt_in("bcmp", [128, 32], F32)
    bslc_d = dt_in("bslc", [128, 512], F32)
    bwin_d = dt_in("bwin", [128, 80], F32)
    o_nsa = dt_out("o_nsa", [TOK, 512], BF16)

    es = contextlib.ExitStack()
    sc = Sched(nc, es)
    sb = lambda name, shape, dt: es.enter_context(nc.sbuf_tensor(name, shape, dt))

    def const_load(name, src, shape, dt):
        t = sb(name + "_s", shape, dt)
        b = Buf()
        sc.dma("sp", t[:, :], src, b, writes=[b])
        return t, b

    A0 = sb("A0", [128, SLAB], BF16)
    A1 = sb("A1", [128, SLAB], BF16)
    BS = sb("BS", [128, S], BF16)
    A0_b, A1_b, BS_b = Buf(), Buf(), Buf()
    tc_t, tc_b = const_load("tc_t", tc_d, [128, TABW], BF16)
    tw_t, tw_b = const_load("tw_t", tw_d, [128, TABW], BF16)
    ident, ident_b = const_load("ident_t", ident_d, [128, 128], BF16)
    sbank = [(es.enter_context(nc.psum_tensor("sbk%d" % i, [128, 512], F32)), Buf()) for i in range(3)]
    accs = [(es.enter_context(nc.psum_tensor("acc%d" % i, [128, 512], F32)), Buf()) for i in range(4)]
    tp_ps = es.enter_context(nc.psum_tensor("tp_ps", [128, 128], BF16))
    tp_b = Buf()
    pTs = [(sb("pT%d" % i, [128, 512], BF16), Buf()) for i in range(3)]
    ost = [sb("ost%d" % i, [128, 512], BF16) for i in range(2)]
    ost_b = bufs(2)
    rd = [sb("rd%d" % i, [128, 1], F32) for i in range(4)]
    rd_b = bufs(4)
    rdi = [0]

    def next_rd():
        k = rdi[0] % 4
        rdi[0] += 1
        return rd[k], rd_b[k]

    if do_mla:
        qn_t = [sb("qn_t%d" % i, [128, 512], BF16) for i in range(2)]
        qn_b = bufs(2)
        qr_t = [sb("qr_t%d" % i, [128, 512], BF16) for i in range(2)]
        qr_b = bufs(2)
        for i in range(2):
            sc.op("dve", lambda E: E.memset(qr_t[i][64:128, :], 0.0), writes=[qr_b[i]])
        sc.op("dve", lambda E: E.memset(BS[64:128, :], 0.0), writes=[BS_b])
        sc.dma("sp", BS[0:64, :], m_kpe, BS_b, writes=[BS_b])
        qi = 0
        oi = 0
        for h in range(6):
            sc.dma("sp", A0[:, :S], m_kn[h], A0_b, writes=[A0_b])
            sc.dma("sp", A1[:, :], m_v[h], A1_b, writes=[A1_b])
            for i in range(nblk):
                k = qi % 2
                qi += 1
                qsl = slice(i * 512, (i + 1) * 512)
                sc.dma("sp", qn_t[k][:, :], m_qn[h, :, qsl], qn_b[k], writes=[qn_b[k]])
                sc.dma("sp", qr_t[k][0:64, :], m_qr[h, :, qsl], qr_b[k], writes=[qr_b[k]])
                steps = []
                for kt in range(16 * i + 16):
                    ks = slice(kt * 128, (kt + 1) * 128)
                    mm = [(A0[:, ks], qn_t[k][:, :], [A0_b, qn_b[k]]),
                          (BS[:, ks], qr_t[k][:, :], [BS_b, qr_b[k]])]
                    if kt >= 16 * i:
                        off = 1920 - 128 * (kt - 16 * i)
                        mm.append((ident[:, :], tc_t[:, off:off + 512], [ident_b, tc_b]))
                    steps.append(dict(mm=mm, scale=MLA_SCALE, bias=None,
                                      v=(A1[:, kt * 129:(kt + 1) * 129], [A1_b])))
                att_block(sc, steps, sbank, pTs, accs, 129)
                for qs in range(4):
                    acc, ab = accs[qs]
                    r_t, r_b = next_rd()
                    sc.op("dve", lambda E: E.reciprocal(out=r_t[:, :], in_=acc[:, 128:129]), reads=[ab], writes=[r_b])
                    o = oi % 2
                    oi += 1
                    sc.op("dve", lambda E: E.tensor_scalar(out=ost[o][:, 0:128], in0=acc[:, 0:128],
                                                           scalar1=r_t[:, 0:1], scalar2=None, op0=ALU.mult),
                          reads=[ab, r_b], writes=[ost_b[o]])
                    rows = slice(i * 512 + qs * 128, i * 512 + (qs + 1) * 128)
                    sc.dma("sp", o_mla[rows, h * 128:(h + 1) * 128], ost[o][:, 0:128], ost_b[o],
                           reads=[ost_b[o]], is_output=True)

    if do_dil:
        dbias, dbias_b = const_load("dbias", d_bias, [128, 8 * 256], F32)
        dsT = [sb("dsT%d" % i, [128, 256], F32) for i in range(2)]
        dsT_b = bufs(2)
        dpT = [sb("dpT%d" % i, [128, 256], BF16) for i in range(2)]
        dpT_b = bufs(2)
        dst = [sb("dst%d" % i, [128, 129], F32) for i in range(2)]
        dst_b = bufs(2)
        dq_t = A0[:, 0:4096]
        dk_t = A0[:, 4096:4096 + 33 * 128]
        dv_t = A1[:, 0:33 * 129]
        di = 0
        for hd in range(6):
            g = hd // 2
            sc.dma("sp", dq_t, d_q[hd], A0_b, writes=[A0_b])
            sc.dma("sp", dk_t, d_k[hd], A0_b, writes=[A0_b])
            sc.dma("sp", dv_t, d_v[hd], A1_b, writes=[A1_b])
            for n in range(32):
                if g == 0:
                    seq_start = False
                    tab = hd if n == 0 else 2 + hd
                else:
                    seq_start = (n == 0) if g == 1 else (n % 8 == 0)
                    tab = 2 + hd
                k = di % 2
                di += 1
                qap = dq_t[:, n * 128:(n + 1) * 128]
                bA, bA_b = sbank[(2 * di) % 3]
                bB, bB_b = sbank[(2 * di + 1) % 3]
                if not seq_start:
                    sc.op("pe", lambda E: E.matmul(bA[:, 0:128], lhsT=dk_t[:, n * 128:(n + 1) * 128], rhs=qap,
                                                   start=True, stop=True), reads=[A0_b], writes=[bA_b])
                sc.op("pe", lambda E: E.matmul(bB[:, 0:128], lhsT=dk_t[:, (n + 1) * 128:(n + 2) * 128], rhs=qap,
                                               start=True, stop=True), reads=[A0_b], writes=[bB_b])
                c0 = 0 if not seq_start else 128
                if not seq_start:
                    sc.op("dve", lambda E: E.scalar_tensor_tensor(out=dsT[k][:, 0:128], in0=bA[:, 0:128],
                                                                  scalar=HD_SCALE,
                                                                  in1=dbias[:, tab * 256:tab * 256 + 128],
                                                                  op0=ALU.mult, op1=ALU.add),
                          reads=[bA_b, dbias_b], writes=[dsT_b[k]])
                sc.op("dve", lambda E: E.scalar_tensor_tensor(out=dsT[k][:, 128:256], in0=bB[:, 0:128],
                                                              scalar=HD_SCALE,
                                                              in1=dbias[:, tab * 256 + 128:tab * 256 + 256],
                                                              op0=ALU.mult, op1=ALU.add),
                      reads=[bB_b, dbias_b], writes=[dsT_b[k]])
                sc.op("act", lambda E: E.activation(out=dpT[k][:, c0:256], in_=dsT[k][:, c0:256], func=AF.Exp),
                      reads=[dsT_b[k]], writes=[dpT_b[k]])
                acc, ab = accs[di % 4]
                if not seq_start:
                    sc.op("pe", lambda E: E.matmul(acc[:, :129], lhsT=dpT[k][:, 0:128],
                                                   rhs=dv_t[:, n * 129:(n + 1) * 129], start=True, stop=False),
                          reads=[dpT_b[k], A1_b], writes=[ab])
                sc.op("pe", lambda E: E.matmul(acc[:, :129], lhsT=dpT[k][:, 128:256],
                                               rhs=dv_t[:, (n + 1) * 129:(n + 2) * 129], start=seq_start, stop=True),
                      reads=[dpT_b[k], A1_b], writes=[ab])
                sc.op("act", lambda E: E.activation(out=dst[k][:, :], in_=acc[:, :129], func=AF.Copy),
                      reads=[ab], writes=[dst_b[k]])
                sc.dma("sp", o_dil[hd, n * 128:(n + 1) * 128, :], dst[k][:, :], dst_b[k], reads=[dst_b[k]],
                       is_output=True)

    if do_nsa:
        cmask, cmask_b = const_load("cmask", cmask_d, [128, 512], BF16)
        cmaskp, cmaskp_b = const_load("cmaskp", cmaskp_d, [128, 512], BF16)
        capt, cap_b = const_load("capt", cap_d, [128, 512], F32)
        floort, floor_b = const_load("floort", floor_d, [128, 512], F32)
        bcmp, bcmp_b = const_load("bcmp", bcmp_d, [128, 32], F32)
        bslc, bslc_b = const_load("bslc", bslc_d, [128, 512], F32)
        bwin, bwin_b = const_load("bwin", bwin_d, [128, 80], F32)
        g_sb, g_b = const_load("g_sb", n_g, [128, 32 * 12], F32)
        w2k, w2k_b = const_load("w2k_t", w2k_d, [128, 128], BF16)
        w2v, w2v_b = const_load("w2v_t", w2v_d, [128, 128], BF16)
        posT, posT_b = const_load("posT_t", posT_d, [128, 32], BF16)
        sc.dma("sp", BS[:, 0:8192], ewide_d, BS_b, writes=[BS_b])
        sc.dma("sp", BS[:, 8192:12288], w1k_d, BS_b, writes=[BS_b])
        sc.dma("sp", BS[:, 12288:16384], w1v_d, BS_b, writes=[BS_b])
        ewide = BS[:, 0:8192]
        vcx = sb("vcx", [128, 8, 385], BF16)
        vcx_b = Buf()
        sc.dma("sp", vcx[:, :, 128:385], cmpsel_d, vcx_b, writes=[vcx_b])
        kcT = sb("kcT", [128, 1024], BF16)
        kcT_b = Buf()
        hT = [sb("hT%d" % i, [128, 1024], BF16) for i in range(2)]
        hT_b = bufs(2)
        bcol = [sb("bcol%d" % i, [128, 1], F32) for i in range(2)]
        bcol_b = bufs(2)
        sc.dma("sp", A0[:, :S], n_kc, A0_b, writes=[A0_b])
        sc.dma("sp", A1[:, :S], n_vc, A1_b, writes=[A1_b])
        for w, (src, src_b) in enumerate(((A0, A0_b), (A1, A1_b))):
            w1 = BS[:, 8192 + w * 4096:8192 + (w + 1) * 4096]
            sc.op("dve", lambda E: E.memset(hT[w][:, :], 0.0), writes=[hT_b[w]])
            bk, bkb = sbank[2]
            for l in range(32):
                sc.op("pe", lambda E: E.matmul(bk[:, 0:1], lhsT=w1[:, l * 128:(l + 1) * 128], rhs=posT[:, l:l + 1],
                                               start=(l == 0), stop=(l == 31)),
                      reads=[BS_b, posT_b], writes=[bkb])
            sc.op("dve", lambda E: E.tensor_copy(out=bcol[w][:, :], in_=bk[:, 0:1]), reads=[bkb], writes=[bcol_b[w]])
            for c2 in range(2):
                ncol = 512 if c2 == 0 else 511
                bank, bb = sbank[c2]
                for l in range(32):
                    st0 = 16 * 512 * c2 + l
                    sc.op("pe", lambda E: E.matmul(bank[:, :ncol], lhsT=w1[:, l * 128:(l + 1) * 128],
                                                   rhs=src[:, st0:st0 + 16 * ncol:16],
                                                   start=(l == 0), stop=(l == 31)),
                          reads=[BS_b, src_b], writes=[bb])
                sc.op("act", lambda E: E.activation(out=hT[w][:, c2 * 512:c2 * 512 + ncol], in_=bank[:, :ncol],
                                                    func=AF.Silu, bias=bcol[w][:, 0:1]),
                      reads=[bb, bcol_b[w]], writes=[hT_b[w]])
        for c2 in range(2):
            bank, bb = sbank[c2]
            sc.op("pe", lambda E: E.matmul(bank[:, :], lhsT=w2k[:, :], rhs=hT[0][:, c2 * 512:(c2 + 1) * 512],
                                           start=True, stop=True), reads=[w2k_b, hT_b[0]], writes=[bb])
            sc.op("dve", lambda E: E.tensor_copy(out=kcT[:, c2 * 512:(c2 + 1) * 512], in_=bank[:, :]),
                  reads=[bb], writes=[kcT_b])
        for ct in range(8):
            bank, bb = sbank[ct % 3]
            sc.op("pe", lambda E: E.matmul(bank[:, 0:128], lhsT=hT[1][:, ct * 128:(ct + 1) * 128], rhs=w2v[:, :],
                                           start=True, stop=True), reads=[w2v_b, hT_b[1]], writes=[bb])
            sc.op("dve", lambda E: E.tensor_copy(out=vcx[:, ct, 0:128], in_=bank[:, 0:128]),
                  reads=[bb], writes=[vcx_b])
        sc.dma("sp", A0[:, :S], n_ks, A0_b, writes=[A0_b])
        sc.dma("sp", A1[:, :], n_vs, A1_b, writes=[A1_b])
        nq_t = sb("nq_t", [128, 4, 512], BF16)
        nq_b = Buf()
        kw_t = sb("kw_t", [128, 20 * 128], BF16)
        vw_t = sb("vw_t", [128, 20 * 129], BF16)
        kw_b, vw_b = Buf(), Buf()
        score = sb("score", [128, 4, 256], F32)
        score_b = bufs(4)
        s2 = sb("s2", [128, 256], F32)
        s2_b = Buf()
        work = sb("work", [128, 256], F32)
        work_b = Buf()
        m8 = sb("m8", [128, 16], F32)
        m8_b = Buf()
        negsel = sb("negsel", [128, 256], BF16)
        negsel_b = Buf()
        negselT = sb("negselT", [128, 2, 512], BF16)
        negselT_b = Buf()
        nso = sb("nso", [128, 4, 512], F32)
        nso_b = bufs(4)
        cf = [sb("cf%d" % i, [128, 1], F32) for i in range(4)]
        cf_b = bufs(4)
        cfi = [0]
        oi = 0

        def coef(acc, ab, dcol, gcol, tile_n, eps):
            r_t, r_b = next_rd()
            if eps:
                sc.op("dve", lambda E: E.tensor_scalar(out=r_t[:, :], in0=acc[:, dcol:dcol + 1], scalar1=1e-30,
                                                       scalar2=None, op0=ALU.add), reads=[ab], writes=[r_b])
                sc.op("dve", lambda E: E.reciprocal(out=r_t[:, :], in_=r_t[:, :]), reads=[r_b], writes=[r_b])
            else:
                sc.op("dve", lambda E: E.reciprocal(out=r_t[:, :], in_=acc[:, dcol:dcol + 1]), reads=[ab],
                      writes=[r_b])
            k = cfi[0] % 4
            cfi[0] += 1
            sc.op("dve", lambda E: E.tensor_tensor(out=cf[k][:, :], in0=r_t[:, :],
                                                   in1=g_sb[:, tile_n * 12 + gcol:tile_n * 12 + gcol + 1],
                                                   op=ALU.mult), reads=[r_b, g_b], writes=[cf_b[k]])
            return (r_t, r_b), (cf[k], cf_b[k])

        for i in range(nblk):
            qsl = slice(i * 512, (i + 1) * 512)
            for h in range(4):
                sc.dma("sp", nq_t[:, h, :], n_q[h, :, qsl], nq_b, writes=[nq_b])
            kt0 = 16 * i - 4
            j0 = 4 if i == 0 else 0
            sc.dma("sp", kw_t[:, j0 * 128:20 * 128], n_kw[:, (kt0 + j0) * 128:(kt0 + 20) * 128], kw_b, writes=[kw_b])
            sc.dma("sp", vw_t[:, j0 * 129:20 * 129], n_vw[:, (kt0 + j0) * 129:(kt0 + 20) * 129], vw_b, writes=[vw_b])
            for h in range(4):
                steps = []
                for ct in range(i + 1):
                    mm = [(kcT[:, ct * 128:(ct + 1) * 128], nq_t[:, h, :], [kcT_b, nq_b])]
                    if ct == i:
                        mm.append((ident[:, :], cmask[:, :], [ident_b, cmask_b]))
                    if ct == i - 1:
                        mm.append((ident[:, :], cmaskp[:, :], [ident_b, cmaskp_b]))
                    bi = h * 8 + (ct - i + 7)
                    steps.append(dict(mm=mm, scale=HD_SCALE, bias=(bcmp[:, bi:bi + 1], bcmp_b),
                                      v=(vcx[:, ct, :], [vcx_b])))
                att_block(sc, steps, sbank, pTs, accs, 385)
                for qs in range(4):
                    acc, ab = accs[qs]
                    (r_t, r_b), (c_t, c_b) = coef(acc, ab, 384, 3 * h + 0, 4 * i + qs, True)
                    sc.op("dve", lambda E: E.tensor_scalar(out=nso[:, qs, h * 128:(h + 1) * 128], in0=acc[:, 0:128],
                                                           scalar1=c_t[:, 0:1], scalar2=None, op0=ALU.mult),
                          reads=[ab, c_b], writes=[nso_b[qs]])
                    if h == 0:
                        sc.op("dve", lambda E: E.tensor_scalar(out=score[:, qs, :], in0=acc[:, 128:384],
                                                               scalar1=r_t[:, 0:1], scalar2=None, op0=ALU.mult),
                              reads=[ab, r_b], writes=[score_b[qs]])
                    else:
                        sc.op("dve", lambda E: E.scalar_tensor_tensor(out=score[:, qs, :], in0=acc[:, 128:384],
                                                                      scalar=r_t[:, 0:1], in1=score[:, qs, :],
                                                                      op0=ALU.mult, op1=ALU.add),
                              reads=[ab, r_b, score_b[qs]], writes=[score_b[qs]])
            for qs in range(4):
                off = 256 - 32 * i - 2 * qs
                sc.op("dve", lambda E: E.tensor_tensor(out=s2[:, :], in0=score[:, qs, :], in1=capt[:, off:off + 256],
                                                       op=ALU.min), reads=[score_b[qs], cap_b], writes=[s2_b])
                sc.op("dve", lambda E: E.tensor_tensor(out=s2[:, :], in0=s2[:, :], in1=floort[:, off:off + 256],
                                                       op=ALU.max), reads=[s2_b, floor_b], writes=[s2_b])
                sc.op("dve", lambda E: E.tensor_scalar(out=s2[:, 0:1], in0=s2[:, 0:1], scalar1=100.0, scalar2=None,
                                                       op0=ALU.max), reads=[s2_b], writes=[s2_b])
                sc.op("dve", lambda E: E.max(out=m8[:, 0:8], in_=s2[:, :]), reads=[s2_b], writes=[m8_b])
                sc.op("dve", lambda E: E.match_replace(out=work[:, :], in_to_replace=m8[:, 0:8], in_values=s2[:, :],
                                                       imm_value=-1e9), reads=[s2_b, m8_b], writes=[work_b])
                sc.op("dve", lambda E: E.max(out=m8[:, 8:16], in_=work[:, :]), reads=[work_b], writes=[m8_b])
                sc.op("dve", lambda E: E.tensor_scalar(out=negsel[:, :], in0=s2[:, :], scalar1=m8[:, 15:16],
                                                       scalar2=None, op0=ALU.is_lt),
                      reads=[s2_b, m8_b], writes=[negsel_b])
                for jh in range(2):
                    sc.op("pe", lambda E: E.transpose(out=tp_ps[:, :], in_=negsel[:, jh * 128:(jh + 1) * 128],
                                                      identity=ident[:, :]),
                          reads=[negsel_b, ident_b], writes=[tp_b])
                    sc.op("dve", lambda E: E.tensor_copy(out=negselT[:, jh, qs * 128:(qs + 1) * 128], in_=tp_ps[:, :]),
                          reads=[tp_b], writes=[negselT_b])
            for h in range(4):
                steps = []
                for kt in range(16 * i + 16):
                    ks = slice(kt * 128, (kt + 1) * 128)
                    e0 = 128 * (kt % 64)
                    mm = [(A0[:, ks], nq_t[:, h, :], [A0_b, nq_b]),
                          (ewide[:, e0:e0 + 128], negselT[:, kt // 64, :], [BS_b, negselT_b])]
                    if kt >= 16 * i:
                        off = 1920 - 128 * (kt - 16 * i)
                        mm.append((ident[:, :], tc_t[:, off:off + 512], [ident_b, tc_b]))
                    bi = h * 128 + (kt - 16 * i + 112)
                    steps.append(dict(mm=mm, scale=HD_SCALE, bias=(bslc[:, bi:bi + 1], bslc_b),
                                      v=(A1[:, kt * 129:(kt + 1) * 129], [A1_b])))
                att_block(sc, steps, sbank, pTs, accs, 129)
                for qs in range(4):
                    acc, ab = accs[qs]
                    (r_t, r_b), (c_t, c_b) = coef(acc, ab, 128, 3 * h + 1, 4 * i + qs, False)
                    sc.op("dve", lambda E: E.scalar_tensor_tensor(out=nso[:, qs, h * 128:(h + 1) * 128],
                                                                  in0=acc[:, 0:128], scalar=c_t[:, 0:1],
                                                                  in1=nso[:, qs, h * 128:(h + 1) * 128],
                                                                  op0=ALU.mult, op1=ALU.add),
                          reads=[ab, c_b, nso_b[qs]], writes=[nso_b[qs]])
                steps = []
                for jw in range(j0, 20):
                    off = 1920 - 128 * (jw - 4)
                    mm = [(kw_t[:, jw * 128:(jw + 1) * 128], nq_t[:, h, :], [kw_b, nq_b]),
                          (ident[:, :], tw_t[:, off:off + 512], [ident_b, tw_b])]
                    bi = h * 20 + jw
                    steps.append(dict(mm=mm, scale=HD_SCALE, bias=(bwin[:, bi:bi + 1], bwin_b),
                                      v=(vw_t[:, jw * 129:(jw + 1) * 129], [vw_b])))
                att_block(sc, steps, sbank, pTs, accs, 129)
                for qs in range(4):
                    acc, ab = accs[qs]
                    (r_t, r_b), (c_t, c_b) = coef(acc, ab, 128, 3 * h + 2, 4 * i + qs, False)
                    sc.op("dve", lambda E: E.scalar_tensor_tensor(out=nso[:, qs, h * 128:(h + 1) * 128],
                                                                  in0=acc[:, 0:128], scalar=c_t[:, 0:1],
                                                                  in1=nso[:, qs, h * 128:(h + 1) * 128],
                                                                  op0=ALU.mult, op1=ALU.add),
                          reads=[ab, c_b, nso_b[qs]], writes=[nso_b[qs]])
            for qs in range(4):
                o = oi % 2
                oi += 1
                sc.op("act", lambda E: E.activation(out=ost[o][:, :], in_=nso[:, qs, :], func=AF.Copy),
                      reads=[nso_b[qs]], writes=[ost_b[o]])
                rows = slice(i * 512 + qs * 128, i * 512 + (qs + 1) * 128)
                sc.dma("sp", o_nsa[rows, :], ost[o][:, :], ost_b[o], reads=[ost_b[o]], is_output=True)
    sc.finish()
    es.close()
    return nc


def alibi_slopes_np():
    return (2.0 ** (-8.0 * np.arange(1, 11, dtype=np.float64) / 10)).astype(np.float32)


DIL_D = (1, 4, 16)


def p2_tables(r):
    t = {}
    k = np.arange(128)[:, None]
    y = np.arange(TABW)[None, :]
    rel = (y - 1920) + 512 * r
    t["tc"] = np.where(k <= rel, 0.0, NEG).astype(NPBF)
    t["tw"] = np.where((rel - k >= 0) & (rel - k < 512), 0.0, NEG).astype(NPBF)
    t["ident"] = np.eye(128, dtype=np.float32).astype(NPBF)
    sl = alibi_slopes_np()
    a = np.arange(128)[None, :]
    c = np.arange(128)[:, None]
    db = np.zeros((128, 8, 256), np.float32)
    for tab in range(8):
        hd = tab if tab < 2 else tab - 2
        d = DIL_D[hd // 2]
        jp = 128 + a - c
        jc = a - c
        prev = np.where(a <= c, -sl[hd] * d * jp, NEG)
        cur = np.where(jc >= 0, -sl[hd] * d * jc, NEG)
        if tab < 2 and r == 0:
            prev = np.full((128, 128), NEG)
        db[:, tab, :128] = prev
        db[:, tab, 128:] = cur
    t["d_bias"] = db.reshape(128, 8 * 256)
    q = np.arange(512)[None, :]
    t["cmask"] = np.where(16 * c + 31 <= 512 * r + q, 0.0, NEG).astype(NPBF)
    t["cmaskp"] = np.where(16 * (c - 128) + 31 <= 512 * r + q, 0.0, NEG).astype(NPBF)
    u = np.arange(512)[None, :]
    dlt = u - 256 - 8 * r - (np.arange(128)[:, None] // 64)
    t["cap"] = np.where(dlt > 0, -1.0, 1e9).astype(np.float32)
    t["floor"] = np.where((dlt == 0) | (dlt == -1), 100.0, -1e9).astype(np.float32)
    ns = sl[6:10].astype(np.float64)
    p = np.arange(128)[:, None]
    bc = np.zeros((128, 4, 8))
    bs = np.zeros((128, 4, 128))
    bw = np.zeros((128, 4, 20))
    for h in range(4):
        bc[:, h, :] = ns[h] * (2048 * (np.arange(8)[None, :] - 7) + 16 * p + 15.5 - 512 * r)
        bs[:, h, :] = ns[h] * (128 * (np.arange(128)[None, :] - 112) + p - 512 * r)
        bw[:, h, :] = ns[h] * (128 * (np.arange(20)[None, :] - 4) + p - 512 * r)
    t["bcmp"] = bc.reshape(128, 32).astype(np.float32)
    t["bslc"] = bs.reshape(128, 512).astype(np.float32)
    t["bwin"] = bw.reshape(128, 80).astype(np.float32)
    x = np.arange(8192)[None, :]
    t["ewide"] = np.where(p == x // 64, NEG, 0.0).astype(NPBF)
    cg = np.arange(1024).reshape(8, 128)
    j = np.arange(256)[None, None, :]
    M = ((cg[:, :, None] >= 4 * j - 1) & (cg[:, :, None] <= 4 * j + 3)).astype(np.float32)
    M = np.concatenate([M, np.ones((8, 128, 1), np.float32)], axis=2)
    t["cmpsel"] = np.ascontiguousarray(M.transpose(1, 0, 2)).astype(NPBF)
    return t


def dil_index(g, r):
    if g == 0:
        return 4096 * r + np.arange(4096)
    if g == 1:
        return np.arange(4096) * 4 + r
    return (np.arange(1024)[None, :] * 16 + (4 * r + np.arange(4))[:, None]).reshape(-1)


def tok_major_tiles(v):
    n = v.shape[0] // 128
    return np.ascontiguousarray(v.reshape(n, 128, -1).transpose(1, 0, 2)).reshape(128, -1)


def prep_p2(r, nat, own, tab=None):
    m = dict(p2_tables(r) if tab is None else tab)
    m["m_qn"], m["m_qr"], m["n_q"], m["n_g"] = own["qn"], own["qr"], own["nq"], own["g"]
    m["m_kn"], m["m_kpe"] = nat["kn"], nat["kpe"]
    m["m_v"] = np.stack([tok_major_tiles(nat["mv"][:, h, :]) for h in range(6)])
    dq, dk, dv = [], [], []
    for hd in range(6):
        g = hd // 2
        idx = dil_index(g, r)
        dq.append(nat["dq"][hd][:, idx])
        if g == 0 and r > 0:
            pk = nat["dk"][hd][:, 4096 * r - 128:4096 * r]
            pv = nat["dv"][4096 * r - 128:4096 * r, hd, :]
        else:
            pk = np.zeros((128, 128), NPBF)
            pv = np.zeros((128, 129), NPBF)
        dk.append(np.concatenate([pk, nat["dk"][hd][:, idx]], axis=1))
        dv.append(tok_major_tiles(np.concatenate([pv, nat["dv"][idx, hd, :]], axis=0)))
    m["d_q"], m["d_k"], m["d_v"] = np.stack(dq), np.stack(dk), np.stack(dv)
    m["n_kc"], m["n_vc"], m["n_ks"], m["n_kw"] = nat["nkc"], nat["nvc"], nat["nks"], nat["nkw"]
    m["n_vs"] = tok_major_tiles(nat["nvs"])
    m["n_vw"] = tok_major_tiles(nat["nvw"])
    for k in ("w1k", "w1v", "w2k", "w2v", "posT"):
        m[k] = nat[k]
    return {k: np.ascontiguousarray(v) for k, v in m.items()}


def build_po():
    nc = new_nc()
    dt_in = lambda name, shape, dt: nc.dram_tensor(name, shape, dt, kind="ExternalInput").ap()
    xT = dt_in("xT", [16, 128, TOK], F32)
    o_m = dt_in("oT_mla", [6, 128, TOK], BF16)
    o_d = dt_in("oT_dil", [6, 128, TOK], F32)
    den = dt_in("denT", [6, TOK], F32)
    o_n = dt_in("oT_nsa", [4, 128, TOK], BF16)
    w_mo = dt_in("w_mo", [16, 128, 16 * 128], BF16)
    sel2_d = dt_in("sel2", [6, 256], F32)
    xo = nc.dram_tensor("xo", [16, 128, TOK], F32, kind="ExternalOutput").ap()
    es = contextlib.ExitStack()
    sc = Sched(nc, es)
    sb = lambda name, shape, dt: es.enter_context(nc.sbuf_tensor(name, shape, dt))
    pr = PsumRing(nc, es)
    sel2 = sb("sel2_s", [6, 256], F32)
    sel2_b = Buf()
    sc.dma("sp", sel2[:, :], sel2_d, sel2_b, writes=[sel2_b])
    ob = sb("ob", [128, 16, PT], BF16)
    ob_b = bufs(16)
    od = sb("od", [128, 6, PT], F32)
    od_b = bufs(6)
    dn = sb("dn", [6, PT], F32)
    dn_b = Buf()
    rec = sb("rec", [128, 2, PT], F32)
    rec_b = bufs(2)
    NW = 3
    wch = [sb("wch%d" % i, [128, 16 * 128], BF16) for i in range(NW)]
    wch_b = bufs(NW)
    NX = 3
    xc = [sb("xc%d" % i, [128, PT], F32) for i in range(NX)]
    xc_b = bufs(NX)
    wi = 0
    xi = 0
    for t in range(TOK // PT):
        tsl = slice(t * PT, (t + 1) * PT)
        for h in range(6):
            sc.dma("sp", ob[:, h, :], o_m[h, :, tsl], ob_b[h], writes=[ob_b[h]])
            sc.dma("sp", od[:, h, :], o_d[h, :, tsl], od_b[h], writes=[od_b[h]])
        for h in range(4):
            sc.dma("sp", ob[:, 12 + h, :], o_n[h, :, tsl], ob_b[12 + h], writes=[ob_b[12 + h]])
        sc.dma("sp", dn[:, :], den[:, tsl], dn_b, writes=[dn_b])
        for s in range(2):
            ps, ps_b = pr.next()
            sc.op("pe", lambda E: E.matmul(ps[:, :], lhsT=sel2[:, s * 128:(s + 1) * 128], rhs=dn[:, :],
                                           start=True, stop=True), reads=[sel2_b, dn_b], writes=[ps_b])
            sc.op("dve", lambda E: E.reciprocal(out=rec[:, s, :], in_=ps[:, :]), reads=[ps_b], writes=[rec_b[s]])
        for hd in range(6):
            sc.op("dve", lambda E: E.tensor_tensor(out=ob[:, 6 + hd, :], in0=od[:, hd, :], in1=rec[:, hd % 2, :],
                                                   op=ALU.mult),
                  reads=[od_b[hd], rec_b[hd % 2]], writes=[ob_b[6 + hd]])
        pend = []

        def load_w(j):
            nonlocal wi
            k = wi % NW
            wi += 1
            sc.dma("sp", wch[k][:, :], w_mo[j], wch_b[k], writes=[wch_b[k]])
            return k
        pend.append(load_w(0))
        for dmc in range(16):
            if dmc + 1 < 16:
                pend.append(load_w(dmc + 1))
            k = pend.pop(0)
            x = xi % NX
            xi += 1
            sc.dma("sp", xc[x][:, :], xT[dmc, :, tsl], xc_b[x], writes=[xc_b[x]])
            ps, ps_b = pr.next()
            for fc in range(16):
                sc.op("pe", lambda E: E.matmul(ps[:, :], lhsT=wch[k][:, fc * 128:(fc + 1) * 128], rhs=ob[:, fc, :],
                                               start=(fc == 0), stop=(fc == 15)),
                      reads=[wch_b[k], ob_b[fc]], writes=[ps_b])
            sc.op("dve", lambda E: E.tensor_tensor(out=xc[x][:, :], in0=ps[:, :], in1=xc[x][:, :], op=ALU.add),
                  reads=[ps_b, xc_b[x]], writes=[xc_b[x]])
            sc.dma("sp", xo[dmc, :, tsl], xc[x][:, :], xc_b[x], reads=[xc_b[x]], is_output=True)
    sc.finish()
    es.close()
    return nc


def lay_w_mo(w):
    a = w.reshape(16, 128, 16, 128)
    return np.ascontiguousarray(a.transpose(2, 1, 0, 3)).reshape(16, 128, 16 * 128)


def sel2_table():
    s = np.zeros((6, 256), np.float32)
    for gs in range(6):
        s[gs, (gs % 2) * 128:(gs % 2 + 1) * 128] = 1.0
    return s


_NC_CACHE = {}
CONV_KEYS = ("ffn1_w_in", "ffn1_w_out", "w_mix_in", "mla_w_uq", "mla_w_ukv", "nsa_cmp_pos", "nsa_phi_k1",
             "nsa_phi_k2", "nsa_phi_v1", "nsa_phi_v2", "w_mix_out", "ffn2_w_in", "ffn2_w_out")
CONV_NC = 21 * CONV_CH


def _launch(key, builder, in_maps):
    if key not in _NC_CACHE:
        _NC_CACHE[key] = builder()
    res = run_bass_kernel_spmd(_NC_CACHE[key], in_maps, core_ids=list(range(NCORES)))
    return res.results


def _convert_layer(inp, l):
    flats = [np.ascontiguousarray(inp[k][l]).reshape(-1) for k in CONV_KEYS]
    n = sum(f.size for f in flats)
    buf = np.zeros(NCORES * 128 * CONV_NC, np.float32)
    o = 0
    for f in flats:
        buf[o:o + f.size] = f
        o += f.size
    buf = buf.reshape(NCORES, 128, CONV_NC)
    res = _launch("conv", lambda: build_conv(CONV_NC), [{"src": buf[c]} for c in range(NCORES)])
    out = np.stack([np.asarray(res[c]["dst"]) for c in range(NCORES)]).reshape(-1)
    w = {}
    o = 0
    for k in CONV_KEYS:
        shp = inp[k][l].shape
        sz = int(np.prod(shp))
        w[k] = out[o:o + sz].reshape(shp)
        o += sz
    return w


def _nat_fm(L):
    sh = L[0].shape
    return np.stack([a.reshape(sh[:-1] + (8, 512)) for a in L], axis=-2).reshape(sh[:-1] + (S,))


def _nat_tm(L):
    c = L[0].shape[1]
    return np.stack([a.reshape(8, 512, c) for a in L], axis=1).reshape(S, c)


def kernel(**inp):
    inp = {k: np.asarray(v) for k, v in inp.items()}
    x = inp["x"]
    pos = [core_positions(r) for r in range(4)]
    rope = [rope_tables(pos[r]) for r in range(4)]
    xs = []
    for c in range(NCORES):
        b, r = divmod(c, 4)
        xs.append(np.ascontiguousarray(x[b][pos[r]].T).reshape(16, 128, TOK))
    sel2 = sel2_table()
    tabs = [p2_tables(r) for r in range(4)]
    for l in range(DEPTH):
        w = _convert_layer(inp, l)
        wi, wo = lay_w_in(w["ffn1_w_in"]), lay_w_out(w["ffn1_w_out"])
        g = lay_gain(inp["ffn1_norm"][l])
        res = _launch("pf", lambda: build_pf(False),
                      [{"xT": xs[c], "g": g, "w_in": wi, "w_out": wo} for c in range(NCORES)])
        xs = [np.asarray(res[c]["xo"]) for c in range(NCORES)]
        del wi, wo
        pw = lay_pp_weights(w["w_mix_in"], w["mla_w_uq"], w["mla_w_ukv"])
        base = dict(pw)
        base.update(g=lay_gain(inp["mix_norm"][l]), gq=lay_gain(inp["mla_q_norm"][l]),
                    gkv=lay_gain(inp["mla_kv_norm"][l]))
        maps = []
        for c in range(NCORES):
            m = dict(base)
            m["xT"] = xs[c]
            m["cos2"], m["sgs"] = rope[c % 4]
            maps.append(m)
        R = _launch("pp", build_pp, maps)
        R = [{k: np.asarray(v) for k, v in R[c].items()} for c in range(NCORES)]
        wn = dict(w1k=lay_k(w["nsa_phi_k1"]), w1v=lay_k(w["nsa_phi_v1"]), w2k=w["nsa_phi_k2"], w2v=w["nsa_phi_v2"],
                  posT=np.ascontiguousarray(w["nsa_cmp_pos"].T))
        maps = []
        for b in range(B):
            C = [R[4 * b + r] for r in range(4)]
            nat = dict(wn)
            nat["kn"] = _nat_fm([c_["o_kn"] for c_ in C])
            nat["kpe"] = _nat_fm([c_["o_kpe"] for c_ in C])
            fm = _nat_fm([c_["o_fm"] for c_ in C])
            nat["dq"], nat["dk"] = fm[0:6], fm[6:12]
            nat["nkc"], nat["nvc"], nat["nks"], nat["nkw"] = fm[16], fm[17], fm[18], fm[19]
            nat["mv"] = _nat_tm([c_["o_mv"] for c_ in C]).reshape(S, 6, 129)
            nat["dv"] = _nat_tm([c_["o_dv"] for c_ in C]).reshape(S, 6, 129)
            nat["nvs"] = _nat_tm([c_["o_nvs"] for c_ in C])
            nat["nvw"] = _nat_tm([c_["o_nvw"] for c_ in C])
            for r in range(4):
                own = dict(qn=C[r]["o_qn"], qr=C[r]["o_qr"], nq=C[r]["o_fm"][12:16],
                           g=tok_major_tiles(C[r]["o_g"]))
                m = prep_p2_with_tables(r, nat, own, tabs[r])
                maps.append(m)
        del R
        A = _launch("p2", build_p2, maps)
        A = [{k: np.asarray(v) for k, v in A[c].items()} for c in range(NCORES)]
        del maps
        wmo = lay_w_mo(w["w_mix_out"])
        maps = []
        for b in range(B):
            dnat = np.zeros((6, S, 129), np.float32)
            for r in range(4):
                for hd in range(6):
                    dnat[hd][dil_index(hd // 2, r)] = A[4 * b + r]["o_dil"][hd]
            for r in range(4):
                c = 4 * b + r
                own = dnat[:, pos[r], :]
                maps.append({
                    "xT": xs[c],
                    "oT_mla": np.ascontiguousarray(A[c]["o_mla"].T).reshape(6, 128, TOK),
                    "oT_nsa": np.ascontiguousarray(A[c]["o_nsa"].T).reshape(4, 128, TOK),
                    "oT_dil": np.ascontiguousarray(own[:, :, :128].transpose(0, 2, 1)),
                    "denT": np.ascontiguousarray(own[:, :, 128]),
                    "w_mo": wmo, "sel2": sel2})
        res = _launch("po", build_po, maps)
        xs = [np.asarray(res[c]["xo"]) for c in range(NCORES)]
        del A, maps
        wi, wo = lay_w_in(w["ffn2_w_in"]), lay_w_out(w["ffn2_w_out"])
        g = lay_gain(inp["ffn2_norm"][l])
        if l < DEPTH - 1:
            res = _launch("pf", lambda: build_pf(False),
                          [{"xT": xs[c], "g": g, "w_in": wi, "w_out": wo} for c in range(NCORES)])
        else:
            gf = lay_gain(inp["final_norm"])
            res = _launch("pff", lambda: build_pf(True),
                          [{"xT": xs[c], "g": g, "gf": gf, "w_in": wi, "w_out": wo} for c in range(NCORES)])
        xs = [np.asarray(res[c]["xo"]) for c in range(NCORES)]
        del wi, wo, w
    out = np.empty((B, S, D), np.float32)
    for c in range(NCORES):
        b, r = divmod(c, 4)
        out[b][pos[r]] = xs[c].reshape(D, TOK).T
    return out


def prep_p2_with_tables(r, nat, own, tab):
    m = prep_p2(r, nat, own, tab)
    return m
```

```python
import contextlib
import numpy as np
import ml_dtypes
import concourse.bass as bass
import concourse.mybir as mybir
from concourse.bass_utils import run_bass_kernel_spmd

F32 = mybir.dt.float32
BF16 = mybir.dt.bfloat16
AF = mybir.ActivationFunctionType
ALU = mybir.AluOpType
NPBF = ml_dtypes.bfloat16

NCORES = 8
D = 2048
DFF = 5504
NFF = 43
DEPTH = 4
B = 2
S = 16384
EPS = 1e-6


class Sem:
    __slots__ = ("h", "cnt")

    def __init__(self, h):
        self.h = h
        self.cnt = 0


class Buf:
    __slots__ = ("w", "r", "ds")

    def __init__(self):
        self.w = None
        self.r = {}
        self.ds = None


def bufs(n):
    return [Buf() for _ in range(n)]


class Sched:
    def __init__(self, nc, es):
        self.nc = nc
        self.es = es
        self.E = {"pe": nc.tensor, "act": nc.scalar, "dve": nc.vector,
                  "pool": nc.gpsimd, "sp": nc.sync}
        self.sem = {k: Sem(es.enter_context(nc.semaphore("s_" + k)))
                    for k in ("pe", "act", "dve", "pool")}
        self.seen = {k: {} for k in self.E}
        self.nds = 0
        self.out_events = []

    def _waits(self, eng, reads, writes):
        need = {}
        for b in reads:
            if b.w is not None:
                s, v = b.w
                if need.get(s, 0) < v:
                    need[s] = v
        for b in writes:
            if b.w is not None:
                s, v = b.w
                if need.get(s, 0) < v:
                    need[s] = v
            for s, v in b.r.items():
                if need.get(s, 0) < v:
                    need[s] = v
        seen = self.seen[eng]
        own = self.sem.get(eng)
        for s, v in need.items():
            if eng == "pe" and s is own:
                continue
            if seen.get(s, 0) < v:
                self.E[eng].wait_ge(s.h, v)
                seen[s] = v

    def _commit(self, ev, reads, writes):
        s, v = ev
        for b in reads:
            if b.r.get(s, 0) < v:
                b.r[s] = v
        for b in writes:
            b.w = ev
            b.r = {}

    def op(self, eng, fn, reads=(), writes=()):
        self._waits(eng, reads, writes)
        inst = fn(self.E[eng])
        s = self.sem[eng]
        s.cnt += 1
        inst.then_inc(s.h, 1)
        self._commit((s, s.cnt), reads, writes)

    def dma(self, q, out, in_, sb, reads=(), writes=(), is_output=False, **kw):
        self._waits(q, reads, writes)
        if sb.ds is None:
            sb.ds = Sem(self.es.enter_context(self.nc.semaphore("d%d" % self.nds)))
            self.nds += 1
        inst = self.E[q].dma_start(out=out, in_=in_, **kw)
        sb.ds.cnt += 16
        inst.then_inc(sb.ds.h, 16)
        ev = (sb.ds, sb.ds.cnt)
        self._commit(ev, reads, writes)
        if is_output:
            self.out_events.append(ev)

    def finish(self):
        need = {}
        for s, v in self.out_events:
            if need.get(s, 0) < v:
                need[s] = v
        for s, v in need.items():
            self.E["sp"].wait_ge(s.h, v)


def new_nc():
    return bass.Bass("TRN2", target_bir_lowering=False)


CONV_CH = 4096


def build_conv(ncols):
    nc = new_nc()
    src = nc.dram_tensor("src", [128, ncols], F32, kind="ExternalInput").ap()
    dst = nc.dram_tensor("dst", [128, ncols], BF16, kind="ExternalOutput").ap()
    es = contextlib.ExitStack()
    sc = Sched(nc, es)
    NB = 3
    st = [es.enter_context(nc.sbuf_tensor("cs%d" % i, [128, CONV_CH], F32)) for i in range(NB)]
    ot = [es.enter_context(nc.sbuf_tensor("co%d" % i, [128, CONV_CH], BF16)) for i in range(NB)]
    sb, ob = bufs(NB), bufs(NB)
    nch = ncols // CONV_CH
    engs = ["dve", "act", "pool"]
    for i in range(nch):
        k = i % NB
        sl = slice(i * CONV_CH, (i + 1) * CONV_CH)
        sc.dma("sp", st[k][:], src[:, sl], sb[k], writes=[sb[k]])
        e = engs[i % 3]
        if e == "act":
            sc.op("act", lambda E: E.activation(out=ot[k][:], in_=st[k][:], func=AF.Copy),
                  reads=[sb[k]], writes=[ob[k]])
        else:
            sc.op(e, lambda E: E.tensor_copy(out=ot[k][:], in_=st[k][:]),
                  reads=[sb[k]], writes=[ob[k]])
        sc.dma("sp", dst[:, sl], ot[k][:], ob[k], reads=[ob[k]], is_output=True)
    sc.finish()
    es.close()
    return nc


TOK = 4096
TT = 1024
FF_HALVES = ((0, 22), (22, 21))


def rms_norm_fm(sc, nc, xs, xsb, xn, xnb, gain, gain_b, ones, ones_b, ss_ps, ss_b,
                sq, sq_b, rs, rs_b, rstd, rstd_b, nchunk, dim, tt):
    nh = tt // 512
    for c in range(nchunk):
        k = c % 2
        sc.op("act", lambda E: E.activation(out=sq[k][:, :tt], in_=xs[c], func=AF.Square),
              reads=[xsb[c]], writes=[sq_b[k]])
        for h in range(nh):
            sc.op("pe", lambda E: E.matmul(ss_ps[h][:, :], lhsT=ones[:, :], rhs=sq[k][:, h * 512:(h + 1) * 512],
                                           start=(c == 0), stop=(c == nchunk - 1)),
                  reads=[sq_b[k], ones_b], writes=[ss_b[h]])
    for h in range(nh):
        sc.op("act", lambda E: E.activation(out=rs[:, h * 512:(h + 1) * 512], in_=ss_ps[h][:, :],
                                            func=AF.Sqrt, scale=1.0 / dim, bias=EPSB[0][:, 0:1]),
              reads=[ss_b[h], EPSB[1]], writes=[rs_b])
    sc.op("dve", lambda E: E.reciprocal(out=rstd[:, :tt], in_=rs[:, :tt]), reads=[rs_b], writes=[rstd_b])
    for c in range(nchunk):
        sc.op("dve", lambda E: E.scalar_tensor_tensor(out=xn[c], in0=xs[c], scalar=gain[:, c:c + 1],
                                                      in1=rstd[:, :tt], op0=ALU.mult, op1=ALU.mult),
              reads=[xsb[c], gain_b, rstd_b], writes=[xnb[c]])


EPSB = [None, None]


def make_consts(sc, nc, es):
    ones = es.enter_context(nc.sbuf_tensor("ones_f", [128, 128], F32))
    ones_b = Buf()
    sc.op("dve", lambda E: E.memset(ones[:, :], 1.0), writes=[ones_b])
    epst = es.enter_context(nc.sbuf_tensor("eps_c", [128, 1], F32))
    eps_b = Buf()
    sc.op("dve", lambda E: E.memset(epst[:, :], EPS), writes=[eps_b])
    EPSB[0] = epst
    EPSB[1] = eps_b
    return ones, ones_b


def build_pf(final_norm=False):
    nc = new_nc()
    xT = nc.dram_tensor("xT", [16, 128, TOK], F32, kind="ExternalInput").ap()
    g = nc.dram_tensor("g", [128, 16], F32, kind="ExternalInput").ap()
    w_in = nc.dram_tensor("w_in", [NFF, 128, 16 * 256], BF16, kind="ExternalInput").ap()
    w_out = nc.dram_tensor("w_out", [16, 128, NFF * 128], BF16, kind="ExternalInput").ap()
    xo = nc.dram_tensor("xo", [16, 128, TOK], F32, kind="ExternalOutput").ap()
    if final_norm:
        gf = nc.dram_tensor("gf", [128, 16], F32, kind="ExternalInput").ap()
    es = contextlib.ExitStack()
    sc = Sched(nc, es)
    sb = lambda name, shape, dt: es.enter_context(nc.sbuf_tensor(name, shape, dt))
    ones, ones_b = make_consts(sc, nc, es)
    xs = sb("xs", [128, 16, TT], F32)
    xs_b = bufs(16)
    xn = sb("xn", [128, 16, TT], BF16)
    xn_b = bufs(16)
    act = sb("act", [128, 22, TT], BF16)
    act_b = bufs(22)
    NW = 3
    win = [sb("win%d" % i, [128, 16 * 256], BF16) for i in range(NW)]
    win_b = bufs(NW)
    NWO = 2
    wout = [sb("wout%d" % i, [128, 22 * 128], BF16) for i in range(NWO)]
    wout_b = bufs(NWO)
    sq = [sb("sq%d" % i, [128, TT], F32) for i in range(2)]
    sq_b = bufs(2)
    rs = sb("rs", [128, TT], F32)
    rs_b = Buf()
    rstd = sb("rstd", [128, TT], F32)
    rstd_b = Buf()
    sg = [sb("sg%d" % i, [128, 512], F32) for i in range(2)]
    sg_b = bufs(2)
    gt = sb("gt", [128, 16], F32)
    gt_b = Buf()
    sc.dma("sp", gt[:, :], g, gt_b, writes=[gt_b])
    if final_norm:
        gft = sb("gft", [128, 16], F32)
        gft_b = Buf()
        sc.dma("sp", gft[:, :], gf, gft_b, writes=[gft_b])
        xo_t = [sb("xot%d" % i, [128, TT], F32) for i in range(2)]
        xo_b = bufs(2)
    ps = [es.enter_context(nc.psum_tensor("ps%d" % i, [128, 512], F32)) for i in range(8)]
    ps_b = bufs(8)

    win_i = 0
    wout_i = 0
    for t in range(TOK // TT):
        tsl = slice(t * TT, (t + 1) * TT)
        for c in range(16):
            sc.dma("sp", xs[:, c, :], xT[c, :, tsl], xs_b[c], writes=[xs_b[c]])
        rms_norm_fm(sc, nc, [xs[:, c, :] for c in range(16)], xs_b, [xn[:, c, :] for c in range(16)], xn_b,
                    gt, gt_b, ones, ones_b, [ps[0], ps[1]], [ps_b[0], ps_b[1]], sq, sq_b, rs, rs_b,
                    rstd, rstd_b, 16, D, TT)
        gu = 0
        for (f0, nf) in FF_HALVES:
            pend = []

            def load_win(f):
                nonlocal win_i
                k = win_i % NW
                win_i += 1
                sc.dma("sp", win[k][:, :], w_in[f], win_b[k], writes=[win_b[k]])
                return k
            pend.append(load_win(f0))
            for fi in range(nf):
                if fi + 1 < nf:
                    pend.append(load_win(f0 + fi + 1))
                k = pend.pop(0)
                for h in range(2):
                    hs = slice(h * 512, (h + 1) * 512)
                    gb = (gu % 2) * 2
                    gu += 1
                    G, U = ps[gb], ps[gb + 1]
                    for kc in range(16):
                        sc.op("pe", lambda E: E.matmul(G[:, :], lhsT=win[k][:, kc * 256:kc * 256 + 128],
                                                       rhs=xn[:, kc, hs], start=(kc == 0), stop=(kc == 15)),
                              reads=[win_b[k], xn_b[kc]], writes=[ps_b[gb]])
                    for kc in range(16):
                        sc.op("pe", lambda E: E.matmul(U[:, :], lhsT=win[k][:, kc * 256 + 128:kc * 256 + 256],
                                                       rhs=xn[:, kc, hs], start=(kc == 0), stop=(kc == 15)),
                              reads=[win_b[k], xn_b[kc]], writes=[ps_b[gb + 1]])
                    j = gu % 2
                    sc.op("act", lambda E: E.activation(out=sg[j][:, :], in_=G[:, :], func=AF.Silu),
                          reads=[ps_b[gb]], writes=[sg_b[j]])
                    sc.op("dve", lambda E: E.tensor_tensor(out=act[:, fi, hs], in0=sg[j][:, :], in1=U[:, :],
                                                           op=ALU.mult),
                          reads=[sg_b[j], ps_b[gb + 1]], writes=[act_b[fi]])
            pend = []

            def load_wout(dmc):
                nonlocal wout_i
                k = wout_i % NWO
                wout_i += 1
                sc.dma("sp", wout[k][:, :nf * 128], w_out[dmc, :, f0 * 128:(f0 + nf) * 128], wout_b[k],
                       writes=[wout_b[k]])
                return k
            pend.append(load_wout(0))
            for dmc in range(16):
                if dmc + 1 < 16:
                    pend.append(load_wout(dmc + 1))
                k = pend.pop(0)
                for h in range(2):
                    hs = slice(h * 512, (h + 1) * 512)
                    yb = 4 + (dmc % 2) * 2 + h
                    Y = ps[yb]
                    for fi in range(nf):
                        sc.op("pe", lambda E: E.matmul(Y[:, :], lhsT=wout[k][:, fi * 128:(fi + 1) * 128],
                                                       rhs=act[:, fi, hs], start=(fi == 0), stop=(fi == nf - 1)),
                              reads=[wout_b[k], act_b[fi]], writes=[ps_b[yb]])
                    sc.op("dve", lambda E: E.scalar_tensor_tensor(out=xs[:, dmc, hs], in0=Y[:, :], scalar=0.5,
                                                                  in1=xs[:, dmc, hs], op0=ALU.mult, op1=ALU.add),
                          reads=[ps_b[yb], xs_b[dmc]], writes=[xs_b[dmc]])
        if not final_norm:
            for c in range(16):
                sc.dma("sp", xo[c, :, tsl], xs[:, c, :], xs_b[c], reads=[xs_b[c]], is_output=True)
        else:
            nh = TT // 512
            for c in range(16):
                k = c % 2
                sc.op("act", lambda E: E.activation(out=sq[k][:, :], in_=xs[:, c, :], func=AF.Square),
                      reads=[xs_b[c]], writes=[sq_b[k]])
                for h in range(nh):
                    sc.op("pe", lambda E: E.matmul(ps[h][:, :], lhsT=ones[:, :], rhs=sq[k][:, h * 512:(h + 1) * 512],
                                                   start=(c == 0), stop=(c == 15)),
                          reads=[sq_b[k], ones_b], writes=[ps_b[h]])
            for h in range(nh):
                sc.op("act", lambda E: E.activation(out=rs[:, h * 512:(h + 1) * 512], in_=ps[h][:, :],
                                                    func=AF.Sqrt, scale=1.0 / D, bias=EPSB[0][:, 0:1]),
                      reads=[ps_b[h], EPSB[1]], writes=[rs_b])
            sc.op("dve", lambda E: E.reciprocal(out=rstd[:, :], in_=rs[:, :]), reads=[rs_b], writes=[rstd_b])
            for c in range(16):
                k = c % 2
                sc.op("dve", lambda E: E.scalar_tensor_tensor(out=xo_t[k][:, :], in0=xs[:, c, :],
                                                              scalar=gft[:, c:c + 1], in1=rstd[:, :],
                                                              op0=ALU.mult, op1=ALU.mult),
                      reads=[xs_b[c], gft_b, rstd_b], writes=[xo_b[k]])
                sc.dma("sp", xo[c, :, tsl], xo_t[k][:, :], xo_b[k], reads=[xo_b[k]], is_output=True)
    sc.finish()
    es.close()
    return nc


PT = 512
N_FM = 28
N_FM_OUT = 20
TM_COLS = 1040


class PsumRing:
    def __init__(self, nc, es, n=8, base=0):
        self.t = [es.enter_context(nc.psum_tensor("pr%d" % (i + base), [128, 512], F32)) for i in range(n)]
        self.b = bufs(n)
        self.i = 0
        self.n = n

    def next(self):
        k = self.i % self.n
        self.i += 1
        return self.t[k], self.b[k]


def build_pp():
    nc = new_nc()
    dt_in = lambda name, shape, dt: nc.dram_tensor(name, shape, dt, kind="ExternalInput").ap()
    dt_out = lambda name, shape, dt: nc.dram_tensor(name, shape, dt, kind="ExternalOutput").ap()
    xT = dt_in("xT", [16, 128, TOK], F32)
    g = dt_in("g", [128, 16], F32)
    gq = dt_in("gq", [128, 4], F32)
    gkv = dt_in("gkv", [128, 4], F32)
    wfm = dt_in("wfm", [N_FM, 128, 16 * 128], BF16)
    wkr = dt_in("wkr", [128, 16 * 128], BF16)
    wtm = dt_in("wtm", [128, 16 * TM_COLS], BF16)
    wqn = dt_in("wqn", [128, 4 * 768], BF16)
    wqr = dt_in("wqr", [128, 4 * 384], BF16)
    wqs = dt_in("wqs", [128, 4 * 384], BF16)
    wkn = dt_in("wkn", [128, 4 * 768], BF16)
    wv = dt_in("wv", [128, 4 * 768], BF16)
    cos2 = dt_in("cos2", [64, TOK], F32)
    sgs = dt_in("sgs", [64, TOK], F32)
    o_qn = dt_out("o_qn", [6, 128, TOK], BF16)
    o_qr = dt_out("o_qr", [6, 64, TOK], BF16)
    o_kn = dt_out("o_kn", [6, 128, TOK], BF16)
    o_kpe = dt_out("o_kpe", [64, TOK], BF16)
    o_fm = dt_out("o_fm", [N_FM_OUT, 128, TOK], BF16)
    o_mv = dt_out("o_mv", [TOK, 6 * 129], BF16)
    o_dv = dt_out("o_dv", [TOK, 6 * 129], BF16)
    o_nvs = dt_out("o_nvs", [TOK, 129], BF16)
    o_nvw = dt_out("o_nvw", [TOK, 129], BF16)
    o_g = dt_out("o_g", [TOK, 12], F32)

    es = contextlib.ExitStack()
    sc = Sched(nc, es)
    sb = lambda name, shape, dt: es.enter_context(nc.sbuf_tensor(name, shape, dt))
    ones, ones_b = make_consts(sc, nc, es)
    pr = PsumRing(nc, es)

    def const_load(name, src, shape, dt):
        t = sb(name + "_s", shape, dt)
        b = Buf()
        sc.dma("sp", t[:, :], src, b, writes=[b])
        return t, b
    gt, gt_b = const_load("gt", g, [128, 16], F32)
    gqt, gq_b = const_load("gqt", gq, [128, 4], F32)
    gkvt, gkv_b = const_load("gkvt", gkv, [128, 4], F32)
    wtm_t, wtm_b = const_load("wtm_t", wtm, [128, 16 * TM_COLS], BF16)
    wkr_t, wkr_b = const_load("wkr_t", wkr, [128, 16 * 128], BF16)
    wqn_t, wqn_b = const_load("wqn_t", wqn, [128, 4 * 768], BF16)
    wqr_t, wqr_b = const_load("wqr_t", wqr, [128, 4 * 384], BF16)
    wqs_t, wqs_b = const_load("wqs_t", wqs, [128, 4 * 384], BF16)
    wkn_t, wkn_b = const_load("wkn_t", wkn, [128, 4 * 768], BF16)
    wv_t, wv_b = const_load("wv_t", wv, [128, 4 * 768], BF16)

    xs = sb("xs", [128, 16, PT], F32)
    xs_b = bufs(16)
    xn = sb("xn", [128, 16, PT], BF16)
    xn_b = bufs(16)
    sq = [sb("sq%d" % i, [128, PT], F32) for i in range(2)]
    sq_b = bufs(2)
    rs = sb("rs", [128, PT], F32)
    rs_b = Buf()
    rstd = sb("rstd", [128, PT], F32)
    rstd_b = Buf()
    NW = 3
    wch = [sb("wch%d" % i, [128, 16 * 128], BF16) for i in range(NW)]
    wch_b = bufs(NW)
    lat = sb("lat", [128, 8, PT], F32)
    lat_b = bufs(8)
    latn = sb("latn", [128, 8, PT], BF16)
    latn_b = bufs(8)
    NST = 4
    stg = [sb("stg%d" % i, [128, PT], BF16) for i in range(NST)]
    stg_b = bufs(NST)
    stg_i = [0]
    cs_t = sb("cs_t", [64, PT], F32)
    cs_b = Buf()
    sn_t = sb("sn_t", [64, PT], F32)
    sn_b = Buf()
    r1 = sb("r1", [64, PT], F32)
    r1_b = Buf()
    r2 = sb("r2", [64, PT], F32)
    r2_b = Buf()
    vst = [sb("vst%d" % i, [128, 6, 129], BF16) for i in range(2)]
    vst_b = bufs(2)
    nst = [sb("nst%d" % i, [128, 2, 129], BF16) for i in range(2)]
    nst_b = bufs(2)
    gst = [sb("gst%d" % i, [128, 12], F32) for i in range(2)]
    gst_b = bufs(2)
    for i in range(2):
        sc.op("dve", lambda E: E.memset(vst[i][:, :, :], 1.0), writes=[vst_b[i]])
        sc.op("dve", lambda E: E.memset(nst[i][:, :, :], 1.0), writes=[nst_b[i]])

    def stage_out(ps, ps_b, dst, npart=128, eng="act"):
        k = stg_i[0] % NST
        stg_i[0] += 1
        if eng == "act":
            sc.op("act", lambda E: E.activation(out=stg[k][:npart, :], in_=ps[:npart, :], func=AF.Copy),
                  reads=[ps_b], writes=[stg_b[k]])
        else:
            sc.op("dve", lambda E: E.tensor_copy(out=stg[k][:npart, :], in_=ps[:npart, :]),
                  reads=[ps_b], writes=[stg_b[k]])
        sc.dma("sp", dst, stg[k][:npart, :], stg_b[k], reads=[stg_b[k]], is_output=True)

    def rope_out(psA, psA_b, psB, psB_b, dst):
        sc.op("dve", lambda E: E.tensor_tensor(out=r1[:, :], in0=psA[:64, :], in1=cs_t[:, :], op=ALU.mult),
              reads=[psA_b, cs_b], writes=[r1_b])
        sc.op("dve", lambda E: E.tensor_tensor(out=r2[:, :], in0=psB[:64, :], in1=sn_t[:, :], op=ALU.mult),
              reads=[psB_b, sn_b], writes=[r2_b])
        k = stg_i[0] % NST
        stg_i[0] += 1
        sc.op("dve", lambda E: E.tensor_tensor(out=stg[k][:64, :], in0=r1[:, :], in1=r2[:, :], op=ALU.add),
              reads=[r1_b, r2_b], writes=[stg_b[k]])
        sc.dma("sp", dst, stg[k][:64, :], stg_b[k], reads=[stg_b[k]], is_output=True)

    wi = 0
    vi = 0
    for t in range(TOK // PT):
        tsl = slice(t * PT, (t + 1) * PT)
        for c in range(16):
            sc.dma("sp", xs[:, c, :], xT[c, :, tsl], xs_b[c], writes=[xs_b[c]])
        sc.dma("sp", cs_t[:, :], cos2[:, tsl], cs_b, writes=[cs_b])
        sc.dma("sp", sn_t[:, :], sgs[:, tsl], sn_b, writes=[sn_b])
        ps0, ps0_b = pr.next()
        rms_norm_fm(sc, nc, [xs[:, c, :] for c in range(16)], xs_b, [xn[:, c, :] for c in range(16)], xn_b,
                    gt, gt_b, ones, ones_b, [ps0], [ps0_b], sq, sq_b, rs, rs_b, rstd, rstd_b, 16, D, PT)
        pend = []

        def load_w(j):
            nonlocal wi
            k = wi % NW
            wi += 1
            sc.dma("sp", wch[k][:, :], wfm[j], wch_b[k], writes=[wch_b[k]])
            return k
        pend.append(load_w(0))
        for j in range(N_FM):
            if j + 1 < N_FM:
                pend.append(load_w(j + 1))
            k = pend.pop(0)
            ps, ps_b = pr.next()
            for kc in range(16):
                sc.op("pe", lambda E: E.matmul(ps[:, :], lhsT=wch[k][:, kc * 128:(kc + 1) * 128], rhs=xn[:, kc, :],
                                               start=(kc == 0), stop=(kc == 15)),
                      reads=[wch_b[k], xn_b[kc]], writes=[ps_b])
            if j < 8:
                sc.op("act", lambda E: E.activation(out=lat[:, j, :], in_=ps[:, :], func=AF.Copy),
                      reads=[ps_b], writes=[lat_b[j]])
            else:
                stage_out(ps, ps_b, o_fm[j - 8, :, tsl], eng=("act" if j % 2 else "dve"))
        psA, psA_b = pr.next()
        psB, psB_b = pr.next()
        for kc in range(16):
            sc.op("pe", lambda E: E.matmul(psA[:64, :], lhsT=wkr_t[:, kc * 128:kc * 128 + 64], rhs=xn[:, kc, :],
                                           start=(kc == 0), stop=(kc == 15)),
                  reads=[wkr_b, xn_b[kc]], writes=[psA_b])
        for kc in range(16):
            sc.op("pe", lambda E: E.matmul(psB[:64, :], lhsT=wkr_t[:, kc * 128 + 64:kc * 128 + 128], rhs=xn[:, kc, :],
                                           start=(kc == 0), stop=(kc == 15)),
                  reads=[wkr_b, xn_b[kc]], writes=[psB_b])
        rope_out(psA, psA_b, psB, psB_b, o_kpe[:, tsl])
        for sub in range(PT // 128):
            ssl = slice(sub * 128, (sub + 1) * 128)
            rows = slice(t * PT + sub * 128, t * PT + (sub + 1) * 128)
            k2 = vi % 2
            vi += 1
            pa, pa_b = pr.next()
            pb, pb_b = pr.next()
            pc, pc_b = pr.next()
            for (pp_, pp_b, c0, ncol) in ((pa, pa_b, 0, 512), (pb, pb_b, 512, 512), (pc, pc_b, 1024, 12)):
                for kc in range(16):
                    sc.op("pe", lambda E: E.matmul(pp_[:, :ncol], lhsT=xn[:, kc, ssl],
                                                   rhs=wtm_t[:, kc * TM_COLS + c0:kc * TM_COLS + c0 + ncol],
                                                   start=(kc == 0), stop=(kc == 15)),
                          reads=[wtm_b, xn_b[kc]], writes=[pp_b])
            sc.op("act", lambda E: E.activation(out=vst[k2][:, 0:4, 0:128],
                                                in_=pa[:, :].rearrange("p (h d) -> p h d", h=4), func=AF.Copy),
                  reads=[pa_b], writes=[vst_b[k2]])
            sc.op("dve", lambda E: E.tensor_copy(out=vst[k2][:, 4:6, 0:128],
                                                 in_=pb[:, 0:256].rearrange("p (h d) -> p h d", h=2)),
                  reads=[pb_b], writes=[vst_b[k2]])
            sc.op("dve", lambda E: E.tensor_copy(out=nst[k2][:, :, 0:128],
                                                 in_=pb[:, 256:512].rearrange("p (h d) -> p h d", h=2)),
                  reads=[pb_b], writes=[nst_b[k2]])
            sc.op("act", lambda E: E.activation(out=gst[k2][:, :], in_=pc[:, 0:12], func=AF.Sigmoid),
                  reads=[pc_b], writes=[gst_b[k2]])
            sc.dma("sp", o_dv[rows, :], vst[k2][:, :, :].rearrange("p h d -> p (h d)"), vst_b[k2],
                   reads=[vst_b[k2]], is_output=True)
            sc.dma("sp", o_nvs[rows, :], nst[k2][:, 0, :], nst_b[k2], reads=[nst_b[k2]], is_output=True)
            sc.dma("sp", o_nvw[rows, :], nst[k2][:, 1, :], nst_b[k2], reads=[nst_b[k2]], is_output=True)
            sc.dma("sp", o_g[rows, :], gst[k2][:, :], gst_b[k2], reads=[gst_b[k2]], is_output=True)
        ps1, ps1_b = pr.next()
        rms_norm_fm(sc, nc, [lat[:, c, :] for c in range(4)], lat_b[0:4], [latn[:, c, :] for c in range(4)],
                    latn_b[0:4], gqt, gq_b, ones, ones_b, [ps1], [ps1_b], sq, sq_b, rs, rs_b, rstd, rstd_b,
                    4, 512, PT)
        ps2, ps2_b = pr.next()
        rms_norm_fm(sc, nc, [lat[:, 4 + c, :] for c in range(4)], lat_b[4:8],
                    [latn[:, 4 + c, :] for c in range(4)], latn_b[4:8], gkvt, gkv_b, ones, ones_b, [ps2], [ps2_b],
                    sq, sq_b, rs, rs_b, rstd, rstd_b, 4, 512, PT)
        for h in range(6):
            ps, ps_b = pr.next()
            for kc in range(4):
                sc.op("pe", lambda E: E.matmul(ps[:, :], lhsT=wqn_t[:, kc * 768 + h * 128:kc * 768 + (h + 1) * 128],
                                               rhs=latn[:, kc, :], start=(kc == 0), stop=(kc == 3)),
                      reads=[wqn_b, latn_b[kc]], writes=[ps_b])
            stage_out(ps, ps_b, o_qn[h, :, tsl], eng="act")
            psA, psA_b = pr.next()
            psB, psB_b = pr.next()
            for kc in range(4):
                sc.op("pe", lambda E: E.matmul(psA[:64, :], lhsT=wqr_t[:, kc * 384 + h * 64:kc * 384 + (h + 1) * 64],
                                               rhs=latn[:, kc, :], start=(kc == 0), stop=(kc == 3)),
                      reads=[wqr_b, latn_b[kc]], writes=[psA_b])
            for kc in range(4):
                sc.op("pe", lambda E: E.matmul(psB[:64, :], lhsT=wqs_t[:, kc * 384 + h * 64:kc * 384 + (h + 1) * 64],
                                               rhs=latn[:, kc, :], start=(kc == 0), stop=(kc == 3)),
                      reads=[wqs_b, latn_b[kc]], writes=[psB_b])
            rope_out(psA, psA_b, psB, psB_b, o_qr[h, :, tsl])
            ps, ps_b = pr.next()
            for kc in range(4):
                sc.op("pe", lambda E: E.matmul(ps[:, :], lhsT=wkn_t[:, kc * 768 + h * 128:kc * 768 + (h + 1) * 128],
                                               rhs=latn[:, 4 + kc, :], start=(kc == 0), stop=(kc == 3)),
                      reads=[wkn_b, latn_b[4 + kc]], writes=[ps_b])
            stage_out(ps, ps_b, o_kn[h, :, tsl], eng="dve")
        for sub in range(PT // 128):
            ssl = slice(sub * 128, (sub + 1) * 128)
            rows = slice(t * PT + sub * 128, t * PT + (sub + 1) * 128)
            k2 = vi % 2
            vi += 1
            pa, pa_b = pr.next()
            pb, pb_b = pr.next()
            for (pp_, pp_b, c0, ncol) in ((pa, pa_b, 0, 512), (pb, pb_b, 512, 256)):
                for kc in range(4):
                    sc.op("pe", lambda E: E.matmul(pp_[:, :ncol], lhsT=latn[:, 4 + kc, ssl],
                                                   rhs=wv_t[:, kc * 768 + c0:kc * 768 + c0 + ncol],
                                                   start=(kc == 0), stop=(kc == 3)),
                          reads=[wv_b, latn_b[4 + kc]], writes=[pp_b])
            sc.op("act", lambda E: E.activation(out=vst[k2][:, 0:4, 0:128],
                                                in_=pa[:, :].rearrange("p (h d) -> p h d", h=4), func=AF.Copy),
                  reads=[pa_b], writes=[vst_b[k2]])
            sc.op("dve", lambda E: E.tensor_copy(out=vst[k2][:, 4:6, 0:128],
                                                 in_=pb[:, 0:256].rearrange("p (h d) -> p h d", h=2)),
                  reads=[pb_b], writes=[vst_b[k2]])
            sc.dma("sp", o_mv[rows, :], vst[k2][:, :, :].rearrange("p h d -> p (h d)"), vst_b[k2],
                   reads=[vst_b[k2]], is_output=True)
    sc.finish()
    es.close()
    return nc


def lay_gain(g):
    n = g.shape[0] // 128
    return np.ascontiguousarray(g.reshape(n, 128).T)


def lay_k(w):
    kc = w.shape[0] // 128
    return np.ascontiguousarray(w.reshape(kc, 128, -1).transpose(1, 0, 2)).reshape(128, -1)


def lay_w_in(w):
    wg = w[:, :DFF].reshape(16, 128, NFF, 128)
    wu = w[:, DFF:].reshape(16, 128, NFF, 128)
    a = np.stack([wg, wu], axis=3)
    return np.ascontiguousarray(a.transpose(2, 1, 0, 3, 4)).reshape(NFF, 128, 16 * 256)


def lay_w_out(w):
    a = w.reshape(NFF, 128, 16, 128)
    return np.ascontiguousarray(a.transpose(2, 1, 0, 3)).reshape(16, 128, NFF * 128)


FM_STARTS = ([0, 128, 256, 384] + [512 + 128 * i for i in range(4)] + [1088 + 128 * i for i in range(6)]
             + [1856 + 128 * i for i in range(6)] + [3392 + 128 * i for i in range(4)] + [3904, 4032, 4160, 4416])
SWAP64 = np.array([(i + 32) % 64 for i in range(64)])


def lay_pp_weights(w_mix_in, w_uq, w_ukv):
    o = {}
    o["wfm"] = np.stack([lay_k(w_mix_in[:, c:c + 128]) for c in FM_STARTS])
    kr = w_mix_in[:, 1024:1088]
    o["wkr"] = lay_k(np.concatenate([kr, kr[:, SWAP64]], axis=1))
    pad = np.zeros((D, 4), dtype=w_mix_in.dtype)
    o["wtm"] = lay_k(np.concatenate([w_mix_in[:, 2624:3392], w_mix_in[:, 4288:4416], w_mix_in[:, 4544:4672],
                                     w_mix_in[:, 4672:4684], pad], axis=1))
    uq = w_uq.reshape(512, 6, 192)
    o["wqn"] = lay_k(uq[:, :, :128].reshape(512, 768))
    o["wqr"] = lay_k(uq[:, :, 128:].reshape(512, 384))
    o["wqs"] = lay_k(uq[:, :, 128:][:, :, SWAP64].reshape(512, 384))
    ukv = w_ukv.reshape(512, 6, 256)
    o["wkn"] = lay_k(ukv[:, :, :128].reshape(512, 768))
    o["wv"] = lay_k(ukv[:, :, 128:].reshape(512, 768))
    return o


def core_positions(r):
    i = np.arange(8)[:, None]
    u = np.arange(512)[None, :]
    return (512 * (4 * i + r) + u).reshape(-1)


def rope_tables(pos):
    inv = (np.float32(10000.0) ** (-np.arange(32, dtype=np.float32) / np.float32(32))).astype(np.float32)
    ang = pos.astype(np.float32)[None, :] * inv[:, None]
    c, s = np.cos(ang).astype(np.float32), np.sin(ang).astype(np.float32)
    return np.concatenate([c, c], 0), np.concatenate([-s, s], 0)


NEG = -30000.0
SLAB = 128 * 129
TABW = 2944
MLA_SCALE = 192.0 ** -0.5
HD_SCALE = 128.0 ** -0.5


def att_block(sc, steps, sbanks, pTs, accs, nv):
    n = len(steps)
    ns, npt = len(sbanks), len(pTs)

    def emit_S(k):
        st = steps[k]
        bank, bb = sbanks[k % ns]
        m = len(st["mm"])
        for idx, (l, r, rb) in enumerate(st["mm"]):
            sc.op("pe", lambda E: E.matmul(bank[:, :], lhsT=l, rhs=r, start=(idx == 0), stop=(idx == m - 1)),
                  reads=rb, writes=[bb])

    emit_S(0)
    if n > 1:
        emit_S(1)
    for k in range(n):
        if k + 2 < n:
            emit_S(k + 2)
        st = steps[k]
        bank, bb = sbanks[k % ns]
        pt, pb = pTs[k % npt]
        if st["bias"] is None:
            sc.op("act", lambda E: E.activation(out=pt[:, :], in_=bank[:, :], func=AF.Exp, scale=st["scale"]),
                  reads=[bb], writes=[pb])
        else:
            bap, bbuf = st["bias"]
            sc.op("act", lambda E: E.activation(out=pt[:, :], in_=bank[:, :], func=AF.Exp, scale=st["scale"],
                                                bias=bap),
                  reads=[bb, bbuf], writes=[pb])
        vap, vb = st["v"]
        for qs in range(4):
            acc, ab = accs[qs]
            sc.op("pe", lambda E: E.matmul(acc[:, :nv], lhsT=pt[:, qs * 128:(qs + 1) * 128], rhs=vap,
                                           start=(k == 0), stop=(k == n - 1)),
                  reads=[pb] + vb, writes=[ab])


def build_p2(do_mla=True, do_dil=True, do_nsa=True, nblk=8):
    nc = new_nc()
    dt_in = lambda name, shape, dt: nc.dram_tensor(name, shape, dt, kind="ExternalInput").ap()
    dt_out = lambda name, shape, dt: nc.dram_tensor(name, shape, dt, kind="ExternalOutput").ap()
    m_qn = dt_in("m_qn", [6, 128, TOK], BF16)
    m_qr = dt_in("m_qr", [6, 64, TOK], BF16)
    m_kn = dt_in("m_kn", [6, 128, S], BF16)
    m_kpe = dt_in("m_kpe", [64, S], BF16)
    m_v = dt_in("m_v", [6, 128, SLAB], BF16)
    tc_d = dt_in("tc", [128, TABW], BF16)
    tw_d = dt_in("tw", [128, TABW], BF16)
    ident_d = dt_in("ident", [128, 128], BF16)
    o_mla = dt_out("o_mla", [TOK, 768], BF16)
    d_q = dt_in("d_q", [6, 128, 32 * 128], BF16)
    d_k = dt_in("d_k", [6, 128, 33 * 128], BF16)
    d_v = dt_in("d_v", [6, 128, 33 * 129], BF16)
    d_bias = dt_in("d_bias", [128, 8 * 256], F32)
    o_dil = dt_out("o_dil", [6, 32 * 128, 129], F32)
    n_q = dt_in("n_q", [4, 128, TOK], BF16)
    n_g = dt_in("n_g", [128, 32 * 12], F32)
    n_kc = dt_in("n_kc", [128, S], BF16)
    n_vc = dt_in("n_vc", [128, S], BF16)
    n_ks = dt_in("n_ks", [128, S], BF16)
    n_vs = dt_in("n_vs", [128, SLAB], BF16)
    n_kw = dt_in("n_kw", [128, S], BF16)
    n_vw = dt_in("n_vw", [128, SLAB], BF16)
    w1k_d = dt_in("w1k", [128, 32 * 128], BF16)
    w1v_d = dt_in("w1v", [128, 32 * 128], BF16)
    w2k_d = dt_in("w2k", [128, 128], BF16)
    w2v_d = dt_in("w2v", [128, 128], BF16)
    posT_d = dt_in("posT", [128, 32], BF16)
    cmpsel_d = dt_in("cmpsel", [128, 8, 257], BF16)
    ewide_d = dt_in("ewide", [128, 8192], BF16)
    cmask_d = dt_in("cmask", [128, 512], BF16)
    cmaskp_d = dt_in("cmaskp", [128, 512], BF16)
    cap_d = dt_in("cap", [128, 512], F32)
    floor_d = dt_in("floor", [128, 512], F32)
    bcmp_d = dt_in("bcmp", [128, 32], F32)
    bslc_d = dt_in("bslc", [128, 512], F32)
    bwin_d = dt_in("bwin", [128, 80], F32)
    o_nsa = dt_out("o_nsa", [TOK, 512], BF16)

    es = contextlib.ExitStack()
    sc = Sched(nc, es)
    sb = lambda name, shape, dt: es.enter_context(nc.sbuf_tensor(name, shape, dt))

    def const_load(name, src, shape, dt):
        t = sb(name + "_s", shape, dt)
        b = Buf()
        sc.dma("sp", t[:, :], src, b, writes=[b])
        return t, b

    A0 = sb("A0", [128, SLAB], BF16)
    A1 = sb("A1", [128, SLAB], BF16)
    BS = sb("BS", [128, S], BF16)
    A0_b, A1_b, BS_b = Buf(), Buf(), Buf()
    tc_t, tc_b = const_load("tc_t", tc_d, [128, TABW], BF16)
    tw_t, tw_b = const_load("tw_t", tw_d, [128, TABW], BF16)
    ident, ident_b = const_load("ident_t", ident_d, [128, 128], BF16)
    sbank = [(es.enter_context(nc.psum_tensor("sbk%d" % i, [128, 512], F32)), Buf()) for i in range(3)]
    accs = [(es.enter_context(nc.psum_tensor("acc%d" % i, [128, 512], F32)), Buf()) for i in range(4)]
    tp_ps = es.enter_context(nc.psum_tensor("tp_ps", [128, 128], BF16))
    tp_b = Buf()
    pTs = [(sb("pT%d" % i, [128, 512], BF16), Buf()) for i in range(3)]
    ost = [sb("ost%d" % i, [128, 512], BF16) for i in range(2)]
    ost_b = bufs(2)
    rd = [sb("rd%d" % i, [128, 1], F32) for i in range(4)]
    rd_b = bufs(4)
    rdi = [0]

    def next_rd():
        k = rdi[0] % 4
        rdi[0] += 1
        return rd[k], rd_b[k]

    if do_mla:
        qn_t = [sb("qn_t%d" % i, [128, 512], BF16) for i in range(2)]
        qn_b = bufs(2)
        qr_t = [sb("qr_t%d" % i, [128, 512], BF16) for i in range(2)]
        qr_b = bufs(2)
        for i in range(2):
            sc.op("dve", lambda E: E.memset(qr_t[i][64:128, :], 0.0), writes=[qr_b[i]])
        sc.op("dve", lambda E: E.memset(BS[64:128, :], 0.0), writes=[BS_b])
        sc.dma("sp", BS[0:64, :], m_kpe, BS_b, writes=[BS_b])
        qi = 0
        oi = 0
        for h in range(6):
            sc.dma("sp", A0[:, :S], m_kn[h], A0_b, writes=[A0_b])
            sc.dma("sp", A1[:, :], m_v[h], A1_b, writes=[A1_b])
            for i in range(nblk):
                k = qi % 2
                qi += 1
                qsl = slice(i * 512, (i + 1) * 512)
                sc.dma("sp", qn_t[k][:, :], m_qn[h, :, qsl], qn_b[k], writes=[qn_b[k]])
                sc.dma("sp", qr_t[k][0:64, :], m_qr[h, :, qsl], qr_b[k], writes=[qr_b[k]])
                steps = []
                for kt in range(16 * i + 16):
                    ks = slice(kt * 128, (kt + 1) * 128)
                    mm = [(A0[:, ks], qn_t[k][:, :], [A0_b, qn_b[k]]),
                          (BS[:, ks], qr_t[k][:, :], [BS_b, qr_b[k]])]
                    if kt >= 16 * i:
                        off = 1920 - 128 * (kt - 16 * i)
                        mm.append((ident[:, :], tc_t[:, off:off + 512], [ident_b, tc_b]))
                    steps.append(dict(mm=mm, scale=MLA_SCALE, bias=None,
                                      v=(A1[:, kt * 129:(kt + 1) * 129], [A1_b])))
                att_block(sc, steps, sbank, pTs, accs, 129)
                for qs in range(4):
                    acc, ab = accs[qs]
                    r_t, r_b = next_rd()
                    sc.op("dve", lambda E: E.reciprocal(out=r_t[:, :], in_=acc[:, 128:129]), reads=[ab], writes=[r_b])
                    o = oi % 2
                    oi += 1
                    sc.op("dve", lambda E: E.tensor_scalar(out=ost[o][:, 0:128], in0=acc[:, 0:128],
                                                           scalar1=r_t[:, 0:1], scalar2=None, op0=ALU.mult),
                          reads=[ab, r_b], writes=[ost_b[o]])
                    rows = slice(i * 512 + qs * 128, i * 512 + (qs + 1) * 128)
                    sc.dma("sp", o_mla[rows, h * 128:(h + 1) * 128], ost[o][:, 0:128], ost_b[o],
                           reads=[ost_b[o]], is_output=True)

    if do_dil:
        dbias, dbias_b = const_load("dbias", d_bias, [128, 8 * 256], F32)
        dsT = [sb("dsT%d" % i, [128, 256], F32) for i in range(2)]
        dsT_b = bufs(2)
        dpT = [sb("dpT%d" % i, [128, 256], BF16) for i in range(2)]
        dpT_b = bufs(2)
        dst = [sb("dst%d" % i, [128, 129], F32) for i in range(2)]
        dst_b = bufs(2)
        dq_t = A0[:, 0:4096]
        dk_t = A0[:, 4096:4096 + 33 * 128]
        dv_t = A1[:, 0:33 * 129]
        di = 0
        for hd in range(6):
            g = hd // 2
            sc.dma("sp", dq_t, d_q[hd], A0_b, writes=[A0_b])
            sc.dma("sp", dk_t, d_k[hd], A0_b, writes=[A0_b])
            sc.dma("sp", dv_t, d_v[hd], A1_b, writes=[A1_b])
            for n in range(32):
                if g == 0:
                    seq_start = False
                    tab = hd if n == 0 else 2 + hd
                else:
                    seq_start = (n == 0) if g == 1 else (n % 8 == 0)
                    tab = 2 + hd
                k = di % 2
                di += 1
                qap = dq_t[:, n * 128:(n + 1) * 128]
                bA, bA_b = sbank[(2 * di) % 3]
                bB, bB_b = sbank[(2 * di + 1) % 3]
                if not seq_start:
                    sc.op("pe", lambda E: E.matmul(bA[:, 0:128], lhsT=dk_t[:, n * 128:(n + 1) * 128], rhs=qap,
                                                   start=True, stop=True), reads=[A0_b], writes=[bA_b])
                sc.op("pe", lambda E: E.matmul(bB[:, 0:128], lhsT=dk_t[:, (n + 1) * 128:(n + 2) * 128], rhs=qap,
                                               start=True, stop=True), reads=[A0_b], writes=[bB_b])
                c0 = 0 if not seq_start else 128
                if not seq_start:
                    sc.op("dve", lambda E: E.scalar_tensor_tensor(out=dsT[k][:, 0:128], in0=bA[:, 0:128],
                                                                  scalar=HD_SCALE,
                                                                  in1=dbias[:, tab * 256:tab * 256 + 128],
                                                                  op0=ALU.mult, op1=ALU.add),
                          reads=[bA_b, dbias_b], writes=[dsT_b[k]])
                sc.op("dve", lambda E: E.scalar_tensor_tensor(out=dsT[k][:, 128:256], in0=bB[:, 0:128],
                                                              scalar=HD_SCALE,
                                                              in1=dbias[:, tab * 256 + 128:tab * 256 + 256],
                                                              op0=ALU.mult, op1=ALU.add),
                      reads=[bB_b, dbias_b], writes=[dsT_b[k]])
                sc.op("act", lambda E: E.activation(out=dpT[k][:, c0:256], in_=dsT[k][:, c0:256], func=AF.Exp),
                      reads=[dsT_b[k]], writes=[dpT_b[k]])
                acc, ab = accs[di % 4]
                if not seq_start:
                    sc.op("pe", lambda E: E.matmul(acc[:, :129], lhsT=dpT[k][:, 0:128],
                                                   rhs=dv_t[:, n * 129:(n + 1) * 129], start=True, stop=False),
                          reads=[dpT_b[k], A1_b], writes=[ab])
                sc.op("pe", lambda E: E.matmul(acc[:, :129], lhsT=dpT[k][:, 128:256],
                                               rhs=dv_t[:, (n + 1) * 129:(n + 2) * 129], start=seq_start, stop=True),
                      reads=[dpT_b[k], A1_b], writes=[ab])
                sc.op("act", lambda E: E.activation(out=dst[k][:, :], in_=acc[:, :129], func=AF.Copy),
                      reads=[ab], writes=[dst_b[k]])
                sc.dma("sp", o_dil[hd, n * 128:(n + 1) * 128, :], dst[k][:, :], dst_b[k], reads=[dst_b[k]],
                       is_output=True)

    if do_nsa:
        cmask, cmask_b = const_load("cmask", cmask_d, [128, 512], BF16)
        cmaskp, cmaskp_b = const_load("cmaskp", cmaskp_d, [128, 512], BF16)
        capt, cap_b = const_load("capt", cap_d, [128, 512], F32)
        floort, floor_b = const_load("floort", floor_d, [128, 512], F32)
        bcmp, bcmp_b = const_load("bcmp", bcmp_d, [128, 32], F32)
        bslc, bslc_b = const_load("bslc", bslc_d, [128, 512], F32)
        bwin, bwin_b = const_load("bwin", bwin_d, [128, 80], F32)
        g_sb, g_b = const_load("g_sb", n_g, [128, 32 * 12], F32)
        w2k, w2k_b = const_load("w2k_t", w2k_d, [128, 128], BF16)
        w2v, w2v_b = const_load("w2v_t", w2v_d, [128, 128], BF16)
        posT, posT_b = const_load("posT_t", posT_d, [128, 32], BF16)
        sc.dma("sp", BS[:, 0:8192], ewide_d, BS_b, writes=[BS_b])
        sc.dma("sp", BS[:, 8192:12288], w1k_d, BS_b, writes=[BS_b])
        sc.dma("sp", BS[:, 12288:16384], w1v_d, BS_b, writes=[BS_b])
        ewide = BS[:, 0:8192]
        vcx = sb("vcx", [128, 8, 385], BF16)
        vcx_b = Buf()
        sc.dma("sp", vcx[:, :, 128:385], cmpsel_d, vcx_b, writes=[vcx_b])
        kcT = sb("kcT", [128, 1024], BF16)
        kcT_b = Buf()
        hT = [sb("hT%d" % i, [128, 1024], BF16) for i in range(2)]
        hT_b = bufs(2)
        bcol = [sb("bcol%d" % i, [128, 1], F32) for i in range(2)]
        bcol_b = bufs(2)
        sc.dma("sp", A0[:, :S], n_kc, A0_b, writes=[A0_b])
        sc.dma("sp", A1[:, :S], n_vc, A1_b, writes=[A1_b])
        for w, (src, src_b) in enumerate(((A0, A0_b), (A1, A1_b))):
            w1 = BS[:, 8192 + w * 4096:8192 + (w + 1) * 4096]
            sc.op("dve", lambda E: E.memset(hT[w][:, :], 0.0), writes=[hT_b[w]])
            bk, bkb = sbank[2]
            for l in range(32):
                sc.op("pe", lambda E: E.matmul(bk[:, 0:1], lhsT=w1[:, l * 128:(l + 1) * 128], rhs=posT[:, l:l + 1],
                                               start=(l == 0), stop=(l == 31)),
                      reads=[BS_b, posT_b], writes=[bkb])
            sc.op("dve", lambda E: E.tensor_copy(out=bcol[w][:, :], in_=bk[:, 0:1]), reads=[bkb], writes=[bcol_b[w]])
            for c2 in range(2):
                ncol = 512 if c2 == 0 else 511
                bank, bb = sbank[c2]
                for l in range(32):
                    st0 = 16 * 512 * c2 + l
                    sc.op("pe", lambda E: E.matmul(bank[:, :ncol], lhsT=w1[:, l * 128:(l + 1) * 128],
                                                   rhs=src[:, st0:st0 + 16 * ncol:16],
                                                   start=(l == 0), stop=(l == 31)),
                          reads=[BS_b, src_b], writes=[bb])
                sc.op("act", lambda E: E.activation(out=hT[w][:, c2 * 512:c2 * 512 + ncol], in_=bank[:, :ncol],
                                                    func=AF.Silu, bias=bcol[w][:, 0:1]),
                      reads=[bb, bcol_b[w]], writes=[hT_b[w]])
        for c2 in range(2):
            bank, bb = sbank[c2]
            sc.op("pe", lambda E: E.matmul(bank[:, :], lhsT=w2k[:, :], rhs=hT[0][:, c2 * 512:(c2 + 1) * 512],
                                           start=True, stop=True), reads=[w2k_b, hT_b[0]], writes=[bb])
            sc.op("dve", lambda E: E.tensor_copy(out=kcT[:, c2 * 512:(c2 + 1) * 512], in_=bank[:, :]),
                  reads=[bb], writes=[kcT_b])
        for ct in range(8):
            bank, bb = sbank[ct % 3]
            sc.op("pe", lambda E: E.matmul(bank[:, 0:128], lhsT=hT[1][:, ct * 128:(ct + 1) * 128], rhs=w2v[:, :],
                                           start=True, stop=True), reads=[w2v_b, hT_b[1]], writes=[bb])
            sc.op("dve", lambda E: E.tensor_copy(out=vcx[:, ct, 0:128], in_=bank[:, 0:128]),
                  reads=[bb], writes=[vcx_b])
        sc.dma("sp", A0[:, :S], n_ks, A0_b, writes=[A0_b])
        sc.dma("sp", A1[:, :], n_vs, A1_b, writes=[A1_b])
        nq_t = sb("nq_t", [128, 4, 512], BF16)
        nq_b = Buf()
        kw_t = sb("kw_t", [128, 20 * 128], BF16)
        vw_t = sb("vw_t", [128, 20 * 129], BF16)
        kw_b, vw_b = Buf(), Buf()
        score = sb("score", [128, 4, 256], F32)
        score_b = bufs(4)
        s2 = sb("s2", [128, 256], F32)
        s2_b = Buf()
        work = sb("work", [128, 256], F32)
        work_b = Buf()
        m8 = sb("m8", [128, 16], F32)
        m8_b = Buf()
        negsel = sb("negsel", [128, 256], BF16)
        negsel_b = Buf()
        negselT = sb("negselT", [128, 2, 512], BF16)
        negselT_b = Buf()
        nso = sb("nso", [128, 4, 512], F32)
        nso_b = bufs(4)
        cf = [sb("cf%d" % i, [128, 1], F32) for i in range(4)]
        cf_b = bufs(4)
        cfi = [0]
        oi = 0

        def coef(acc, ab, dcol, gcol, tile_n, eps):
            r_t, r_b = next_rd()
            if eps:
                sc.op("dve", lambda E: E.tensor_scalar(out=r_t[:, :], in0=acc[:, dcol:dcol + 1], scalar1=1e-30,
                                                       scalar2=None, op0=ALU.add), reads=[ab], writes=[r_b])
                sc.op("dve", lambda E: E.reciprocal(out=r_t[:, :], in_=r_t[:, :]), reads=[r_b], writes=[r_b])
            else:
                sc.op("dve", lambda E: E.reciprocal(out=r_t[:, :], in_=acc[:, dcol:dcol + 1]), reads=[ab],
                      writes=[r_b])
            k = cfi[0] % 4
            cfi[0] += 1
            sc.op("dve", lambda E: E.tensor_tensor(out=cf[k][:, :], in0=r_t[:, :],
                                                   in1=g_sb[:, tile_n * 12 + gcol:tile_n * 12 + gcol + 1],
                                                   op=ALU.mult), reads=[r_b, g_b], writes=[cf_b[k]])
            return (r_t, r_b), (cf[k], cf_b[k])

        for i in range(nblk):
            qsl = slice(i * 512, (i + 1) * 512)
            for h in range(4):
                sc.dma("sp", nq_t[:, h, :], n_q[h, :, qsl], nq_b, writes=[nq_b])
            kt0 = 16 * i - 4
            j0 = 4 if i == 0 else 0
            sc.dma("sp", kw_t[:, j0 * 128:20 * 128], n_kw[:, (kt0 + j0) * 128:(kt0 + 20) * 128], kw_b, writes=[kw_b])
            sc.dma("sp", vw_t[:, j0 * 129:20 * 129], n_vw[:, (kt0 + j0) * 129:(kt0 + 20) * 129], vw_b, writes=[vw_b])
            for h in range(4):
                steps = []
                for ct in range(i + 1):
                    mm = [(kcT[:, ct * 128:(ct + 1) * 128], nq_t[:, h, :], [kcT_b, nq_b])]
                    if ct == i:
                        mm.append((ident[:, :], cmask[:, :], [ident_b, cmask_b]))
                    if ct == i - 1:
                        mm.append((ident[:, :], cmaskp[:, :], [ident_b, cmaskp_b]))
                    bi = h * 8 + (ct - i + 7)
                    steps.append(dict(mm=mm, scale=HD_SCALE, bias=(bcmp[:, bi:bi + 1], bcmp_b),
                                      v=(vcx[:, ct, :], [vcx_b])))
                att_block(sc, steps, sbank, pTs, accs, 385)
                for qs in range(4):
                    acc, ab = accs[qs]
                    (r_t, r_b), (c_t, c_b) = coef(acc, ab, 384, 3 * h + 0, 4 * i + qs, True)
                    sc.op("dve", lambda E: E.tensor_scalar(out=nso[:, qs, h * 128:(h + 1) * 128], in0=acc[:, 0:128],
                                                           scalar1=c_t[:, 0:1], scalar2=None, op0=ALU.mult),
                          reads=[ab, c_b], writes=[nso_b[qs]])
                    if h == 0:
                        sc.op("dve", lambda E: E.tensor_scalar(out=score[:, qs, :], in0=acc[:, 128:384],
                                                               scalar1=r_t[:, 0:1], scalar2=None, op0=ALU.mult),
                              reads=[ab, r_b], writes=[score_b[qs]])
                    else:
                        sc.op("dve", lambda E: E.scalar_tensor_tensor(out=score[:, qs, :], in0=acc[:, 128:384],
                                                                      scalar=r_t[:, 0:1], in1=score[:, qs, :],
                                                                      op0=ALU.mult, op1=ALU.add),
                              reads=[ab, r_b, score_b[qs]], writes=[score_b[qs]])
            for qs in range(4):
                off = 256 - 32 * i - 2 * qs
                sc.op("dve", lambda E: E.tensor_tensor(out=s2[:, :], in0=score[:, qs, :], in1=capt[:, off:off + 256],
                                                       op=ALU.min), reads=[score_b[qs], cap_b], writes=[s2_b])
                sc.op("dve", lambda E: E.tensor_tensor(out=s2[:, :], in0=s2[:, :], in1=floort[:, off:off + 256],
                                                       op=ALU.max), reads=[s2_b, floor_b], writes=[s2_b])
                sc.op("dve", lambda E: E.tensor_scalar(out=s2[:, 0:1], in0=s2[:, 0:1], scalar1=100.0, scalar2=None,
                                                       op0=ALU.max), reads=[s2_b], writes=[s2_b])
                sc.op("dve", lambda E: E.max(out=m8[:, 0:8], in_=s2[:, :]), reads=[s2_b], writes=[m8_b])
                sc.op("dve", lambda E: E.match_replace(out=work[:, :], in_to_replace=m8[:, 0:8], in_values=s2[:, :],
                                                       imm_value=-1e9), reads=[s2_b, m8_b], writes=[work_b])
                sc.op("dve", lambda E: E.max(out=m8[:, 8:16], in_=work[:, :]), reads=[work_b], writes=[m8_b])
                sc.op("dve", lambda E: E.tensor_scalar(out=negsel[:, :], in0=s2[:, :], scalar1=m8[:, 15:16],
                                                       scalar2=None, op0=ALU.is_lt),
                      reads=[s2_b, m8_b], writes=[negsel_b])
                for jh in range(2):
                    sc.op("pe", lambda E: E.transpose(out=tp_ps[:, :], in_=negsel[:, jh * 128:(jh + 1) * 128],
                                                      identity=ident[:, :]),
                          reads=[negsel_b, ident_b], writes=[tp_b])
                    sc.op("dve", lambda E: E.tensor_copy(out=negselT[:, jh, qs * 128:(qs + 1) * 128], in_=tp_ps[:, :]),
                          reads=[tp_b], writes=[negselT_b])
            for h in range(4):
                steps = []
                for kt in range(16 * i + 16):
                    ks = slice(kt * 128, (kt + 1) * 128)
                    e0 = 128 * (kt % 64)
                    mm = [(A0[:, ks], nq_t[:, h, :], [A0_b, nq_b]),
                          (ewide[:, e0:e0 + 128], negselT[:, kt // 64, :], [BS_b, negselT_b])]
                    if kt >= 16 * i:
                        off = 1920 - 128 * (kt - 16 * i)
                        mm.append((ident[:, :], tc_t[:, off:off + 512], [ident_b, tc_b]))
                    bi = h * 128 + (kt - 16 * i + 112)
                    steps.append(dict(mm=mm, scale=HD_SCALE, bias=(bslc[:, bi:bi + 1], bslc_b),
                                      v=(A1[:, kt * 129:(kt + 1) * 129], [A1_b])))
                att_block(sc, steps, sbank, pTs, accs, 129)
                for qs in range(4):
                    acc, ab = accs[qs]
                    (r_t, r_b), (c_t, c_b) = coef(acc, ab, 128, 3 * h + 1, 4 * i + qs, False)
                    sc.op("dve", lambda E: E.scalar_tensor_tensor(out=nso[:, qs, h * 128:(h + 1) * 128],
                                                                  in0=acc[:, 0:128], scalar=c_t[:, 0:1],
                                                                  in1=nso[:, qs, h * 128:(h + 1) * 128],
                                                                  op0=ALU.mult, op1=ALU.add),
                          reads=[ab, c_b, nso_b[qs]], writes=[nso_b[qs]])
                steps = []
                for jw in range(j0, 20):
                    off = 1920 - 128 * (jw - 4)
                    mm = [(kw_t[:, jw * 128:(jw + 1) * 128], nq_t[:, h, :], [kw_b, nq_b]),
                          (ident[:, :], tw_t[:, off:off + 512], [ident_b, tw_b])]
                    bi = h * 20 + jw
                    steps.append(dict(mm=mm, scale=HD_SCALE, bias=(bwin[:, bi:bi + 1], bwin_b),
                                      v=(vw_t[:, jw * 129:(jw + 1) * 129], [vw_b])))
                att_block(sc, steps, sbank, pTs, accs, 129)
                for qs in range(4):
                    acc, ab = accs[qs]
                    (r_t, r_b), (c_t, c_b) = coef(acc, ab, 128, 3 * h + 2, 4 * i + qs, False)
                    sc.op("dve", lambda E: E.scalar_tensor_tensor(out=nso[:, qs, h * 128:(h + 1) * 128],
                                                                  in0=acc[:, 0:128], scalar=c_t[:, 0:1],
                                                                  in1=nso[:, qs, h * 128:(h + 1) * 128],
                                                                  op0=ALU.mult, op1=ALU.add),
                          reads=[ab, c_b, nso_b[qs]], writes=[nso_b[qs]])
            for qs in range(4):
                o = oi % 2
                oi += 1
                sc.op("act", lambda E: E.activation(out=ost[o][:, :], in_=nso[:, qs, :], func=AF.Copy),
                      reads=[nso_b[qs]], writes=[ost_b[o]])
                rows = slice(i * 512 + qs * 128, i * 512 + (qs + 1) * 128)
                sc.dma("sp", o_nsa[rows, :], ost[o][:, :], ost_b[o], reads=[ost_b[o]], is_output=True)
    sc.finish()
    es.close()
    return nc


def alibi_slopes_np():
    return (2.0 ** (-8.0 * np.arange(1, 11, dtype=np.float64) / 10)).astype(np.float32)


DIL_D = (1, 4, 16)


def p2_tables(r):
    t = {}
    k = np.arange(128)[:, None]
    y = np.arange(TABW)[None, :]
    rel = (y - 1920) + 512 * r
    t["tc"] = np.where(k <= rel, 0.0, NEG).astype(NPBF)
    t["tw"] = np.where((rel - k >= 0) & (rel - k < 512), 0.0, NEG).astype(NPBF)
    t["ident"] = np.eye(128, dtype=np.float32).astype(NPBF)
    sl = alibi_slopes_np()
    a = np.arange(128)[None, :]
    c = np.arange(128)[:, None]
    db = np.zeros((128, 8, 256), np.float32)
    for tab in range(8):
        hd = tab if tab < 2 else tab - 2
        d = DIL_D[hd // 2]
        jp = 128 + a - c
        jc = a - c
        prev = np.where(a <= c, -sl[hd] * d * jp, NEG)
        cur = np.where(jc >= 0, -sl[hd] * d * jc, NEG)
        if tab < 2 and r == 0:
            prev = np.full((128, 128), NEG)
        db[:, tab, :128] = prev
        db[:, tab, 128:] = cur
    t["d_bias"] = db.reshape(128, 8 * 256)
    q = np.arange(512)[None, :]
    t["cmask"] = np.where(16 * c + 31 <= 512 * r + q, 0.0, NEG).astype(NPBF)
    t["cmaskp"] = np.where(16 * (c - 128) + 31 <= 512 * r + q, 0.0, NEG).astype(NPBF)
    u = np.arange(512)[None, :]
    dlt = u - 256 - 8 * r - (np.arange(128)[:, None] // 64)
    t["cap"] = np.where(dlt > 0, -1.0, 1e9).astype(np.float32)
    t["floor"] = np.where((dlt == 0) | (dlt == -1), 100.0, -1e9).astype(np.float32)
    ns = sl[6:10].astype(np.float64)
    p = np.arange(128)[:, None]
    bc = np.zeros((128, 4, 8))
    bs = np.zeros((128, 4, 128))
    bw = np.zeros((128, 4, 20))
    for h in range(4):
        bc[:, h, :] = ns[h] * (2048 * (np.arange(8)[None, :] - 7) + 16 * p + 15.5 - 512 * r)
        bs[:, h, :] = ns[h] * (128 * (np.arange(128)[None, :] - 112) + p - 512 * r)
        bw[:, h, :] = ns[h] * (128 * (np.arange(20)[None, :] - 4) + p - 512 * r)
    t["bcmp"] = bc.reshape(128, 32).astype(np.float32)
    t["bslc"] = bs.reshape(128, 512).astype(np.float32)
    t["bwin"] = bw.reshape(128, 80).astype(np.float32)
    x = np.arange(8192)[None, :]
    t["ewide"] = np.where(p == x // 64, NEG, 0.0).astype(NPBF)
    cg = np.arange(1024).reshape(8, 128)
    j = np.arange(256)[None, None, :]
    M = ((cg[:, :, None] >= 4 * j - 1) & (cg[:, :, None] <= 4 * j + 3)).astype(np.float32)
    M = np.concatenate([M, np.ones((8, 128, 1), np.float32)], axis=2)
    t["cmpsel"] = np.ascontiguousarray(M.transpose(1, 0, 2)).astype(NPBF)
    return t


def dil_index(g, r):
    if g == 0:
        return 4096 * r + np.arange(4096)
    if g == 1:
        return np.arange(4096) * 4 + r
    return (np.arange(1024)[None, :] * 16 + (4 * r + np.arange(4))[:, None]).reshape(-1)


def tok_major_tiles(v):
    n = v.shape[0] // 128
    return np.ascontiguousarray(v.reshape(n, 128, -1).transpose(1, 0, 2)).reshape(128, -1)


def prep_p2(r, nat, own, tab=None):
    m = dict(p2_tables(r) if tab is None else tab)
    m["m_qn"], m["m_qr"], m["n_q"], m["n_g"] = own["qn"], own["qr"], own["nq"], own["g"]
    m["m_kn"], m["m_kpe"] = nat["kn"], nat["kpe"]
    m["m_v"] = np.stack([tok_major_tiles(nat["mv"][:, h, :]) for h in range(6)])
    dq, dk, dv = [], [], []
    for hd in range(6):
        g = hd // 2
        idx = dil_index(g, r)
        dq.append(nat["dq"][hd][:, idx])
        if g == 0 and r > 0:
            pk = nat["dk"][hd][:, 4096 * r - 128:4096 * r]
            pv = nat["dv"][4096 * r - 128:4096 * r, hd, :]
        else:
            pk = np.zeros((128, 128), NPBF)
            pv = np.zeros((128, 129), NPBF)
        dk.append(np.concatenate([pk, nat["dk"][hd][:, idx]], axis=1))
        dv.append(tok_major_tiles(np.concatenate([pv, nat["dv"][idx, hd, :]], axis=0)))
    m["d_q"], m["d_k"], m["d_v"] = np.stack(dq), np.stack(dk), np.stack(dv)
    m["n_kc"], m["n_vc"], m["n_ks"], m["n_kw"] = nat["nkc"], nat["nvc"], nat["nks"], nat["nkw"]
    m["n_vs"] = tok_major_tiles(nat["nvs"])
    m["n_vw"] = tok_major_tiles(nat["nvw"])
    for k in ("w1k", "w1v", "w2k", "w2v", "posT"):
        m[k] = nat[k]
    return {k: np.ascontiguousarray(v) for k, v in m.items()}


def build_po():
    nc = new_nc()
    dt_in = lambda name, shape, dt: nc.dram_tensor(name, shape, dt, kind="ExternalInput").ap()
    xT = dt_in("xT", [16, 128, TOK], F32)
    o_m = dt_in("oT_mla", [6, 128, TOK], BF16)
    o_d = dt_in("oT_dil", [6, 128, TOK], F32)
    den = dt_in("denT", [6, TOK], F32)
    o_n = dt_in("oT_nsa", [4, 128, TOK], BF16)
    w_mo = dt_in("w_mo", [16, 128, 16 * 128], BF16)
    sel2_d = dt_in("sel2", [6, 256], F32)
    xo = nc.dram_tensor("xo", [16, 128, TOK], F32, kind="ExternalOutput").ap()
    es = contextlib.ExitStack()
    sc = Sched(nc, es)
    sb = lambda name, shape, dt: es.enter_context(nc.sbuf_tensor(name, shape, dt))
    pr = PsumRing(nc, es)
    sel2 = sb("sel2_s", [6, 256], F32)
    sel2_b = Buf()
    sc.dma("sp", sel2[:, :], sel2_d, sel2_b, writes=[sel2_b])
    ob = sb("ob", [128, 16, PT], BF16)
    ob_b = bufs(16)
    od = sb("od", [128, 6, PT], F32)
    od_b = bufs(6)
    dn = sb("dn", [6, PT], F32)
    dn_b = Buf()
    rec = sb("rec", [128, 2, PT], F32)
    rec_b = bufs(2)
    NW = 3
    wch = [sb("wch%d" % i, [128, 16 * 128], BF16) for i in range(NW)]
    wch_b = bufs(NW)
    NX = 3
    xc = [sb("xc%d" % i, [128, PT], F32) for i in range(NX)]
    xc_b = bufs(NX)
    wi = 0
    xi = 0
    for t in range(TOK // PT):
        tsl = slice(t * PT, (t + 1) * PT)
        for h in range(6):
            sc.dma("sp", ob[:, h, :], o_m[h, :, tsl], ob_b[h], writes=[ob_b[h]])
            sc.dma("sp", od[:, h, :], o_d[h, :, tsl], od_b[h], writes=[od_b[h]])
        for h in range(4):
            sc.dma("sp", ob[:, 12 + h, :], o_n[h, :, tsl], ob_b[12 + h], writes=[ob_b[12 + h]])
        sc.dma("sp", dn[:, :], den[:, tsl], dn_b, writes=[dn_b])
        for s in range(2):
            ps, ps_b = pr.next()
            sc.op("pe", lambda E: E.matmul(ps[:, :], lhsT=sel2[:, s * 128:(s + 1) * 128], rhs=dn[:, :],
                                           start=True, stop=True), reads=[sel2_b, dn_b], writes=[ps_b])
            sc.op("dve", lambda E: E.reciprocal(out=rec[:, s, :], in_=ps[:, :]), reads=[ps_b], writes=[rec_b[s]])
        for hd in range(6):
            sc.op("dve", lambda E: E.tensor_tensor(out=ob[:, 6 + hd, :], in0=od[:, hd, :], in1=rec[:, hd % 2, :],
                                                   op=ALU.mult),
                  reads=[od_b[hd], rec_b[hd % 2]], writes=[ob_b[6 + hd]])
        pend = []

        def load_w(j):
            nonlocal wi
            k = wi % NW
            wi += 1
            sc.dma("sp", wch[k][:, :], w_mo[j], wch_b[k], writes=[wch_b[k]])
            return k
        pend.append(load_w(0))
        for dmc in range(16):
            if dmc + 1 < 16:
                pend.append(load_w(dmc + 1))
            k = pend.pop(0)
            x = xi % NX
            xi += 1
            sc.dma("sp", xc[x][:, :], xT[dmc, :, tsl], xc_b[x], writes=[xc_b[x]])
            ps, ps_b = pr.next()
            for fc in range(16):
                sc.op("pe", lambda E: E.matmul(ps[:, :], lhsT=wch[k][:, fc * 128:(fc + 1) * 128], rhs=ob[:, fc, :],
                                               start=(fc == 0), stop=(fc == 15)),
                      reads=[wch_b[k], ob_b[fc]], writes=[ps_b])
            sc.op("dve", lambda E: E.tensor_tensor(out=xc[x][:, :], in0=ps[:, :], in1=xc[x][:, :], op=ALU.add),
                  reads=[ps_b, xc_b[x]], writes=[xc_b[x]])
            sc.dma("sp", xo[dmc, :, tsl], xc[x][:, :], xc_b[x], reads=[xc_b[x]], is_output=True)
    sc.finish()
    es.close()
    return nc


def lay_w_mo(w):
    a = w.reshape(16, 128, 16, 128)
    return np.ascontiguousarray(a.transpose(2, 1, 0, 3)).reshape(16, 128, 16 * 128)


def sel2_table():
    s = np.zeros((6, 256), np.float32)
    for gs in range(6):
        s[gs, (gs % 2) * 128:(gs % 2 + 1) * 128] = 1.0
    return s


_NC_CACHE = {}
CONV_KEYS = ("ffn1_w_in", "ffn1_w_out", "w_mix_in", "mla_w_uq", "mla_w_ukv", "nsa_cmp_pos", "nsa_phi_k1",
             "nsa_phi_k2", "nsa_phi_v1", "nsa_phi_v2", "w_mix_out", "ffn2_w_in", "ffn2_w_out")
CONV_NC = 21 * CONV_CH


def _launch(key, builder, in_maps):
    if key not in _NC_CACHE:
        _NC_CACHE[key] = builder()
    res = run_bass_kernel_spmd(_NC_CACHE[key], in_maps, core_ids=list(range(NCORES)))
    return res.results


def _convert_layer(inp, l):
    flats = [np.ascontiguousarray(inp[k][l]).reshape(-1) for k in CONV_KEYS]
    n = sum(f.size for f in flats)
    buf = np.zeros(NCORES * 128 * CONV_NC, np.float32)
    o = 0
    for f in flats:
        buf[o:o + f.size] = f
        o += f.size
    buf = buf.reshape(NCORES, 128, CONV_NC)
    res = _launch("conv", lambda: build_conv(CONV_NC), [{"src": buf[c]} for c in range(NCORES)])
    out = np.stack([np.asarray(res[c]["dst"]) for c in range(NCORES)]).reshape(-1)
    w = {}
    o = 0
    for k in CONV_KEYS:
        shp = inp[k][l].shape
        sz = int(np.prod(shp))
        w[k] = out[o:o + sz].reshape(shp)
        o += sz
    return w


def _nat_fm(L):
    sh = L[0].shape
    return np.stack([a.reshape(sh[:-1] + (8, 512)) for a in L], axis=-2).reshape(sh[:-1] + (S,))


def _nat_tm(L):
    c = L[0].shape[1]
    return np.stack([a.reshape(8, 512, c) for a in L], axis=1).reshape(S, c)


def kernel(**inp):
    inp = {k: np.asarray(v) for k, v in inp.items()}
    x = inp["x"]
    pos = [core_positions(r) for r in range(4)]
    rope = [rope_tables(pos[r]) for r in range(4)]
    xs = []
    for c in range(NCORES):
        b, r = divmod(c, 4)
        xs.append(np.ascontiguousarray(x[b][pos[r]].T).reshape(16, 128, TOK))
    sel2 = sel2_table()
    tabs = [p2_tables(r) for r in range(4)]
    for l in range(DEPTH):
        w = _convert_layer(inp, l)
        wi, wo = lay_w_in(w["ffn1_w_in"]), lay_w_out(w["ffn1_w_out"])
        g = lay_gain(inp["ffn1_norm"][l])
        res = _launch("pf", lambda: build_pf(False),
                      [{"xT": xs[c], "g": g, "w_in": wi, "w_out": wo} for c in range(NCORES)])
        xs = [np.asarray(res[c]["xo"]) for c in range(NCORES)]
        del wi, wo
        pw = lay_pp_weights(w["w_mix_in"], w["mla_w_uq"], w["mla_w_ukv"])
        base = dict(pw)
        base.update(g=lay_gain(inp["mix_norm"][l]), gq=lay_gain(inp["mla_q_norm"][l]),
                    gkv=lay_gain(inp["mla_kv_norm"][l]))
        maps = []
        for c in range(NCORES):
            m = dict(base)
            m["xT"] = xs[c]
            m["cos2"], m["sgs"] = rope[c % 4]
            maps.append(m)
        R = _launch("pp", build_pp, maps)
        R = [{k: np.asarray(v) for k, v in R[c].items()} for c in range(NCORES)]
        wn = dict(w1k=lay_k(w["nsa_phi_k1"]), w1v=lay_k(w["nsa_phi_v1"]), w2k=w["nsa_phi_k2"], w2v=w["nsa_phi_v2"],
                  posT=np.ascontiguousarray(w["nsa_cmp_pos"].T))
        maps = []
        for b in range(B):
            C = [R[4 * b + r] for r in range(4)]
            nat = dict(wn)
            nat["kn"] = _nat_fm([c_["o_kn"] for c_ in C])
            nat["kpe"] = _nat_fm([c_["o_kpe"] for c_ in C])
            fm = _nat_fm([c_["o_fm"] for c_ in C])
            nat["dq"], nat["dk"] = fm[0:6], fm[6:12]
            nat["nkc"], nat["nvc"], nat["nks"], nat["nkw"] = fm[16], fm[17], fm[18], fm[19]
            nat["mv"] = _nat_tm([c_["o_mv"] for c_ in C]).reshape(S, 6, 129)
            nat["dv"] = _nat_tm([c_["o_dv"] for c_ in C]).reshape(S, 6, 129)
            nat["nvs"] = _nat_tm([c_["o_nvs"] for c_ in C])
            nat["nvw"] = _nat_tm([c_["o_nvw"] for c_ in C])
            for r in range(4):
                own = dict(qn=C[r]["o_qn"], qr=C[r]["o_qr"], nq=C[r]["o_fm"][12:16],
                           g=tok_major_tiles(C[r]["o_g"]))
                m = prep_p2_with_tables(r, nat, own, tabs[r])
                maps.append(m)
        del R
        A = _launch("p2", build_p2, maps)
        A = [{k: np.asarray(v) for k, v in A[c].items()} for c in range(NCORES)]
        del maps
        wmo = lay_w_mo(w["w_mix_out"])
        maps = []
        for b in range(B):
            dnat = np.zeros((6, S, 129), np.float32)
            for r in range(4):
                for hd in range(6):
                    dnat[hd][dil_index(hd // 2, r)] = A[4 * b + r]["o_dil"][hd]
            for r in range(4):
                c = 4 * b + r
                own = dnat[:, pos[r], :]
                maps.append({
                    "xT": xs[c],
                    "oT_mla": np.ascontiguousarray(A[c]["o_mla"].T).reshape(6, 128, TOK),
                    "oT_nsa": np.ascontiguousarray(A[c]["o_nsa"].T).reshape(4, 128, TOK),
                    "oT_dil": np.ascontiguousarray(own[:, :, :128].transpose(0, 2, 1)),
                    "denT": np.ascontiguousarray(own[:, :, 128]),
                    "w_mo": wmo, "sel2": sel2})
        res = _launch("po", build_po, maps)
        xs = [np.asarray(res[c]["xo"]) for c in range(NCORES)]
        del A, maps
        wi, wo = lay_w_in(w["ffn2_w_in"]), lay_w_out(w["ffn2_w_out"])
        g = lay_gain(inp["ffn2_norm"][l])
        if l < DEPTH - 1:
            res = _launch("pf", lambda: build_pf(False),
                          [{"xT": xs[c], "g": g, "w_in": wi, "w_out": wo} for c in range(NCORES)])
        else:
            gf = lay_gain(inp["final_norm"])
            res = _launch("pff", lambda: build_pf(True),
                          [{"xT": xs[c], "g": g, "gf": gf, "w_in": wi, "w_out": wo} for c in range(NCORES)])
        xs = [np.asarray(res[c]["xo"]) for c in range(NCORES)]
        del wi, wo, w
    out = np.empty((B, S, D), np.float32)
    for c in range(NCORES):
        b, r = divmod(c, 4)
        out[b][pos[r]] = xs[c].reshape(D, TOK).T
    return out


def prep_p2_with_tables(r, nat, own, tab):
    m = prep_p2(r, nat, own, tab)
    return m
```

```python
import contextlib
import numpy as np
import ml_dtypes
import concourse.bass as bass
import concourse.mybir as mybir
from concourse.bass_utils import run_bass_kernel_spmd

F32 = mybir.dt.float32
BF16 = mybir.dt.bfloat16
AF = mybir.ActivationFunctionType
ALU = mybir.AluOpType
NPBF = ml_dtypes.bfloat16

NCORES = 8
D = 2048
DFF = 5504
NFF = 43
DEPTH = 4
B = 2
S = 16384
EPS = 1e-6


class Sem:
    __slots__ = ("h", "cnt")

    def __init__(self, h):
        self.h = h
        self.cnt = 0


class Buf:
    __slots__ = ("w", "r", "ds")

    def __init__(self):
        self.w = None
        self.r = {}
        self.ds = None


def bufs(n):
    return [Buf() for _ in range(n)]


class Sched:
    def __init__(self, nc, es):
        self.nc = nc
        self.es = es
        self.E = {"pe": nc.tensor, "act": nc.scalar, "dve": nc.vector,
                  "pool": nc.gpsimd, "sp": nc.sync}
        self.sem = {k: Sem(es.enter_context(nc.semaphore("s_" + k)))
                    for k in ("pe", "act", "dve", "pool")}
        self.seen = {k: {} for k in self.E}
        self.nds = 0
        self.out_events = []
        self.store_q = None

    def _waits(self, eng, reads, writes):
        need = {}
        for b in reads:
            if b.w is not None:
                s, v = b.w
                if need.get(s, 0) < v:
                    need[s] = v
        for b in writes:
            if b.w is not None:
                s, v = b.w
                if need.get(s, 0) < v:
                    need[s] = v
            for s, v in b.r.items():
                if need.get(s, 0) < v:
                    need[s] = v
        seen = self.seen[eng]
        own = self.sem.get(eng)
        for s, v in need.items():
            if eng == "pe" and s is own:
                continue
            if seen.get(s, 0) < v:
                self.E[eng].wait_ge(s.h, v)
                seen[s] = v

    def _commit(self, ev, reads, writes):
        s, v = ev
        for b in reads:
            if b.r.get(s, 0) < v:
                b.r[s] = v
        for b in writes:
            b.w = ev
            b.r = {}

    def op(self, eng, fn, reads=(), writes=()):
        self._waits(eng, reads, writes)
        inst = fn(self.E[eng])
        s = self.sem[eng]
        s.cnt += 1
        inst.then_inc(s.h, 1)
        self._commit((s, s.cnt), reads, writes)

    def dma(self, q, out, in_, sb, reads=(), writes=(), is_output=False, **kw):
        if is_output and self.store_q is not None:
            q = self.store_q
        self._waits(q, reads, writes)
        if sb.ds is None:
            sb.ds = Sem(self.es.enter_context(self.nc.semaphore("d%d" % self.nds)))
            self.nds += 1
        inst = self.E[q].dma_start(out=out, in_=in_, **kw)
        sb.ds.cnt += 16
        inst.then_inc(sb.ds.h, 16)
        ev = (sb.ds, sb.ds.cnt)
        self._commit(ev, reads, writes)
        if is_output:
            self.out_events.append(ev)

    def finish(self):
        need = {}
        for s, v in self.out_events:
            if need.get(s, 0) < v:
                need[s] = v
        for s, v in need.items():
            self.E["sp"].wait_ge(s.h, v)


def new_nc():
    return bass.Bass("TRN2", target_bir_lowering=False)


CONV_CH = 4096


def build_conv(ncols):
    nc = new_nc()
    src = nc.dram_tensor("src", [128, ncols], F32, kind="ExternalInput").ap()
    dst = nc.dram_tensor("dst", [128, ncols], BF16, kind="ExternalOutput").ap()
    es = contextlib.ExitStack()
    sc = Sched(nc, es)
    NB = 3
    st = [es.enter_context(nc.sbuf_tensor("cs%d" % i, [128, CONV_CH], F32)) for i in range(NB)]
    ot = [es.enter_context(nc.sbuf_tensor("co%d" % i, [128, CONV_CH], BF16)) for i in range(NB)]
    sb, ob = bufs(NB), bufs(NB)
    nch = ncols // CONV_CH
    engs = ["dve", "act", "pool"]
    for i in range(nch):
        k = i % NB
        sl = slice(i * CONV_CH, (i + 1) * CONV_CH)
        sc.dma("sp", st[k][:], src[:, sl], sb[k], writes=[sb[k]])
        e = engs[i % 3]
        if e == "act":
            sc.op("act", lambda E: E.activation(out=ot[k][:], in_=st[k][:], func=AF.Copy),
                  reads=[sb[k]], writes=[ob[k]])
        else:
            sc.op(e, lambda E: E.tensor_copy(out=ot[k][:], in_=st[k][:]),
                  reads=[sb[k]], writes=[ob[k]])
        sc.dma("sp", dst[:, sl], ot[k][:], ob[k], reads=[ob[k]], is_output=True)
    sc.finish()
    es.close()
    return nc


TOK = 4096
TT = 1024
FF_HALVES = ((0, 22), (22, 21))


def rms_norm_fm(sc, nc, xs, xsb, xn, xnb, gain, gain_b, ones, ones_b, ss_ps, ss_b,
                sq, sq_b, rs, rs_b, rstd, rstd_b, nchunk, dim, tt):
    nh = tt // 512
    for c in range(nchunk):
        k = c % 2
        sc.op("act", lambda E: E.activation(out=sq[k][:, :tt], in_=xs[c], func=AF.Square),
              reads=[xsb[c]], writes=[sq_b[k]])
        for h in range(nh):
            sc.op("pe", lambda E: E.matmul(ss_ps[h][:, :], lhsT=ones[:, :], rhs=sq[k][:, h * 512:(h + 1) * 512],
                                           start=(c == 0), stop=(c == nchunk - 1)),
                  reads=[sq_b[k], ones_b], writes=[ss_b[h]])
    for h in range(nh):
        sc.op("act", lambda E: E.activation(out=rs[:, h * 512:(h + 1) * 512], in_=ss_ps[h][:, :],
                                            func=AF.Sqrt, scale=1.0 / dim, bias=EPSB[0][:, 0:1]),
              reads=[ss_b[h], EPSB[1]], writes=[rs_b])
    sc.op("dve", lambda E: E.reciprocal(out=rstd[:, :tt], in_=rs[:, :tt]), reads=[rs_b], writes=[rstd_b])
    for c in range(nchunk):
        sc.op("dve", lambda E: E.scalar_tensor_tensor(out=xn[c], in0=xs[c], scalar=gain[:, c:c + 1],
                                                      in1=rstd[:, :tt], op0=ALU.mult, op1=ALU.mult),
              reads=[xsb[c], gain_b, rstd_b], writes=[xnb[c]])


EPSB = [None, None]


def make_consts(sc, nc, es):
    ones = es.enter_context(nc.sbuf_tensor("ones_f", [128, 128], F32))
    ones_b = Buf()
    sc.op("dve", lambda E: E.memset(ones[:, :], 1.0), writes=[ones_b])
    epst = es.enter_context(nc.sbuf_tensor("eps_c", [128, 1], F32))
    eps_b = Buf()
    sc.op("dve", lambda E: E.memset(epst[:, :], EPS), writes=[eps_b])
    EPSB[0] = epst
    EPSB[1] = eps_b
    return ones, ones_b


def build_pf(final_norm=False):
    nc = new_nc()
    xT = nc.dram_tensor("xT", [16, 128, TOK], F32, kind="ExternalInput").ap()
    g = nc.dram_tensor("g", [128, 16], F32, kind="ExternalInput").ap()
    w_in = nc.dram_tensor("w_in", [NFF, 128, 16 * 256], BF16, kind="ExternalInput").ap()
    w_out = nc.dram_tensor("w_out", [16, 128, NFF * 128], BF16, kind="ExternalInput").ap()
    xo = nc.dram_tensor("xo", [16, 128, TOK], F32, kind="ExternalOutput").ap()
    if final_norm:
        gf = nc.dram_tensor("gf", [128, 16], F32, kind="ExternalInput").ap()
    es = contextlib.ExitStack()
    sc = Sched(nc, es)
    sb = lambda name, shape, dt: es.enter_context(nc.sbuf_tensor(name, shape, dt))
    ones, ones_b = make_consts(sc, nc, es)
    xs = sb("xs", [128, 16, TT], F32)
    xs_b = bufs(16)
    xn = sb("xn", [128, 16, TT], BF16)
    xn_b = bufs(16)
    act = sb("act", [128, 22, TT], BF16)
    act_b = bufs(22)
    NW = 3
    win = [sb("win%d" % i, [128, 16 * 256], BF16) for i in range(NW)]
    win_b = bufs(NW)
    NWO = 2
    wout = [sb("wout%d" % i, [128, 22 * 128], BF16) for i in range(NWO)]
    wout_b = bufs(NWO)
    sq = [sb("sq%d" % i, [128, TT], F32) for i in range(2)]
    sq_b = bufs(2)
    rs = sb("rs", [128, TT], F32)
    rs_b = Buf()
    rstd = sb("rstd", [128, TT], F32)
    rstd_b = Buf()
    sg = [sb("sg%d" % i, [128, 512], F32) for i in range(2)]
    sg_b = bufs(2)
    gt = sb("gt", [128, 16], F32)
    gt_b = Buf()
    sc.dma("sp", gt[:, :], g, gt_b, writes=[gt_b])
    if final_norm:
        gft = sb("gft", [128, 16], F32)
        gft_b = Buf()
        sc.dma("sp", gft[:, :], gf, gft_b, writes=[gft_b])
        xo_t = [sb("xot%d" % i, [128, TT], F32) for i in range(2)]
        xo_b = bufs(2)
    ps = [es.enter_context(nc.psum_tensor("ps%d" % i, [128, 512], F32)) for i in range(8)]
    ps_b = bufs(8)

    win_i = 0
    wout_i = 0
    for t in range(TOK // TT):
        tsl = slice(t * TT, (t + 1) * TT)
        for c in range(16):
            sc.dma("sp", xs[:, c, :], xT[c, :, tsl], xs_b[c], writes=[xs_b[c]])
        rms_norm_fm(sc, nc, [xs[:, c, :] for c in range(16)], xs_b, [xn[:, c, :] for c in range(16)], xn_b,
                    gt, gt_b, ones, ones_b, [ps[0], ps[1]], [ps_b[0], ps_b[1]], sq, sq_b, rs, rs_b,
                    rstd, rstd_b, 16, D, TT)
        gu = 0
        for (f0, nf) in FF_HALVES:
            pend = []

            def load_win(f):
                nonlocal win_i
                k = win_i % NW
                win_i += 1
                sc.dma("sp", win[k][:, :], w_in[f], win_b[k], writes=[win_b[k]])
                return k
            pend.append(load_win(f0))
            for fi in range(nf):
                if fi + 1 < nf:
                    pend.append(load_win(f0 + fi + 1))
                k = pend.pop(0)
                for h in range(2):
                    hs = slice(h * 512, (h + 1) * 512)
                    gb = (gu % 2) * 2
                    gu += 1
                    G, U = ps[gb], ps[gb + 1]
                    for kc in range(16):
                        sc.op("pe", lambda E: E.matmul(G[:, :], lhsT=win[k][:, kc * 256:kc * 256 + 128],
                                                       rhs=xn[:, kc, hs], start=(kc == 0), stop=(kc == 15)),
                              reads=[win_b[k], xn_b[kc]], writes=[ps_b[gb]])
                    for kc in range(16):
                        sc.op("pe", lambda E: E.matmul(U[:, :], lhsT=win[k][:, kc * 256 + 128:kc * 256 + 256],
                                                       rhs=xn[:, kc, hs], start=(kc == 0), stop=(kc == 15)),
                              reads=[win_b[k], xn_b[kc]], writes=[ps_b[gb + 1]])
                    j = gu % 2
                    sc.op("act", lambda E: E.activation(out=sg[j][:, :], in_=G[:, :], func=AF.Silu),
                          reads=[ps_b[gb]], writes=[sg_b[j]])
                    sc.op("dve", lambda E: E.tensor_tensor(out=act[:, fi, hs], in0=sg[j][:, :], in1=U[:, :],
                                                           op=ALU.mult),
                          reads=[sg_b[j], ps_b[gb + 1]], writes=[act_b[fi]])
            pend = []

            def load_wout(dmc):
                nonlocal wout_i
                k = wout_i % NWO
                wout_i += 1
                sc.dma("sp", wout[k][:, :nf * 128], w_out[dmc, :, f0 * 128:(f0 + nf) * 128], wout_b[k],
                       writes=[wout_b[k]])
                return k
            pend.append(load_wout(0))
            for dmc in range(16):
                if dmc + 1 < 16:
                    pend.append(load_wout(dmc + 1))
                k = pend.pop(0)
                for h in range(2):
                    hs = slice(h * 512, (h + 1) * 512)
                    yb = 4 + (dmc % 2) * 2 + h
                    Y = ps[yb]
                    for fi in range(nf):
                        sc.op("pe", lambda E: E.matmul(Y[:, :], lhsT=wout[k][:, fi * 128:(fi + 1) * 128],
                                                       rhs=act[:, fi, hs], start=(fi == 0), stop=(fi == nf - 1)),
                              reads=[wout_b[k], act_b[fi]], writes=[ps_b[yb]])
                    sc.op("dve", lambda E: E.scalar_tensor_tensor(out=xs[:, dmc, hs], in0=Y[:, :], scalar=0.5,
                                                                  in1=xs[:, dmc, hs], op0=ALU.mult, op1=ALU.add),
                          reads=[ps_b[yb], xs_b[dmc]], writes=[xs_b[dmc]])
        if not final_norm:
            for c in range(16):
                sc.dma("sp", xo[c, :, tsl], xs[:, c, :], xs_b[c], reads=[xs_b[c]], is_output=True)
        else:
            nh = TT // 512
            for c in range(16):
                k = c % 2
                sc.op("act", lambda E: E.activation(out=sq[k][:, :], in_=xs[:, c, :], func=AF.Square),
                      reads=[xs_b[c]], writes=[sq_b[k]])
                for h in range(nh):
                    sc.op("pe", lambda E: E.matmul(ps[h][:, :], lhsT=ones[:, :], rhs=sq[k][:, h * 512:(h + 1) * 512],
                                                   start=(c == 0), stop=(c == 15)),
                          reads=[sq_b[k], ones_b], writes=[ps_b[h]])
            for h in range(nh):
                sc.op("act", lambda E: E.activation(out=rs[:, h * 512:(h + 1) * 512], in_=ps[h][:, :],
                                                    func=AF.Sqrt, scale=1.0 / D, bias=EPSB[0][:, 0:1]),
                      reads=[ps_b[h], EPSB[1]], writes=[rs_b])
            sc.op("dve", lambda E: E.reciprocal(out=rstd[:, :], in_=rs[:, :]), reads=[rs_b], writes=[rstd_b])
            for c in range(16):
                k = c % 2
                sc.op("dve", lambda E: E.scalar_tensor_tensor(out=xo_t[k][:, :], in0=xs[:, c, :],
                                                              scalar=gft[:, c:c + 1], in1=rstd[:, :],
                                                              op0=ALU.mult, op1=ALU.mult),
                      reads=[xs_b[c], gft_b, rstd_b], writes=[xo_b[k]])
                sc.dma("sp", xo[c, :, tsl], xo_t[k][:, :], xo_b[k], reads=[xo_b[k]], is_output=True)
    sc.finish()
    es.close()
    return nc


PT = 512
N_FM = 28
N_FM_OUT = 20
TM_COLS = 1040


class PsumRing:
    def __init__(self, nc, es, n=8, base=0):
        self.t = [es.enter_context(nc.psum_tensor("pr%d" % (i + base), [128, 512], F32)) for i in range(n)]
        self.b = bufs(n)
        self.i = 0
        self.n = n

    def next(self):
        k = self.i % self.n
        self.i += 1
        return self.t[k], self.b[k]


def build_pp():
    nc = new_nc()
    dt_in = lambda name, shape, dt: nc.dram_tensor(name, shape, dt, kind="ExternalInput").ap()
    dt_out = lambda name, shape, dt: nc.dram_tensor(name, shape, dt, kind="ExternalOutput").ap()
    xT = dt_in("xT", [16, 128, TOK], F32)
    g = dt_in("g", [128, 16], F32)
    gq = dt_in("gq", [128, 4], F32)
    gkv = dt_in("gkv", [128, 4], F32)
    wfm = dt_in("wfm", [N_FM, 128, 16 * 128], BF16)
    wkr = dt_in("wkr", [128, 16 * 128], BF16)
    wtm = dt_in("wtm", [128, 16 * TM_COLS], BF16)
    wqn = dt_in("wqn", [128, 4 * 768], BF16)
    wqr = dt_in("wqr", [128, 4 * 384], BF16)
    wqs = dt_in("wqs", [128, 4 * 384], BF16)
    wkn = dt_in("wkn", [128, 4 * 768], BF16)
    wv = dt_in("wv", [128, 4 * 768], BF16)
    cos2 = dt_in("cos2", [64, TOK], F32)
    sgs = dt_in("sgs", [64, TOK], F32)
    o_qn = dt_out("o_qn", [6, 128, TOK], BF16)
    o_qr = dt_out("o_qr", [6, 64, TOK], BF16)
    o_kn = dt_out("o_kn", [6, 128, TOK], BF16)
    o_kpe = dt_out("o_kpe", [64, TOK], BF16)
    o_fm = dt_out("o_fm", [N_FM_OUT, 128, TOK], BF16)
    o_mv = dt_out("o_mv", [TOK, 6 * 129], BF16)
    o_dv = dt_out("o_dv", [TOK, 6 * 129], BF16)
    o_nvs = dt_out("o_nvs", [TOK, 129], BF16)
    o_nvw = dt_out("o_nvw", [TOK, 129], BF16)
    o_g = dt_out("o_g", [TOK, 12], F32)

    es = contextlib.ExitStack()
    sc = Sched(nc, es)
    sc.store_q = "pool"
    sb = lambda name, shape, dt: es.enter_context(nc.sbuf_tensor(name, shape, dt))
    ones, ones_b = make_consts(sc, nc, es)
    pr = PsumRing(nc, es)

    def const_load(name, src, shape, dt):
        t = sb(name + "_s", shape, dt)
        b = Buf()
        sc.dma("sp", t[:, :], src, b, writes=[b])
        return t, b
    gt, gt_b = const_load("gt", g, [128, 16], F32)
    gqt, gq_b = const_load("gqt", gq, [128, 4], F32)
    gkvt, gkv_b = const_load("gkvt", gkv, [128, 4], F32)
    wtm_t, wtm_b = const_load("wtm_t", wtm, [128, 16 * TM_COLS], BF16)
    wkr_t, wkr_b = const_load("wkr_t", wkr, [128, 16 * 128], BF16)
    wqn_t, wqn_b = const_load("wqn_t", wqn, [128, 4 * 768], BF16)
    wqr_t, wqr_b = const_load("wqr_t", wqr, [128, 4 * 384], BF16)
    wqs_t, wqs_b = const_load("wqs_t", wqs, [128, 4 * 384], BF16)
    wkn_t, wkn_b = const_load("wkn_t", wkn, [128, 4 * 768], BF16)
    wv_t, wv_b = const_load("wv_t", wv, [128, 4 * 768], BF16)

    xs = sb("xs", [128, 16, PT], F32)
    xs_b = bufs(16)
    xn = sb("xn", [128, 16, PT], BF16)
    xn_b = bufs(16)
    sq = [sb("sq%d" % i, [128, PT], F32) for i in range(2)]
    sq_b = bufs(2)
    rs = sb("rs", [128, PT], F32)
    rs_b = Buf()
    rstd = sb("rstd", [128, PT], F32)
    rstd_b = Buf()
    NW = 3
    wch = [sb("wch%d" % i, [128, 16 * 128], BF16) for i in range(NW)]
    wch_b = bufs(NW)
    lat = sb("lat", [128, 8, PT], F32)
    lat_b = bufs(8)
    latn = sb("latn", [128, 8, PT], BF16)
    latn_b = bufs(8)
    NST = 4
    stg = [sb("stg%d" % i, [128, PT], BF16) for i in range(NST)]
    stg_b = bufs(NST)
    stg_i = [0]
    cs_t = sb("cs_t", [64, PT], F32)
    cs_b = Buf()
    sn_t = sb("sn_t", [64, PT], F32)
    sn_b = Buf()
    r1 = sb("r1", [64, PT], F32)
    r1_b = Buf()
    r2 = sb("r2", [64, PT], F32)
    r2_b = Buf()
    vst = [sb("vst%d" % i, [128, 6, 129], BF16) for i in range(2)]
    vst_b = bufs(2)
    nst = [sb("nst%d" % i, [128, 2, 129], BF16) for i in range(2)]
    nst_b = bufs(2)
    gst = [sb("gst%d" % i, [128, 12], F32) for i in range(2)]
    gst_b = bufs(2)
    for i in range(2):
        sc.op("dve", lambda E: E.memset(vst[i][:, :, :], 1.0), writes=[vst_b[i]])
        sc.op("dve", lambda E: E.memset(nst[i][:, :, :], 1.0), writes=[nst_b[i]])

    def stage_out(ps, ps_b, dst, npart=128, eng="act"):
        k = stg_i[0] % NST
        stg_i[0] += 1
        if eng == "act":
            sc.op("act", lambda E: E.activation(out=stg[k][:npart, :], in_=ps[:npart, :], func=AF.Copy),
                  reads=[ps_b], writes=[stg_b[k]])
        else:
            sc.op("dve", lambda E: E.tensor_copy(out=stg[k][:npart, :], in_=ps[:npart, :]),
                  reads=[ps_b], writes=[stg_b[k]])
        sc.dma("sp", dst, stg[k][:npart, :], stg_b[k], reads=[stg_b[k]], is_output=True)

    def rope_out(psA, psA_b, psB, psB_b, dst):
        sc.op("dve", lambda E: E.tensor_tensor(out=r1[:, :], in0=psA[:64, :], in1=cs_t[:, :], op=ALU.mult),
              reads=[psA_b, cs_b], writes=[r1_b])
        sc.op("dve", lambda E: E.tensor_tensor(out=r2[:, :], in0=psB[:64, :], in1=sn_t[:, :], op=ALU.mult),
              reads=[psB_b, sn_b], writes=[r2_b])
        k = stg_i[0] % NST
        stg_i[0] += 1
        sc.op("dve", lambda E: E.tensor_tensor(out=stg[k][:64, :], in0=r1[:, :], in1=r2[:, :], op=ALU.add),
              reads=[r1_b, r2_b], writes=[stg_b[k]])
        sc.dma("sp", dst, stg[k][:64, :], stg_b[k], reads=[stg_b[k]], is_output=True)

    wi = 0
    vi = 0
    for t in range(TOK // PT):
        tsl = slice(t * PT, (t + 1) * PT)
        for c in range(16):
            sc.dma("sp", xs[:, c, :], xT[c, :, tsl], xs_b[c], writes=[xs_b[c]])
        sc.dma("sp", cs_t[:, :], cos2[:, tsl], cs_b, writes=[cs_b])
        sc.dma("sp", sn_t[:, :], sgs[:, tsl], sn_b, writes=[sn_b])
        ps0, ps0_b = pr.next()
        rms_norm_fm(sc, nc, [xs[:, c, :] for c in range(16)], xs_b, [xn[:, c, :] for c in range(16)], xn_b,
                    gt, gt_b, ones, ones_b, [ps0], [ps0_b], sq, sq_b, rs, rs_b, rstd, rstd_b, 16, D, PT)
        pend = []

        def load_w(j):
            nonlocal wi
            k = wi % NW
            wi += 1
            sc.dma("sp", wch[k][:, :], wfm[j], wch_b[k], writes=[wch_b[k]])
            return k
        pend.append(load_w(0))
        for j in range(N_FM):
            if j + 1 < N_FM:
                pend.append(load_w(j + 1))
            k = pend.pop(0)
            ps, ps_b = pr.next()
            for kc in range(16):
                sc.op("pe", lambda E: E.matmul(ps[:, :], lhsT=wch[k][:, kc * 128:(kc + 1) * 128], rhs=xn[:, kc, :],
                                               start=(kc == 0), stop=(kc == 15)),
                      reads=[wch_b[k], xn_b[kc]], writes=[ps_b])
            if j < 8:
                sc.op("act", lambda E: E.activation(out=lat[:, j, :], in_=ps[:, :], func=AF.Copy),
                      reads=[ps_b], writes=[lat_b[j]])
            else:
                stage_out(ps, ps_b, o_fm[j - 8, :, tsl], eng=("act" if j % 2 else "dve"))
        psA, psA_b = pr.next()
        psB, psB_b = pr.next()
        for kc in range(16):
            sc.op("pe", lambda E: E.matmul(psA[:64, :], lhsT=wkr_t[:, kc * 128:kc * 128 + 64], rhs=xn[:, kc, :],
                                           start=(kc == 0), stop=(kc == 15)),
                  reads=[wkr_b, xn_b[kc]], writes=[psA_b])
        for kc in range(16):
            sc.op("pe", lambda E: E.matmul(psB[:64, :], lhsT=wkr_t[:, kc * 128 + 64:kc * 128 + 128], rhs=xn[:, kc, :],
                                           start=(kc == 0), stop=(kc == 15)),
                  reads=[wkr_b, xn_b[kc]], writes=[psB_b])
        rope_out(psA, psA_b, psB, psB_b, o_kpe[:, tsl])
        for sub in range(PT // 128):
            ssl = slice(sub * 128, (sub + 1) * 128)
            rows = slice(t * PT + sub * 128, t * PT + (sub + 1) * 128)
            k2 = vi % 2
            vi += 1
            pa, pa_b = pr.next()
            pb, pb_b = pr.next()
            pc, pc_b = pr.next()
            for (pp_, pp_b, c0, ncol) in ((pa, pa_b, 0, 512), (pb, pb_b, 512, 512), (pc, pc_b, 1024, 12)):
                for kc in range(16):
                    sc.op("pe", lambda E: E.matmul(pp_[:, :ncol], lhsT=xn[:, kc, ssl],
                                                   rhs=wtm_t[:, kc * TM_COLS + c0:kc * TM_COLS + c0 + ncol],
                                                   start=(kc == 0), stop=(kc == 15)),
                          reads=[wtm_b, xn_b[kc]], writes=[pp_b])
            sc.op("act", lambda E: E.activation(out=vst[k2][:, 0:4, 0:128],
                                                in_=pa[:, :].rearrange("p (h d) -> p h d", h=4), func=AF.Copy),
                  reads=[pa_b], writes=[vst_b[k2]])
            sc.op("dve", lambda E: E.tensor_copy(out=vst[k2][:, 4:6, 0:128],
                                                 in_=pb[:, 0:256].rearrange("p (h d) -> p h d", h=2)),
                  reads=[pb_b], writes=[vst_b[k2]])
            sc.op("dve", lambda E: E.tensor_copy(out=nst[k2][:, :, 0:128],
                                                 in_=pb[:, 256:512].rearrange("p (h d) -> p h d", h=2)),
                  reads=[pb_b], writes=[nst_b[k2]])
            sc.op("act", lambda E: E.activation(out=gst[k2][:, :], in_=pc[:, 0:12], func=AF.Sigmoid),
                  reads=[pc_b], writes=[gst_b[k2]])
            sc.dma("sp", o_dv[rows, :], vst[k2][:, :, :].rearrange("p h d -> p (h d)"), vst_b[k2],
                   reads=[vst_b[k2]], is_output=True)
            sc.dma("sp", o_nvs[rows, :], nst[k2][:, 0, :], nst_b[k2], reads=[nst_b[k2]], is_output=True)
            sc.dma("sp", o_nvw[rows, :], nst[k2][:, 1, :], nst_b[k2], reads=[nst_b[k2]], is_output=True)
            sc.dma("sp", o_g[rows, :], gst[k2][:, :], gst_b[k2], reads=[gst_b[k2]], is_output=True)
        ps1, ps1_b = pr.next()
        rms_norm_fm(sc, nc, [lat[:, c, :] for c in range(4)], lat_b[0:4], [latn[:, c, :] for c in range(4)],
                    latn_b[0:4], gqt, gq_b, ones, ones_b, [ps1], [ps1_b], sq, sq_b, rs, rs_b, rstd, rstd_b,
                    4, 512, PT)
        ps2, ps2_b = pr.next()
        rms_norm_fm(sc, nc, [lat[:, 4 + c, :] for c in range(4)], lat_b[4:8],
                    [latn[:, 4 + c, :] for c in range(4)], latn_b[4:8], gkvt, gkv_b, ones, ones_b, [ps2], [ps2_b],
                    sq, sq_b, rs, rs_b, rstd, rstd_b, 4, 512, PT)
        for h in range(6):
            ps, ps_b = pr.next()
            for kc in range(4):
                sc.op("pe", lambda E: E.matmul(ps[:, :], lhsT=wqn_t[:, kc * 768 + h * 128:kc * 768 + (h + 1) * 128],
                                               rhs=latn[:, kc, :], start=(kc == 0), stop=(kc == 3)),
                      reads=[wqn_b, latn_b[kc]], writes=[ps_b])
            stage_out(ps, ps_b, o_qn[h, :, tsl], eng="act")
            psA, psA_b = pr.next()
            psB, psB_b = pr.next()
            for kc in range(4):
                sc.op("pe", lambda E: E.matmul(psA[:64, :], lhsT=wqr_t[:, kc * 384 + h * 64:kc * 384 + (h + 1) * 64],
                                               rhs=latn[:, kc, :], start=(kc == 0), stop=(kc == 3)),
                      reads=[wqr_b, latn_b[kc]], writes=[psA_b])
            for kc in range(4):
                sc.op("pe", lambda E: E.matmul(psB[:64, :], lhsT=wqs_t[:, kc * 384 + h * 64:kc * 384 + (h + 1) * 64],
                                               rhs=latn[:, kc, :], start=(kc == 0), stop=(kc == 3)),
                      reads=[wqs_b, latn_b[kc]], writes=[psB_b])
            rope_out(psA, psA_b, psB, psB_b, o_qr[h, :, tsl])
            ps, ps_b = pr.next()
            for kc in range(4):
                sc.op("pe", lambda E: E.matmul(ps[:, :], lhsT=wkn_t[:, kc * 768 + h * 128:kc * 768 + (h + 1) * 128],
                                               rhs=latn[:, 4 + kc, :], start=(kc == 0), stop=(kc == 3)),
                      reads=[wkn_b, latn_b[4 + kc]], writes=[ps_b])
            stage_out(ps, ps_b, o_kn[h, :, tsl], eng="dve")
        for sub in range(PT // 128):
            ssl = slice(sub * 128, (sub + 1) * 128)
            rows = slice(t * PT + sub * 128, t * PT + (sub + 1) * 128)
            k2 = vi % 2
            vi += 1
            pa, pa_b = pr.next()
            pb, pb_b = pr.next()
            for (pp_, pp_b, c0, ncol) in ((pa, pa_b, 0, 512), (pb, pb_b, 512, 256)):
                for kc in range(4):
                    sc.op("pe", lambda E: E.matmul(pp_[:, :ncol], lhsT=latn[:, 4 + kc, ssl],
                                                   rhs=wv_t[:, kc * 768 + c0:kc * 768 + c0 + ncol],
                                                   start=(kc == 0), stop=(kc == 3)),
                          reads=[wv_b, latn_b[4 + kc]], writes=[pp_b])
            sc.op("act", lambda E: E.activation(out=vst[k2][:, 0:4, 0:128],
                                                in_=pa[:, :].rearrange("p (h d) -> p h d", h=4), func=AF.Copy),
                  reads=[pa_b], writes=[vst_b[k2]])
            sc.op("dve", lambda E: E.tensor_copy(out=vst[k2][:, 4:6, 0:128],
                                                 in_=pb[:, 0:256].rearrange("p (h d) -> p h d", h=2)),
                  reads=[pb_b], writes=[vst_b[k2]])
            sc.dma("sp", o_mv[rows, :], vst[k2][:, :, :].rearrange("p h d -> p (h d)"), vst_b[k2],
                   reads=[vst_b[k2]], is_output=True)
    sc.finish()
    es.close()
    return nc


def lay_gain(g):
    n = g.shape[0] // 128
    return np.ascontiguousarray(g.reshape(n, 128).T)


def lay_k(w):
    kc = w.shape[0] // 128
    return np.ascontiguousarray(w.reshape(kc, 128, -1).transpose(1, 0, 2)).reshape(128, -1)


def lay_w_in(w):
    wg = w[:, :DFF].reshape(16, 128, NFF, 128)
    wu = w[:, DFF:].reshape(16, 128, NFF, 128)
    a = np.stack([wg, wu], axis=3)
    return np.ascontiguousarray(a.transpose(2, 1, 0, 3, 4)).reshape(NFF, 128, 16 * 256)


def lay_w_out(w):
    a = w.reshape(NFF, 128, 16, 128)
    return np.ascontiguousarray(a.transpose(2, 1, 0, 3)).reshape(16, 128, NFF * 128)


FM_STARTS = ([0, 128, 256, 384] + [512 + 128 * i for i in range(4)] + [1088 + 128 * i for i in range(6)]
             + [1856 + 128 * i for i in range(6)] + [3392 + 128 * i for i in range(4)] + [3904, 4032, 4160, 4416])
SWAP64 = np.array([(i + 32) % 64 for i in range(64)])


def lay_pp_weights(w_mix_in, w_uq, w_ukv):
    o = {}
    o["wfm"] = np.stack([lay_k(w_mix_in[:, c:c + 128]) for c in FM_STARTS])
    kr = w_mix_in[:, 1024:1088]
    o["wkr"] = lay_k(np.concatenate([kr, kr[:, SWAP64]], axis=1))
    pad = np.zeros((D, 4), dtype=w_mix_in.dtype)
    o["wtm"] = lay_k(np.concatenate([w_mix_in[:, 2624:3392], w_mix_in[:, 4288:4416], w_mix_in[:, 4544:4672],
                                     w_mix_in[:, 4672:4684], pad], axis=1))
    uq = w_uq.reshape(512, 6, 192)
    o["wqn"] = lay_k(uq[:, :, :128].reshape(512, 768))
    o["wqr"] = lay_k(uq[:, :, 128:].reshape(512, 384))
    o["wqs"] = lay_k(uq[:, :, 128:][:, :, SWAP64].reshape(512, 384))
    ukv = w_ukv.reshape(512, 6, 256)
    o["wkn"] = lay_k(ukv[:, :, :128].reshape(512, 768))
    o["wv"] = lay_k(ukv[:, :, 128:].reshape(512, 768))
    return o


def core_positions(r):
    i = np.arange(8)[:, None]
    u = np.arange(512)[None, :]
    return (512 * (4 * i + r) + u).reshape(-1)


def rope_tables(pos):
    inv = (np.float32(10000.0) ** (-np.arange(32, dtype=np.float32) / np.float32(32))).astype(np.float32)
    ang = pos.astype(np.float32)[None, :] * inv[:, None]
    c, s = np.cos(ang).astype(np.float32), np.sin(ang).astype(np.float32)
    return np.concatenate([c, c], 0), np.concatenate([-s, s], 0)


NEG = -30000.0
SLAB = 128 * 129
TABW = 2944
MLA_SCALE = 192.0 ** -0.5
HD_SCALE = 128.0 ** -0.5


def att_block(sc, steps, sbanks, pTs, accs, nv):
    n = len(steps)
    ns, npt = len(sbanks), len(pTs)

    def emit_S(k):
        st = steps[k]
        bank, bb = sbanks[k % ns]
        m = len(st["mm"])
        for idx, (l, r, rb) in enumerate(st["mm"]):
            sc.op("pe", lambda E: E.matmul(bank[:, :], lhsT=l, rhs=r, start=(idx == 0), stop=(idx == m - 1)),
                  reads=rb, writes=[bb])

    emit_S(0)
    if n > 1:
        emit_S(1)
    for k in range(n):
        if k + 2 < n:
            emit_S(k + 2)
        st = steps[k]
        bank, bb = sbanks[k % ns]
        pt, pb = pTs[k % npt]
        if st["bias"] is None:
            sc.op("act", lambda E: E.activation(out=pt[:, :], in_=bank[:, :], func=AF.Exp, scale=st["scale"]),
                  reads=[bb], writes=[pb])
        else:
            bap, bbuf = st["bias"]
            sc.op("act", lambda E: E.activation(out=pt[:, :], in_=bank[:, :], func=AF.Exp, scale=st["scale"],
                                                bias=bap),
                  reads=[bb, bbuf], writes=[pb])
        vap, vb = st["v"]
        for qs in range(4):
            acc, ab = accs[qs]
            sc.op("pe", lambda E: E.matmul(acc[:, :nv], lhsT=pt[:, qs * 128:(qs + 1) * 128], rhs=vap,
                                           start=(k == 0), stop=(k == n - 1)),
                  reads=[pb] + vb, writes=[ab])


def att_block_T(sc, steps, sbanks, pTs, accO, accPs):
    n = len(steps)
    ns, npt, na = len(sbanks), len(pTs), len(accPs)
    acc, ab = accO

    def emit_S(k):
        st = steps[k]
        bank, bb = sbanks[k % ns]
        m = len(st["mm"])
        for idx, (l, r, rb) in enumerate(st["mm"]):
            sc.op("pe", lambda E: E.matmul(bank[:, :], lhsT=l, rhs=r, start=(idx == 0), stop=(idx == m - 1)),
                  reads=rb, writes=[bb])

    emit_S(0)
    if n > 1:
        emit_S(1)
    for k in range(n):
        if k + 2 < n:
            emit_S(k + 2)
        st = steps[k]
        bank, bb = sbanks[k % ns]
        pt, pb = pTs[k % npt]
        if st["bias"] is None:
            sc.op("act", lambda E: E.activation(out=pt[:, :], in_=bank[:, :], func=AF.Exp, scale=st["scale"]),
                  reads=[bb], writes=[pb])
        else:
            bap, bbuf = st["bias"]
            sc.op("act", lambda E: E.activation(out=pt[:, :], in_=bank[:, :], func=AF.Exp, scale=st["scale"],
                                                bias=bap),
                  reads=[bb, bbuf], writes=[pb])
        vap, vb = st["v"]
        sc.op("pe", lambda E: E.matmul(acc[:, :], lhsT=vap, rhs=pt[:, :], start=(k == 0), stop=(k == n - 1)),
              reads=[pb] + vb, writes=[ab])
        ap_, apb, aeng = accPs[k % na]
        if k < na:
            sc.op(aeng, lambda E: E.tensor_copy(out=ap_[:, :], in_=pt[:, :]), reads=[pb], writes=[apb])
        else:
            sc.op(aeng, lambda E: E.tensor_tensor(out=ap_[:, :], in0=pt[:, :], in1=ap_[:, :], op=ALU.add),
                  reads=[pb, apb], writes=[apb])


def finish_block_T(sc, accO, accPs, nused, denB, ones, ones_b, sumP, sumP_b, rec, rec_b):
    first = True
    for j in range(min(nused, len(accPs))):
        ap_, apb, _ = accPs[j]
        if first:
            src, src_b = ap_, apb
            first = False
        else:
            sc.op("dve", lambda E: E.tensor_tensor(out=sumP[:, :], in0=src[:, :], in1=ap_[:, :], op=ALU.add),
                  reads=[src_b, apb], writes=[sumP_b])
            src, src_b = sumP, sumP_b
    db, db_b = denB
    sc.op("pe", lambda E: E.matmul(db[:, :], lhsT=ones[:, :], rhs=src[:, :], start=True, stop=True),
          reads=[ones_b, src_b], writes=[db_b])
    sc.op("dve", lambda E: E.reciprocal(out=rec[:, :], in_=db[:, :]), reads=[db_b], writes=[rec_b])


def build_p2(do_mla=True, do_dil=True, do_nsa=True, nblk=8):
    nc = new_nc()
    dt_in = lambda name, shape, dt: nc.dram_tensor(name, shape, dt, kind="ExternalInput").ap()
    dt_out = lambda name, shape, dt: nc.dram_tensor(name, shape, dt, kind="ExternalOutput").ap()
    m_qn = dt_in("m_qn", [6, 128, TOK], BF16)
    m_qr = dt_in("m_qr", [6, 64, TOK], BF16)
    m_kn = dt_in("m_kn", [6, 128, S], BF16)
    m_kpe = dt_in("m_kpe", [64, S], BF16)
    m_v = dt_in("m_v", [6, 128, SLAB], BF16)
    tc_d = dt_in("tc", [128, TABW], BF16)
    tw_d = dt_in("tw", [128, TABW], BF16)
    ident_d = dt_in("ident", [128, 128], BF16)
    o_mla = dt_out("o_mla", [6, 128, TOK], BF16)
    d_q = dt_in("d_q", [6, 128, 32 * 128], BF16)
    d_k = dt_in("d_k", [6, 128, 33 * 128], BF16)
    d_v = dt_in("d_v", [6, 128, 33 * 129], BF16)
    d_bias = dt_in("d_bias", [128, 8 * 256], F32)
    o_dil = dt_out("o_dil", [6, 32 * 128, 129], F32)
    n_q = dt_in("n_q", [4, 128, TOK], BF16)
    n_g = dt_in("n_g", [128, 32 * 12], F32)
    n_kc = dt_in("n_kc", [128, S], BF16)
    n_vc = dt_in("n_vc", [128, S], BF16)
    n_ks = dt_in("n_ks", [128, S], BF16)
    n_vs = dt_in("n_vs", [128, SLAB], BF16)
    n_kw = dt_in("n_kw", [128, S], BF16)
    n_vw = dt_in("n_vw", [128, SLAB], BF16)
    w1k_d = dt_in("w1k", [128, 32 * 128], BF16)
    w1v_d = dt_in("w1v", [128, 32 * 128], BF16)
    w2k_d = dt_in("w2k", [128, 128], BF16)
    w2v_d = dt_in("w2v", [128, 128], BF16)
    posT_d = dt_in("posT", [128, 32], BF16)
    cmpsel_d = dt_in("cmpsel", [128, 8, 257], BF16)
    ewide_d = dt_in("ewide", [128, 8192], BF16)
    cmask_d = dt_in("cmask", [128, 512], BF16)
    cmaskp_d = dt_in("cmaskp", [128, 512], BF16)
    cap_d = dt_in("cap", [128, 512], F32)
    floor_d = dt_in("floor", [128, 512], F32)
    bcmp_d = dt_in("bcmp", [128, 32], F32)
    bslc_d = dt_in("bslc", [128, 512], F32)
    bwin_d = dt_in("bwin", [128, 80], F32)
    o_nsa = dt_out("o_nsa", [TOK, 512], BF16)

    es = contextlib.ExitStack()
    sc = Sched(nc, es)
    sc.store_q = "pool"
    sb = lambda name, shape, dt: es.enter_context(nc.sbuf_tensor(name, shape, dt))

    def const_load(name, src, shape, dt):
        t = sb(name + "_s", shape, dt)
        b = Buf()
        sc.dma("sp", t[:, :], src, b, writes=[b])
        return t, b

    A0 = sb("A0", [128, SLAB], BF16)
    A1 = sb("A1", [128, SLAB], BF16)
    BS = sb("BS", [128, S], BF16)
    A0_b, A1_b, BS_b = Buf(), Buf(), Buf()
    A0c, A1c = [], []
    tc_t, tc_b = const_load("tc_t", tc_d, [128, TABW], BF16)
    tw_t, tw_b = const_load("tw_t", tw_d, [128, TABW], BF16)
    ident, ident_b = const_load("ident_t", ident_d, [128, 128], BF16)
    sbank = [(es.enter_context(nc.psum_tensor("sbk%d" % i, [128, 512], F32)), Buf()) for i in range(3)]
    accs = [(es.enter_context(nc.psum_tensor("acc%d" % i, [128, 512], F32)), Buf()) for i in range(4)]
    tp_ps = es.enter_context(nc.psum_tensor("tp_ps", [128, 128], BF16))
    tp_b = Buf()
    pTs = [(sb("pT%d" % i, [128, 512], BF16), Buf()) for i in range(3)]
    ost = [sb("ost%d" % i, [128, 512], BF16) for i in range(2)]
    ost_b = bufs(2)
    rd = [sb("rd%d" % i, [128, 1], F32) for i in range(4)]
    rd_b = bufs(4)
    rdi = [0]

    def next_rd():
        k = rdi[0] % 4
        rdi[0] += 1
        return rd[k], rd_b[k]

    if do_mla:
        onesf = sb("onesf", [128, 128], F32)
        onesf_b = Buf()
        sc.op("dve", lambda E: E.memset(onesf[:, :], 1.0), writes=[onesf_b])
        accP_t = [sb("accP%d" % i, [128, 512], F32) for i in range(2)]
        accPs = [(accP_t[0], Buf(), "dve"), (accP_t[1], Buf(), "dve")]
        sumP = sb("sumP", [128, 512], F32)
        sumP_b = Buf()
        recT = sb("recT", [128, 512], F32)
        recT_b = Buf()
        mblk = [0]
        A0c, A1c = bufs(8), bufs(8)
        qn_t = [sb("qn_t%d" % i, [128, 512], BF16) for i in range(2)]
        qn_b = bufs(2)
        qr_t = [sb("qr_t%d" % i, [128, 512], BF16) for i in range(2)]
        qr_b = bufs(2)
        for i in range(2):
            sc.op("dve", lambda E: E.memset(qr_t[i][64:128, :], 0.0), writes=[qr_b[i]])
        sc.op("dve", lambda E: E.memset(BS[64:128, :], 0.0), writes=[BS_b])
        sc.dma("sp", BS[0:64, :], m_kpe, BS_b, writes=[BS_b])
        qi = 0
        oi = 0
        for h in range(6):
            for c8 in range(8):
                sc.dma("sp", A0[:, c8 * 2048:(c8 + 1) * 2048], m_kn[h, :, c8 * 2048:(c8 + 1) * 2048], A0c[c8],
                       writes=[A0c[c8]])
                sc.dma("sp", A1[:, c8 * 2064:(c8 + 1) * 2064], m_v[h, :, c8 * 2064:(c8 + 1) * 2064], A1c[c8],
                       writes=[A1c[c8]])
            for i in range(nblk):
                k = qi % 2
                qi += 1
                qsl = slice(i * 512, (i + 1) * 512)
                sc.dma("sp", qn_t[k][:, :], m_qn[h, :, qsl], qn_b[k], writes=[qn_b[k]])
                sc.dma("sp", qr_t[k][0:64, :], m_qr[h, :, qsl], qr_b[k], writes=[qr_b[k]])
                steps = []
                for kt in range(16 * i + 16):
                    ks = slice(kt * 128, (kt + 1) * 128)
                    mm = [(A0[:, ks], qn_t[k][:, :], [A0c[kt // 16], qn_b[k]]),
                          (BS[:, ks], qr_t[k][:, :], [BS_b, qr_b[k]])]
                    if kt >= 16 * i:
                        off = 1920 - 128 * (kt - 16 * i)
                        mm.append((ident[:, :], tc_t[:, off:off + 512], [ident_b, tc_b]))
                    steps.append(dict(mm=mm, scale=MLA_SCALE, bias=None,
                                      v=(A1[:, kt * 129:kt * 129 + 128], [A1c[kt // 16]])))
                bi_ = mblk[0] % 2
                mblk[0] += 1
                att_block_T(sc, steps, sbank + [accs[3]], pTs, accs[bi_], accPs)
                finish_block_T(sc, accs[bi_], accPs, len(steps), accs[2], onesf, onesf_b, sumP, sumP_b, recT, recT_b)
                o = oi % 2
                oi += 1
                accO_t, accO_b = accs[bi_]
                sc.op("dve", lambda E: E.tensor_tensor(out=ost[o][:, :], in0=accO_t[:, :], in1=recT[:, :],
                                                       op=ALU.mult),
                      reads=[accO_b, recT_b], writes=[ost_b[o]])
                sc.dma("sp", o_mla[h, :, qsl], ost[o][:, :], ost_b[o], reads=[ost_b[o]], is_output=True)

    if do_dil:
        dbias, dbias_b = const_load("dbias", d_bias, [128, 8 * 256], F32)
        dsT = [sb("dsT%d" % i, [128, 256], F32) for i in range(2)]
        dsT_b = bufs(2)
        dpT = [sb("dpT%d" % i, [128, 256], BF16) for i in range(2)]
        dpT_b = bufs(2)
        dst = [sb("dst%d" % i, [128, 129], F32) for i in range(2)]
        dst_b = bufs(2)
        dq_t = A0[:, 0:4096]
        dk_t = A0[:, 4096:4096 + 33 * 128]
        dv_t = A1[:, 0:33 * 129]
        di = 0
        for hd in range(6):
            g = hd // 2
            sc.dma("sp", dq_t, d_q[hd], A0_b, writes=[A0_b] + A0c)
            sc.dma("sp", dk_t, d_k[hd], A0_b, writes=[A0_b] + A0c)
            sc.dma("sp", dv_t, d_v[hd], A1_b, writes=[A1_b] + A1c)
            for n in range(32):
                if g == 0:
                    seq_start = False
                    tab = hd if n == 0 else 2 + hd
                else:
                    seq_start = (n == 0) if g == 1 else (n % 8 == 0)
                    tab = 2 + hd
                k = di % 2
                di += 1
                qap = dq_t[:, n * 128:(n + 1) * 128]
                bA, bA_b = sbank[(2 * di) % 3]
                bB, bB_b = sbank[(2 * di + 1) % 3]
                if not seq_start:
                    sc.op("pe", lambda E: E.matmul(bA[:, 0:128], lhsT=dk_t[:, n * 128:(n + 1) * 128], rhs=qap,
                                                   start=True, stop=True), reads=[A0_b], writes=[bA_b])
                sc.op("pe", lambda E: E.matmul(bB[:, 0:128], lhsT=dk_t[:, (n + 1) * 128:(n + 2) * 128], rhs=qap,
                                               start=True, stop=True), reads=[A0_b], writes=[bB_b])
                c0 = 0 if not seq_start else 128
                if not seq_start:
                    sc.op("dve", lambda E: E.scalar_tensor_tensor(out=dsT[k][:, 0:128], in0=bA[:, 0:128],
                                                                  scalar=HD_SCALE,
                                                                  in1=dbias[:, tab * 256:tab * 256 + 128],
                                                                  op0=ALU.mult, op1=ALU.add),
                          reads=[bA_b, dbias_b], writes=[dsT_b[k]])
                sc.op("dve", lambda E: E.scalar_tensor_tensor(out=dsT[k][:, 128:256], in0=bB[:, 0:128],
                                                              scalar=HD_SCALE,
                                                              in1=dbias[:, tab * 256 + 128:tab * 256 + 256],
                                                              op0=ALU.mult, op1=ALU.add),
                      reads=[bB_b, dbias_b], writes=[dsT_b[k]])
                sc.op("act", lambda E: E.activation(out=dpT[k][:, c0:256], in_=dsT[k][:, c0:256], func=AF.Exp),
                      reads=[dsT_b[k]], writes=[dpT_b[k]])
                acc, ab = accs[di % 4]
                if not seq_start:
                    sc.op("pe", lambda E: E.matmul(acc[:, :129], lhsT=dpT[k][:, 0:128],
                                                   rhs=dv_t[:, n * 129:(n + 1) * 129], start=True, stop=False),
                          reads=[dpT_b[k], A1_b], writes=[ab])
                sc.op("pe", lambda E: E.matmul(acc[:, :129], lhsT=dpT[k][:, 128:256],
                                               rhs=dv_t[:, (n + 1) * 129:(n + 2) * 129], start=seq_start, stop=True),
                      reads=[dpT_b[k], A1_b], writes=[ab])
                sc.op("act", lambda E: E.activation(out=dst[k][:, :], in_=acc[:, :129], func=AF.Copy),
                      reads=[ab], writes=[dst_b[k]])
                sc.dma("sp", o_dil[hd, n * 128:(n + 1) * 128, :], dst[k][:, :], dst_b[k], reads=[dst_b[k]],
                       is_output=True)

    if do_nsa:
        cmask, cmask_b = const_load("cmask", cmask_d, [128, 512], BF16)
        cmaskp, cmaskp_b = const_load("cmaskp", cmaskp_d, [128, 512], BF16)
        capt, cap_b = const_load("capt", cap_d, [128, 512], F32)
        floort, floor_b = const_load("floort", floor_d, [128, 512], F32)
        bcmp, bcmp_b = const_load("bcmp", bcmp_d, [128, 32], F32)
        bslc, bslc_b = const_load("bslc", bslc_d, [128, 512], F32)
        bwin, bwin_b = const_load("bwin", bwin_d, [128, 80], F32)
        g_sb, g_b = const_load("g_sb", n_g, [128, 32 * 12], F32)
        w2k, w2k_b = const_load("w2k_t", w2k_d, [128, 128], BF16)
        w2v, w2v_b = const_load("w2v_t", w2v_d, [128, 128], BF16)
        posT, posT_b = const_load("posT_t", posT_d, [128, 32], BF16)
        sc.dma("sp", BS[:, 0:8192], ewide_d, BS_b, writes=[BS_b])
        sc.dma("sp", BS[:, 8192:12288], w1k_d, BS_b, writes=[BS_b])
        sc.dma("sp", BS[:, 12288:16384], w1v_d, BS_b, writes=[BS_b])
        ewide = BS[:, 0:8192]
        vcx = sb("vcx", [128, 8, 385], BF16)
        vcx_b = Buf()
        sc.dma("sp", vcx[:, :, 128:385], cmpsel_d, vcx_b, writes=[vcx_b])
        kcT = sb("kcT", [128, 1024], BF16)
        kcT_b = Buf()
        hT = [sb("hT%d" % i, [128, 1024], BF16) for i in range(2)]
        hT_b = bufs(2)
        bcol = [sb("bcol%d" % i, [128, 1], F32) for i in range(2)]
        bcol_b = bufs(2)
        sc.dma("sp", A0[:, :S], n_kc, A0_b, writes=[A0_b] + A0c)
        sc.dma("sp", A1[:, :S], n_vc, A1_b, writes=[A1_b] + A1c)
        for w, (src, src_b) in enumerate(((A0, A0_b), (A1, A1_b))):
            w1 = BS[:, 8192 + w * 4096:8192 + (w + 1) * 4096]
            sc.op("dve", lambda E: E.memset(hT[w][:, :], 0.0), writes=[hT_b[w]])
            bk, bkb = sbank[2]
            for l in range(32):
                sc.op("pe", lambda E: E.matmul(bk[:, 0:1], lhsT=w1[:, l * 128:(l + 1) * 128], rhs=posT[:, l:l + 1],
                                               start=(l == 0), stop=(l == 31)),
                      reads=[BS_b, posT_b], writes=[bkb])
            sc.op("dve", lambda E: E.tensor_copy(out=bcol[w][:, :], in_=bk[:, 0:1]), reads=[bkb], writes=[bcol_b[w]])
            for c2 in range(2):
                ncol = 512 if c2 == 0 else 511
                bank, bb = sbank[c2]
                for l in range(32):
                    st0 = 16 * 512 * c2 + l
                    sc.op("pe", lambda E: E.matmul(bank[:, :ncol], lhsT=w1[:, l * 128:(l + 1) * 128],
                                                   rhs=src[:, st0:st0 + 16 * ncol:16],
                                                   start=(l == 0), stop=(l == 31)),
                          reads=[BS_b, src_b], writes=[bb])
                sc.op("act", lambda E: E.activation(out=hT[w][:, c2 * 512:c2 * 512 + ncol], in_=bank[:, :ncol],
                                                    func=AF.Silu, bias=bcol[w][:, 0:1]),
                      reads=[bb, bcol_b[w]], writes=[hT_b[w]])
        for c2 in range(2):
            bank, bb = sbank[c2]
            sc.op("pe", lambda E: E.matmul(bank[:, :], lhsT=w2k[:, :], rhs=hT[0][:, c2 * 512:(c2 + 1) * 512],
                                           start=True, stop=True), reads=[w2k_b, hT_b[0]], writes=[bb])
            sc.op("dve", lambda E: E.tensor_copy(out=kcT[:, c2 * 512:(c2 + 1) * 512], in_=bank[:, :]),
                  reads=[bb], writes=[kcT_b])
        for ct in range(8):
            bank, bb = sbank[ct % 3]
            sc.op("pe", lambda E: E.matmul(bank[:, 0:128], lhsT=hT[1][:, ct * 128:(ct + 1) * 128], rhs=w2v[:, :],
                                           start=True, stop=True), reads=[w2v_b, hT_b[1]], writes=[bb])
            sc.op("dve", lambda E: E.tensor_copy(out=vcx[:, ct, 0:128], in_=bank[:, 0:128]),
                  reads=[bb], writes=[vcx_b])
        sc.dma("sp", A0[:, :S], n_ks, A0_b, writes=[A0_b])
        sc.dma("sp", A1[:, :], n_vs, A1_b, writes=[A1_b])
        nq_t = sb("nq_t", [128, 4, 512], BF16)
        nq_b = Buf()
        kw_t = sb("kw_t", [128, 20 * 128], BF16)
        vw_t = sb("vw_t", [128, 20 * 129], BF16)
        kw_b, vw_b = Buf(), Buf()
        score = sb("score", [128, 4, 256], F32)
        score_b = bufs(4)
        s2 = sb("s2", [128, 256], F32)
        s2_b = Buf()
        work = sb("work", [128, 256], F32)
        work_b = Buf()
        m8 = sb("m8", [128, 16], F32)
        m8_b = Buf()
        negsel = sb("negsel", [128, 256], BF16)
        negsel_b = Buf()
        negselT = sb("negselT", [128, 2, 512], BF16)
        negselT_b = Buf()
        nso = sb("nso", [128, 4, 512], F32)
        nso_b = bufs(4)
        cf = [sb("cf%d" % i, [128, 1], F32) for i in range(4)]
        cf_b = bufs(4)
        cfi = [0]
        oi = 0

        def coef(acc, ab, dcol, gcol, tile_n, eps):
            r_t, r_b = next_rd()
            if eps:
                sc.op("dve", lambda E: E.tensor_scalar(out=r_t[:, :], in0=acc[:, dcol:dcol + 1], scalar1=1e-30,
                                                       scalar2=None, op0=ALU.add), reads=[ab], writes=[r_b])
                sc.op("dve", lambda E: E.reciprocal(out=r_t[:, :], in_=r_t[:, :]), reads=[r_b], writes=[r_b])
            else:
                sc.op("dve", lambda E: E.reciprocal(out=r_t[:, :], in_=acc[:, dcol:dcol + 1]), reads=[ab],
                      writes=[r_b])
            k = cfi[0] % 4
            cfi[0] += 1
            sc.op("dve", lambda E: E.tensor_tensor(out=cf[k][:, :], in0=r_t[:, :],
                                                   in1=g_sb[:, tile_n * 12 + gcol:tile_n * 12 + gcol + 1],
                                                   op=ALU.mult), reads=[r_b, g_b], writes=[cf_b[k]])
            return (r_t, r_b), (cf[k], cf_b[k])

        for i in range(nblk):
            qsl = slice(i * 512, (i + 1) * 512)
            for h in range(4):
                sc.dma("sp", nq_t[:, h, :], n_q[h, :, qsl], nq_b, writes=[nq_b])
            kt0 = 16 * i - 4
            j0 = 4 if i == 0 else 0
            sc.dma("sp", kw_t[:, j0 * 128:20 * 128], n_kw[:, (kt0 + j0) * 128:(kt0 + 20) * 128], kw_b, writes=[kw_b])
            sc.dma("sp", vw_t[:, j0 * 129:20 * 129], n_vw[:, (kt0 + j0) * 129:(kt0 + 20) * 129], vw_b, writes=[vw_b])
            for h in range(4):
                steps = []
                for ct in range(i + 1):
                    mm = [(kcT[:, ct * 128:(ct + 1) * 128], nq_t[:, h, :], [kcT_b, nq_b])]
                    if ct == i:
                        mm.append((ident[:, :], cmask[:, :], [ident_b, cmask_b]))
                    if ct == i - 1:
                        mm.append((ident[:, :], cmaskp[:, :], [ident_b, cmaskp_b]))
                    bi = h * 8 + (ct - i + 7)
                    steps.append(dict(mm=mm, scale=HD_SCALE, bias=(bcmp[:, bi:bi + 1], bcmp_b),
                                      v=(vcx[:, ct, :], [vcx_b])))
                att_block(sc, steps, sbank, pTs, accs, 385)
                for qs in range(4):
                    acc, ab = accs[qs]
                    (r_t, r_b), (c_t, c_b) = coef(acc, ab, 384, 3 * h + 0, 4 * i + qs, True)
                    sc.op("dve", lambda E: E.tensor_scalar(out=nso[:, qs, h * 128:(h + 1) * 128], in0=acc[:, 0:128],
                                                           scalar1=c_t[:, 0:1], scalar2=None, op0=ALU.mult),
                          reads=[ab, c_b], writes=[nso_b[qs]])
                    if h == 0:
                        sc.op("dve", lambda E: E.tensor_scalar(out=score[:, qs, :], in0=acc[:, 128:384],
                                                               scalar1=r_t[:, 0:1], scalar2=None, op0=ALU.mult),
                              reads=[ab, r_b], writes=[score_b[qs]])
                    else:
                        sc.op("dve", lambda E: E.scalar_tensor_tensor(out=score[:, qs, :], in0=acc[:, 128:384],
                                                                      scalar=r_t[:, 0:1], in1=score[:, qs, :],
                                                                      op0=ALU.mult, op1=ALU.add),
                              reads=[ab, r_b, score_b[qs]], writes=[score_b[qs]])
            for qs in range(4):
                off = 256 - 32 * i - 2 * qs
                sc.op("dve", lambda E: E.tensor_tensor(out=s2[:, :], in0=score[:, qs, :], in1=capt[:, off:off + 256],
                                                       op=ALU.min), reads=[score_b[qs], cap_b], writes=[s2_b])
                sc.op("dve", lambda E: E.tensor_tensor(out=s2[:, :], in0=s2[:, :], in1=floort[:, off:off + 256],
                                                       op=ALU.max), reads=[s2_b, floor_b], writes=[s2_b])
                sc.op("dve", lambda E: E.tensor_scalar(out=s2[:, 0:1], in0=s2[:, 0:1], scalar1=100.0, scalar2=None,
                                                       op0=ALU.max), reads=[s2_b], writes=[s2_b])
                sc.op("dve", lambda E: E.max(out=m8[:, 0:8], in_=s2[:, :]), reads=[s2_b], writes=[m8_b])
                sc.op("dve", lambda E: E.match_replace(out=work[:, :], in_to_replace=m8[:, 0:8], in_values=s2[:, :],
                                                       imm_value=-1e9), reads=[s2_b, m8_b], writes=[work_b])
                sc.op("dve", lambda E: E.max(out=m8[:, 8:16], in_=work[:, :]), reads=[work_b], writes=[m8_b])
                sc.op("dve", lambda E: E.tensor_scalar(out=negsel[:, :], in0=s2[:, :], scalar1=m8[:, 15:16],
                                                       scalar2=None, op0=ALU.is_lt),
                      reads=[s2_b, m8_b], writes=[negsel_b])
                for jh in range(2):
                    sc.op("pe", lambda E: E.transpose(out=tp_ps[:, :], in_=negsel[:, jh * 128:(jh + 1) * 128],
                                                      identity=ident[:, :]),
                          reads=[negsel_b, ident_b], writes=[tp_b])
                    sc.op("dve", lambda E: E.tensor_copy(out=negselT[:, jh, qs * 128:(qs + 1) * 128], in_=tp_ps[:, :]),
                          reads=[tp_b], writes=[negselT_b])
            for h in range(4):
                steps = []
                for kt in range(16 * i + 16):
                    ks = slice(kt * 128, (kt + 1) * 128)
                    e0 = 128 * (kt % 64)
                    mm = [(A0[:, ks], nq_t[:, h, :], [A0_b, nq_b]),
                          (ewide[:, e0:e0 + 128], negselT[:, kt // 64, :], [BS_b, negselT_b])]
                    if kt >= 16 * i:
                        off = 1920 - 128 * (kt - 16 * i)
                        mm.append((ident[:, :], tc_t[:, off:off + 512], [ident_b, tc_b]))
                    bi = h * 128 + (kt - 16 * i + 112)
                    steps.append(dict(mm=mm, scale=HD_SCALE, bias=(bslc[:, bi:bi + 1], bslc_b),
                                      v=(A1[:, kt * 129:(kt + 1) * 129], [A1_b])))
                att_block(sc, steps, sbank, pTs, accs, 129)
                for qs in range(4):
                    acc, ab = accs[qs]
                    (r_t, r_b), (c_t, c_b) = coef(acc, ab, 128, 3 * h + 1, 4 * i + qs, False)
                    sc.op("dve", lambda E: E.scalar_tensor_tensor(out=nso[:, qs, h * 128:(h + 1) * 128],
                                                                  in0=acc[:, 0:128], scalar=c_t[:, 0:1],
                                                                  in1=nso[:, qs, h * 128:(h + 1) * 128],
                                                                  op0=ALU.mult, op1=ALU.add),
                          reads=[ab, c_b, nso_b[qs]], writes=[nso_b[qs]])
                steps = []
                for jw in range(j0, 20):
                    off = 1920 - 128 * (jw - 4)
                    mm = [(kw_t[:, jw * 128:(jw + 1) * 128], nq_t[:, h, :], [kw_b, nq_b]),
                          (ident[:, :], tw_t[:, off:off + 512], [ident_b, tw_b])]
                    bi = h * 20 + jw
                    steps.append(dict(mm=mm, scale=HD_SCALE, bias=(bwin[:, bi:bi + 1], bwin_b),
                                      v=(vw_t[:, jw * 129:(jw + 1) * 129], [vw_b])))
                att_block(sc, steps, sbank, pTs, accs, 129)
                for qs in range(4):
                    acc, ab = accs[qs]
                    (r_t, r_b), (c_t, c_b) = coef(acc, ab, 128, 3 * h + 2, 4 * i + qs, False)
                    sc.op("dve", lambda E: E.scalar_tensor_tensor(out=nso[:, qs, h * 128:(h + 1) * 128],
                                                                  in0=acc[:, 0:128], scalar=c_t[:, 0:1],
                                                                  in1=nso[:, qs, h * 128:(h + 1) * 128],
                                                                  op0=ALU.mult, op1=ALU.add),
                          reads=[ab, c_b, nso_b[qs]], writes=[nso_b[qs]])
            for qs in range(4):
                o = oi % 2
                oi += 1
                sc.op("act", lambda E: E.activation(out=ost[o][:, :], in_=nso[:, qs, :], func=AF.Copy),
                      reads=[nso_b[qs]], writes=[ost_b[o]])
                rows = slice(i * 512 + qs * 128, i * 512 + (qs + 1) * 128)
                sc.dma("sp", o_nsa[rows, :], ost[o][:, :], ost_b[o], reads=[ost_b[o]], is_output=True)
    sc.finish()
    es.close()
    return nc


def alibi_slopes_np():
    return (2.0 ** (-8.0 * np.arange(1, 11, dtype=np.float64) / 10)).astype(np.float32)


DIL_D = (1, 4, 16)


def p2_tables(r):
    t = {}
    k = np.arange(128)[:, None]
    y = np.arange(TABW)[None, :]
    rel = (y - 1920) + 512 * r
    t["tc"] = np.where(k <= rel, 0.0, NEG).astype(NPBF)
    t["tw"] = np.where((rel - k >= 0) & (rel - k < 512), 0.0, NEG).astype(NPBF)
    t["ident"] = np.eye(128, dtype=np.float32).astype(NPBF)
    sl = alibi_slopes_np()
    a = np.arange(128)[None, :]
    c = np.arange(128)[:, None]
    db = np.zeros((128, 8, 256), np.float32)
    for tab in range(8):
        hd = tab if tab < 2 else tab - 2
        d = DIL_D[hd // 2]
        jp = 128 + a - c
        jc = a - c
        prev = np.where(a <= c, -sl[hd] * d * jp, NEG)
        cur = np.where(jc >= 0, -sl[hd] * d * jc, NEG)
        if tab < 2 and r == 0:
            prev = np.full((128, 128), NEG)
        db[:, tab, :128] = prev
        db[:, tab, 128:] = cur
    t["d_bias"] = db.reshape(128, 8 * 256)
    q = np.arange(512)[None, :]
    t["cmask"] = np.where(16 * c + 31 <= 512 * r + q, 0.0, NEG).astype(NPBF)
    t["cmaskp"] = np.where(16 * (c - 128) + 31 <= 512 * r + q, 0.0, NEG).astype(NPBF)
    u = np.arange(512)[None, :]
    dlt = u - 256 - 8 * r - (np.arange(128)[:, None] // 64)
    t["cap"] = np.where(dlt > 0, -1.0, 1e9).astype(np.float32)
    t["floor"] = np.where((dlt == 0) | (dlt == -1), 100.0, -1e9).astype(np.float32)
    ns = sl[6:10].astype(np.float64)
    p = np.arange(128)[:, None]
    bc = np.zeros((128, 4, 8))
    bs = np.zeros((128, 4, 128))
    bw = np.zeros((128, 4, 20))
    for h in range(4):
        bc[:, h, :] = ns[h] * (2048 * (np.arange(8)[None, :] - 7) + 16 * p + 15.5 - 512 * r)
        bs[:, h, :] = ns[h] * (128 * (np.arange(128)[None, :] - 112) + p - 512 * r)
        bw[:, h, :] = ns[h] * (128 * (np.arange(20)[None, :] - 4) + p - 512 * r)
    t["bcmp"] = bc.reshape(128, 32).astype(np.float32)
    t["bslc"] = bs.reshape(128, 512).astype(np.float32)
    t["bwin"] = bw.reshape(128, 80).astype(np.float32)
    x = np.arange(8192)[None, :]
    t["ewide"] = np.where(p == x // 64, NEG, 0.0).astype(NPBF)
    cg = np.arange(1024).reshape(8, 128)
    j = np.arange(256)[None, None, :]
    M = ((cg[:, :, None] >= 4 * j - 1) & (cg[:, :, None] <= 4 * j + 3)).astype(np.float32)
    M = np.concatenate([M, np.ones((8, 128, 1), np.float32)], axis=2)
    t["cmpsel"] = np.ascontiguousarray(M.transpose(1, 0, 2)).astype(NPBF)
    return t


def dil_index(g, r):
    if g == 0:
        return 4096 * r + np.arange(4096)
    if g == 1:
        return np.arange(4096) * 4 + r
    return (np.arange(1024)[None, :] * 16 + (4 * r + np.arange(4))[:, None]).reshape(-1)


def tok_major_tiles(v):
    n = v.shape[0] // 128
    return np.ascontiguousarray(v.reshape(n, 128, -1).transpose(1, 0, 2)).reshape(128, -1)


def prep_p2(r, nat, own, tab=None):
    m = dict(p2_tables(r) if tab is None else tab)
    m["m_qn"], m["m_qr"], m["n_q"], m["n_g"] = own["qn"], own["qr"], own["nq"], own["g"]
    m["m_kn"], m["m_kpe"] = nat["kn"], nat["kpe"]
    m["m_v"] = np.stack([tok_major_tiles(nat["mv"][:, h, :]) for h in range(6)])
    dq, dk, dv = [], [], []
    for hd in range(6):
        g = hd // 2
        idx = dil_index(g, r)
        dq.append(nat["dq"][hd][:, idx])
        if g == 0 and r > 0:
            pk = nat["dk"][hd][:, 4096 * r - 128:4096 * r]
            pv = nat["dv"][4096 * r - 128:4096 * r, hd, :]
        else:
            pk = np.zeros((128, 128), NPBF)
            pv = np.zeros((128, 129), NPBF)
        dk.append(np.concatenate([pk, nat["dk"][hd][:, idx]], axis=1))
        dv.append(tok_major_tiles(np.concatenate([pv, nat["dv"][idx, hd, :]], axis=0)))
    m["d_q"], m["d_k"], m["d_v"] = np.stack(dq), np.stack(dk), np.stack(dv)
    m["n_kc"], m["n_vc"], m["n_ks"], m["n_kw"] = nat["nkc"], nat["nvc"], nat["nks"], nat["nkw"]
    m["n_vs"] = tok_major_tiles(nat["nvs"])
    m["n_vw"] = tok_major_tiles(nat["nvw"])
    for k in ("w1k", "w1v", "w2k", "w2v", "posT"):
        m[k] = nat[k]
    return {k: np.ascontiguousarray(v) for k, v in m.items()}


def build_po():
    nc = new_nc()
    dt_in = lambda name, shape, dt: nc.dram_tensor(name, shape, dt, kind="ExternalInput").ap()
    xT = dt_in("xT", [16, 128, TOK], F32)
    o_m = dt_in("oT_mla", [6, 128, TOK], BF16)
    o_d = dt_in("oT_dil", [6, 128, TOK], F32)
    den = dt_in("denT", [6, TOK], F32)
    o_n = dt_in("oT_nsa", [4, 128, TOK], BF16)
    w_mo = dt_in("w_mo", [16, 128, 16 * 128], BF16)
    sel2_d = dt_in("sel2", [6, 256], F32)
    xo = nc.dram_tensor("xo", [16, 128, TOK], F32, kind="ExternalOutput").ap()
    es = contextlib.ExitStack()
    sc = Sched(nc, es)
    sc.store_q = "pool"
    sb = lambda name, shape, dt: es.enter_context(nc.sbuf_tensor(name, shape, dt))
    pr = PsumRing(nc, es)
    sel2 = sb("sel2_s", [6, 256], F32)
    sel2_b = Buf()
    sc.dma("sp", sel2[:, :], sel2_d, sel2_b, writes=[sel2_b])
    ob2 = [sb("ob%d" % i, [128, 16, PT], BF16) for i in range(2)]
    ob_b2 = [bufs(16) for i in range(2)]
    od2 = [sb("od%d" % i, [128, 6, PT], F32) for i in range(2)]
    od_b2 = [bufs(6) for i in range(2)]
    dn2 = [sb("dn%d" % i, [6, PT], F32) for i in range(2)]
    dn_b2 = bufs(2)
    rec = sb("rec", [128, 2, PT], F32)
    rec_b = bufs(2)
    NW = 3
    wch = [sb("wch%d" % i, [128, 16 * 128], BF16) for i in range(NW)]
    wch_b = bufs(NW)
    NX = 3
    xc = [sb("xc%d" % i, [128, PT], F32) for i in range(NX)]
    xc_b = bufs(NX)
    wi = 0
    xi = 0
    for t in range(TOK // PT):
        tsl = slice(t * PT, (t + 1) * PT)
        ob, ob_b, od, od_b, dn, dn_b = ob2[t % 2], ob_b2[t % 2], od2[t % 2], od_b2[t % 2], dn2[t % 2], dn_b2[t % 2]
        for h in range(6):
            sc.dma("sp", ob[:, h, :], o_m[h, :, tsl], ob_b[h], writes=[ob_b[h]])
            sc.dma("sp", od[:, h, :], o_d[h, :, tsl], od_b[h], writes=[od_b[h]])
        for h in range(4):
            sc.dma("sp", ob[:, 12 + h, :], o_n[h, :, tsl], ob_b[12 + h], writes=[ob_b[12 + h]])
        sc.dma("sp", dn[:, :], den[:, tsl], dn_b, writes=[dn_b])
        for s in range(2):
            ps, ps_b = pr.next()
            sc.op("pe", lambda E: E.matmul(ps[:, :], lhsT=sel2[:, s * 128:(s + 1) * 128], rhs=dn[:, :],
                                           start=True, stop=True), reads=[sel2_b, dn_b], writes=[ps_b])
            sc.op("dve", lambda E: E.reciprocal(out=rec[:, s, :], in_=ps[:, :]), reads=[ps_b], writes=[rec_b[s]])
        for hd in range(6):
            sc.op("dve", lambda E: E.tensor_tensor(out=ob[:, 6 + hd, :], in0=od[:, hd, :], in1=rec[:, hd % 2, :],
                                                   op=ALU.mult),
                  reads=[od_b[hd], rec_b[hd % 2]], writes=[ob_b[6 + hd]])
        pend = []

        def load_w(j):
            nonlocal wi
            k = wi % NW
            wi += 1
            sc.dma("sp", wch[k][:, :], w_mo[j], wch_b[k], writes=[wch_b[k]])
            return k
        pend.append(load_w(0))
        for dmc in range(16):
            if dmc + 1 < 16:
                pend.append(load_w(dmc + 1))
            k = pend.pop(0)
            x = xi % NX
            xi += 1
            sc.dma("sp", xc[x][:, :], xT[dmc, :, tsl], xc_b[x], writes=[xc_b[x]])
            ps, ps_b = pr.next()
            for fc in range(16):
                sc.op("pe", lambda E: E.matmul(ps[:, :], lhsT=wch[k][:, fc * 128:(fc + 1) * 128], rhs=ob[:, fc, :],
                                               start=(fc == 0), stop=(fc == 15)),
                      reads=[wch_b[k], ob_b[fc]], writes=[ps_b])
            sc.op("dve", lambda E: E.tensor_tensor(out=xc[x][:, :], in0=ps[:, :], in1=xc[x][:, :], op=ALU.add),
                  reads=[ps_b, xc_b[x]], writes=[xc_b[x]])
            sc.dma("sp", xo[dmc, :, tsl], xc[x][:, :], xc_b[x], reads=[xc_b[x]], is_output=True)
    sc.finish()
    es.close()
    return nc


def lay_w_mo(w):
    a = w.reshape(16, 128, 16, 128)
    return np.ascontiguousarray(a.transpose(2, 1, 0, 3)).reshape(16, 128, 16 * 128)


def sel2_table():
    s = np.zeros((6, 256), np.float32)
    for gs in range(6):
        s[gs, (gs % 2) * 128:(gs % 2 + 1) * 128] = 1.0
    return s


_NC_CACHE = {}
CONV_KEYS = ("ffn1_w_in", "ffn1_w_out", "w_mix_in", "mla_w_uq", "mla_w_ukv", "nsa_cmp_pos", "nsa_phi_k1",
             "nsa_phi_k2", "nsa_phi_v1", "nsa_phi_v2", "w_mix_out", "ffn2_w_in", "ffn2_w_out")
CONV_NC = 21 * CONV_CH


def _launch(key, builder, in_maps):
    if key not in _NC_CACHE:
        _NC_CACHE[key] = builder()
    res = run_bass_kernel_spmd(_NC_CACHE[key], in_maps, core_ids=list(range(NCORES)))
    return res.results


def _convert_layer(inp, l):
    flats = [np.ascontiguousarray(inp[k][l]).reshape(-1) for k in CONV_KEYS]
    n = sum(f.size for f in flats)
    buf = np.zeros(NCORES * 128 * CONV_NC, np.float32)
    o = 0
    for f in flats:
        buf[o:o + f.size] = f
        o += f.size
    buf = buf.reshape(NCORES, 128, CONV_NC)
    res = _launch("conv", lambda: build_conv(CONV_NC), [{"src": buf[c]} for c in range(NCORES)])
    out = np.stack([np.asarray(res[c]["dst"]) for c in range(NCORES)]).reshape(-1)
    w = {}
    o = 0
    for k in CONV_KEYS:
        shp = inp[k][l].shape
        sz = int(np.prod(shp))
        w[k] = out[o:o + sz].reshape(shp)
        o += sz
    return w


def _nat_fm(L):
    sh = L[0].shape
    return np.stack([a.reshape(sh[:-1] + (8, 512)) for a in L], axis=-2).reshape(sh[:-1] + (S,))


def _nat_tm(L):
    c = L[0].shape[1]
    return np.stack([a.reshape(8, 512, c) for a in L], axis=1).reshape(S, c)


def kernel(**inp):
    inp = {k: np.asarray(v) for k, v in inp.items()}
    x = inp["x"]
    pos = [core_positions(r) for r in range(4)]
    rope = [rope_tables(pos[r]) for r in range(4)]
    xs = []
    for c in range(NCORES):
        b, r = divmod(c, 4)
        xs.append(np.ascontiguousarray(x[b][pos[r]].T).reshape(16, 128, TOK))
    sel2 = sel2_table()
    tabs = [p2_tables(r) for r in range(4)]
    for l in range(DEPTH):
        w = _convert_layer(inp, l)
        wi, wo = lay_w_in(w["ffn1_w_in"]), lay_w_out(w["ffn1_w_out"])
        g = lay_gain(inp["ffn1_norm"][l])
        res = _launch("pf", lambda: build_pf(False),
                      [{"xT": xs[c], "g": g, "w_in": wi, "w_out": wo} for c in range(NCORES)])
        xs = [np.asarray(res[c]["xo"]) for c in range(NCORES)]
        del wi, wo
        pw = lay_pp_weights(w["w_mix_in"], w["mla_w_uq"], w["mla_w_ukv"])
        base = dict(pw)
        base.update(g=lay_gain(inp["mix_norm"][l]), gq=lay_gain(inp["mla_q_norm"][l]),
                    gkv=lay_gain(inp["mla_kv_norm"][l]))
        maps = []
        for c in range(NCORES):
            m = dict(base)
            m["xT"] = xs[c]
            m["cos2"], m["sgs"] = rope[c % 4]
            maps.append(m)
        R = _launch("pp", build_pp, maps)
        R = [{k: np.asarray(v) for k, v in R[c].items()} for c in range(NCORES)]
        wn = dict(w1k=lay_k(w["nsa_phi_k1"]), w1v=lay_k(w["nsa_phi_v1"]), w2k=w["nsa_phi_k2"], w2v=w["nsa_phi_v2"],
                  posT=np.ascontiguousarray(w["nsa_cmp_pos"].T))
        maps = []
        for b in range(B):
            C = [R[4 * b + r] for r in range(4)]
            nat = dict(wn)
            nat["kn"] = _nat_fm([c_["o_kn"] for c_ in C])
            nat["kpe"] = _nat_fm([c_["o_kpe"] for c_ in C])
            fm = _nat_fm([c_["o_fm"] for c_ in C])
            nat["dq"], nat["dk"] = fm[0:6], fm[6:12]
            nat["nkc"], nat["nvc"], nat["nks"], nat["nkw"] = fm[16], fm[17], fm[18], fm[19]
            nat["mv"] = _nat_tm([c_["o_mv"] for c_ in C]).reshape(S, 6, 129)
            nat["dv"] = _nat_tm([c_["o_dv"] for c_ in C]).reshape(S, 6, 129)
            nat["nvs"] = _nat_tm([c_["o_nvs"] for c_ in C])
            nat["nvw"] = _nat_tm([c_["o_nvw"] for c_ in C])
            for r in range(4):
                own = dict(qn=C[r]["o_qn"], qr=C[r]["o_qr"], nq=C[r]["o_fm"][12:16],
                           g=tok_major_tiles(C[r]["o_g"]))
                m = prep_p2_with_tables(r, nat, own, tabs[r])
                maps.append(m)
        del R
        A = _launch("p2", build_p2, maps)
        A = [{k: np.asarray(v) for k, v in A[c].items()} for c in range(NCORES)]
        del maps
        wmo = lay_w_mo(w["w_mix_out"])
        maps = []
        for b in range(B):
            dnat = np.zeros((6, S, 129), np.float32)
            for r in range(4):
                for hd in range(6):
                    dnat[hd][dil_index(hd // 2, r)] = A[4 * b + r]["o_dil"][hd]
            for r in range(4):
                c = 4 * b + r
                own = dnat[:, pos[r], :]
                maps.append({
                    "xT": xs[c],
                    "oT_mla": A[c]["o_mla"],
                    "oT_nsa": np.ascontiguousarray(A[c]["o_nsa"].T).reshape(4, 128, TOK),
                    "oT_dil": np.ascontiguousarray(own[:, :, :128].transpose(0, 2, 1)),
                    "denT": np.ascontiguousarray(own[:, :, 128]),
                    "w_mo": wmo, "sel2": sel2})
        res = _launch("po", build_po, maps)
        xs = [np.asarray(res[c]["xo"]) for c in range(NCORES)]
        del A, maps
        wi, wo = lay_w_in(w["ffn2_w_in"]), lay_w_out(w["ffn2_w_out"])
        g = lay_gain(inp["ffn2_norm"][l])
        if l < DEPTH - 1:
            res = _launch("pf", lambda: build_pf(False),
                          [{"xT": xs[c], "g": g, "w_in": wi, "w_out": wo} for c in range(NCORES)])
        else:
            gf = lay_gain(inp["final_norm"])
            res = _launch("pff", lambda: build_pf(True),
                          [{"xT": xs[c], "g": g, "gf": gf, "w_in": wi, "w_out": wo} for c in range(NCORES)])
        xs = [np.asarray(res[c]["xo"]) for c in range(NCORES)]
        del wi, wo, w
    out = np.empty((B, S, D), np.float32)
    for c in range(NCORES):
        b, r = divmod(c, 4)
        out[b][pos[r]] = xs[c].reshape(D, TOK).T
    return out


def prep_p2_with_tables(r, nat, own, tab):
    m = prep_p2(r, nat, own, tab)
    return m
```
